# Optimizing a Trainium2 kernel written in Bass

```python
import math
import jax, jax.numpy as jnp
from jax import lax
import numpy as np

D_MODEL = 1024
BATCH = 8
SEQ = 8192
DEPTH = 2

GRID_W = 64
CTX_LEN = 256
N_MIXERS = 2
N_SSM_LAYERS = (DEPTH + 1) // 2
N_ATTN_LAYERS = DEPTH // 2
NORM_EPS = 1e-6

SSM_WIDTH = D_MODEL
SSM_GROUP = 16
SSM_GROUPS = SSM_WIDTH // SSM_GROUP
SSM_STATE = 64
DT_MIN = 1e-3
DT_MAX = 1e-1

HEAD_DIM = 64
N_Q_HEADS = D_MODEL // HEAD_DIM
N_KV_HEADS = 4
KV_REP = N_Q_HEADS // N_KV_HEADS
ATTN_WIDTH = N_Q_HEADS * HEAD_DIM
KV_WIDTH = N_KV_HEADS * HEAD_DIM
ATTN_IN = ATTN_WIDTH + 2 * KV_WIDTH + ATTN_WIDTH
Q_BLOCK = 128
ROPE_THETA = 10000.0
ROPE_AXIS_DIM = HEAD_DIM // 2

kernel_name = "hybrid_s5_gqa_prefix_dit"


def _rmsnorm(x, g):
    xf = x.astype(jnp.float32)
    y = xf * lax.rsqrt(jnp.mean(xf * xf, axis=-1, keepdims=True) + NORM_EPS)
    return (y * g.astype(jnp.float32)).astype(x.dtype)


def _rope_tables(L):
    rows = L // GRID_W
    row = jnp.repeat(jnp.arange(rows), GRID_W).astype(jnp.float32)
    col = jnp.tile(jnp.arange(GRID_W), rows).astype(jnp.float32)
    n_freq = ROPE_AXIS_DIM // 2
    freqs = ROPE_THETA ** (-jnp.arange(n_freq, dtype=jnp.float32) / n_freq)
    ang_r = row[:, None] * freqs[None]
    ang_c = col[:, None] * freqs[None]
    return (jnp.cos(ang_r), jnp.sin(ang_r), jnp.cos(ang_c), jnp.sin(ang_c))


def _rope_half(x, cos, sin):
    h = x.shape[-1] // 2
    x1, x2 = x[..., :h], x[..., h:]
    cs, sn = cos[:, None, :], sin[:, None, :]
    return jnp.concatenate([x1 * cs - x2 * sn, x2 * cs + x1 * sn], axis=-1)


def _rope_2d(x, rope):
    cos_r, sin_r, cos_c, sin_c = rope
    xf = x.astype(jnp.float32)
    out = jnp.concatenate([_rope_half(xf[..., :ROPE_AXIS_DIM], cos_r, sin_r),
                           _rope_half(xf[..., ROPE_AXIS_DIM:], cos_c, sin_c)], axis=-1)
    return out.astype(x.dtype)


def _linear_scan(bu, abar, reverse):
    L = bu.shape[1]
    a = jnp.broadcast_to(abar, (1, L) + abar.shape)

    def combine(e_i, e_j):
        a_i, b_i = e_i
        a_j, b_j = e_j
        return a_j * a_i, a_j * b_i + b_j

    _, h = lax.associative_scan(combine, (a, bu), axis=1, reverse=reverse)
    return h


def _s5_core(u_lat, u_ctx, a_re, a_im, log_dt, b_re, b_im, c_re, c_im, d_skip, with_ctx):
    B, L, E = u_lat.shape
    C = u_ctx.shape[1]
    f32 = jnp.float32
    ul = u_lat.astype(f32).reshape(B, L, SSM_GROUPS, SSM_GROUP).astype(jnp.complex64)
    uc = u_ctx.astype(f32).reshape(B, C, SSM_GROUPS, SSM_GROUP).astype(jnp.complex64)
    dsk = d_skip.astype(f32)
    y_lat = u_lat.astype(f32) * dsk
    y_ctx = u_ctx.astype(f32) * dsk if with_ctx else None
    for d in range(2):
        rev = d == 1
        lam = lax.complex(a_re[d].astype(f32), a_im[d].astype(f32))
        lam_dt = lam * jnp.exp(log_dt[d].astype(f32))[:, None]
        abar = jnp.exp(lam_dt)
        bmat = lax.complex(b_re[d].astype(f32), b_im[d].astype(f32))
        bbar = ((abar - 1.0) / lam)[..., None] * bmat
        cmat = lax.complex(c_re[d].astype(f32), c_im[d].astype(f32))
        h_ctx = _linear_scan(jnp.einsum('bcgh,gph->bcgp', uc, bbar), abar, rev)
        h0 = h_ctx[:, 0] if rev else h_ctx[:, -1]
        steps = jnp.arange(L, 0, -1) if rev else jnp.arange(1, L + 1)
        carry = jnp.exp(lam_dt[None] * steps.astype(f32)[:, None, None])
        h_lat = _linear_scan(jnp.einsum('blgh,gph->blgp', ul, bbar), abar, rev) \
            + carry[None] * h0[:, None]
        y_lat = y_lat + jnp.real(jnp.einsum('blgp,ghp->blgh', h_lat, cmat)).reshape(B, L, E)
        if with_ctx:
            y_ctx = y_ctx + jnp.real(jnp.einsum('bcgp,ghp->bcgh', h_ctx, cmat)).reshape(B, C, E)
    return y_lat.astype(u_lat.dtype), (y_ctx.astype(u_ctx.dtype) if with_ctx else None)


def _s5_post(y, z, w_glu, b_glu, w_out):
    y = jax.nn.gelu(y, approximate=False)
    y = y * jax.nn.sigmoid(y @ w_glu + b_glu)
    return (y * jax.nn.silu(z)) @ w_out


def _ssm_layer(h, hc, w_in, a_re, a_im, log_dt, b_re, b_im, c_re, c_im, d_skip,
               w_glu, b_glu, w_out, with_ctx):
    proj = h @ w_in
    u, z = proj[..., :SSM_WIDTH], proj[..., SSM_WIDTH:]
    if with_ctx:
        proj_c = hc @ w_in
        u_c, z_c = proj_c[..., :SSM_WIDTH], proj_c[..., SSM_WIDTH:]
    else:
        u_c = hc @ w_in[:, :SSM_WIDTH]
    y, y_c = _s5_core(u, u_c, a_re, a_im, log_dt, b_re, b_im, c_re, c_im, d_skip, with_ctx)
    out = _s5_post(y, z, w_glu, b_glu, w_out)
    out_c = _s5_post(y_c, z_c, w_glu, b_glu, w_out) if with_ctx else None
    return out, out_c


def _sdpa(qb, k, v):
    s = jnp.einsum('bqgrd,bkgd->bgrqk', qb, k, preferred_element_type=jnp.float32)
    p = jax.nn.softmax(s * (1.0 / math.sqrt(HEAD_DIM)), axis=-1).astype(v.dtype)
    return jnp.einsum('bgrqk,bkgd->bqgrd', p, v)


def _attn_layer(h, hc, w_in, q_norm, k_norm, w_out, rope, with_ctx):
    B, L, _ = h.shape
    C = hc.shape[1]
    proj = h @ w_in
    q = proj[..., :ATTN_WIDTH].reshape(B, L, N_Q_HEADS, HEAD_DIM)
    k = proj[..., ATTN_WIDTH:ATTN_WIDTH + KV_WIDTH].reshape(B, L, N_KV_HEADS, HEAD_DIM)
    v = proj[..., ATTN_WIDTH + KV_WIDTH:ATTN_WIDTH + 2 * KV_WIDTH].reshape(B, L, N_KV_HEADS, HEAD_DIM)
    z = proj[..., ATTN_WIDTH + 2 * KV_WIDTH:]
    q = _rope_2d(_rmsnorm(q, q_norm), rope)
    k = _rope_2d(_rmsnorm(k, k_norm), rope)
    proj_c = hc @ w_in if with_ctx else hc @ w_in[:, ATTN_WIDTH:ATTN_WIDTH + 2 * KV_WIDTH]
    off = ATTN_WIDTH if with_ctx else 0
    k_c = _rmsnorm(proj_c[..., off:off + KV_WIDTH].reshape(B, C, N_KV_HEADS, HEAD_DIM), k_norm)
    v_c = proj_c[..., off + KV_WIDTH:off + 2 * KV_WIDTH].reshape(B, C, N_KV_HEADS, HEAD_DIM)
    k_all = jnp.concatenate([k, k_c], axis=1)
    v_all = jnp.concatenate([v, v_c], axis=1)
    nb = L // Q_BLOCK
    qb = q.reshape(B, nb, Q_BLOCK, N_KV_HEADS, KV_REP, HEAD_DIM).transpose(1, 0, 2, 3, 4, 5)
    o = lax.map(lambda blk: _sdpa(blk, k_all, v_all), qb)
    o = o.transpose(1, 0, 2, 3, 4, 5).reshape(B, L, ATTN_WIDTH)
    out = (o * jax.nn.silu(z)) @ w_out
    out_c = None
    if with_ctx:
        q_c = _rmsnorm(proj_c[..., :ATTN_WIDTH].reshape(B, C, N_Q_HEADS, HEAD_DIM), q_norm)
        o_c = _sdpa(q_c.reshape(B, C, N_KV_HEADS, KV_REP, HEAD_DIM), k_c, v_c).reshape(B, C, ATTN_WIDTH)
        out_c = (o_c * jax.nn.silu(proj_c[..., ATTN_WIDTH + 2 * KV_WIDTH:])) @ w_out
    return out, out_c


def setup_inputs(seed: int = 0) -> dict:
    key = jax.random.key(seed)
    ks = jax.random.split(key, 24)
    f32 = jnp.float32
    nrm = lambda k, shape, s: jax.random.normal(k, shape, f32) * s
    NA, NB, G, P, H = N_SSM_LAYERS, N_ATTN_LAYERS, SSM_GROUPS, SSM_STATE, SSM_GROUP
    a_im0 = jnp.pi * jnp.arange(P, dtype=f32)
    return {
        "x": nrm(ks[0], (BATCH, SEQ, D_MODEL), 1.0),
        "c": nrm(ks[1], (BATCH, D_MODEL), 1.0),
        "ctx": nrm(ks[2], (BATCH, CTX_LEN, D_MODEL), 1.0),
        "c_ctx": nrm(ks[3], (D_MODEL,), 1.0),
        "w_mod": nrm(ks[4], (DEPTH, D_MODEL, 3 * D_MODEL), D_MODEL ** -0.5),
        "b_mod": nrm(ks[5], (DEPTH, 3 * D_MODEL), 0.02),
        "norm_g": 1.0 + nrm(ks[6], (DEPTH, D_MODEL), 0.05),
        "ssm_w_in": nrm(ks[7], (NA, D_MODEL, 2 * SSM_WIDTH), D_MODEL ** -0.5),
        "ssm_a_re": -0.5 + nrm(ks[8], (NA, 2, G, P), 0.01),
        "ssm_a_im": a_im0 + nrm(ks[9], (NA, 2, G, P), 0.01),
        "ssm_log_dt": jax.random.uniform(ks[10], (NA, 2, G), f32,
                                         minval=math.log(DT_MIN), maxval=math.log(DT_MAX)),
        "ssm_b_re": nrm(ks[11], (NA, 2, G, P, H), (2.0 * H) ** -0.5),
        "ssm_b_im": nrm(ks[12], (NA, 2, G, P, H), (2.0 * H) ** -0.5),
        "ssm_c_re": nrm(ks[13], (NA, 2, G, H, P), (2.0 * P) ** -0.5),
        "ssm_c_im": nrm(ks[14], (NA, 2, G, H, P), (2.0 * P) ** -0.5),
        "ssm_d": nrm(ks[15], (NA, SSM_WIDTH), 0.5),
        "ssm_w_glu": nrm(ks[16], (NA, SSM_WIDTH, SSM_WIDTH), SSM_WIDTH ** -0.5),
        "ssm_b_glu": nrm(ks[17], (NA, SSM_WIDTH), 0.02),
        "ssm_w_out": nrm(ks[18], (NA, SSM_WIDTH, D_MODEL), SSM_WIDTH ** -0.5),
        "attn_w_in": nrm(ks[19], (NB, D_MODEL, ATTN_IN), D_MODEL ** -0.5),
        "attn_q_norm": 1.0 + nrm(ks[20], (NB, HEAD_DIM), 0.05),
        "attn_k_norm": 1.0 + nrm(ks[21], (NB, HEAD_DIM), 0.05),
        "attn_w_out": nrm(ks[22], (NB, ATTN_WIDTH, D_MODEL), ATTN_WIDTH ** -0.5),
        "final_norm_g": 1.0 + nrm(ks[23], (D_MODEL,), 0.05),
    }


def reference(x, c, ctx, c_ctx, w_mod, b_mod, norm_g, ssm_w_in, ssm_a_re, ssm_a_im,
              ssm_log_dt, ssm_b_re, ssm_b_im, ssm_c_re, ssm_c_im, ssm_d, ssm_w_glu,
              ssm_b_glu, ssm_w_out, attn_w_in, attn_q_norm, attn_k_norm, attn_w_out,
              final_norm_g):
    L = x.shape[1]
    rope = _rope_tables(L)
    s_c = jax.nn.silu(c)
    s_cc = jax.nn.silu(c_ctx)
    for i in range(DEPTH):
        kind, j = i % N_MIXERS, i // N_MIXERS
        with_ctx = i < DEPTH - 1
        mod = s_c @ w_mod[i] + b_mod[i]
        shift, scale, gate = jnp.split(mod, 3, axis=-1)
        mod_c = s_cc @ w_mod[i] + b_mod[i]
        shift_c, scale_c, gate_c = jnp.split(mod_c, 3, axis=-1)
        h = _rmsnorm(x, norm_g[i]) * (1.0 + scale[:, None]) + shift[:, None]
        hc = _rmsnorm(ctx, norm_g[i]) * (1.0 + scale_c) + shift_c
        if kind == 0:
            out, out_c = _ssm_layer(h, hc, ssm_w_in[j], ssm_a_re[j], ssm_a_im[j], ssm_log_dt[j],
                                    ssm_b_re[j], ssm_b_im[j], ssm_c_re[j], ssm_c_im[j], ssm_d[j],
                                    ssm_w_glu[j], ssm_b_glu[j], ssm_w_out[j], with_ctx)
        else:
            out, out_c = _attn_layer(h, hc, attn_w_in[j], attn_q_norm[j], attn_k_norm[j],
                                     attn_w_out[j], rope, with_ctx)
        x = x + gate[:, None] * out
        if with_ctx:
            ctx = ctx + gate_c * out_c
    return _rmsnorm(x, final_norm_g)
```

```python
import numpy as np
from contextlib import ExitStack
import concourse.bass as bass
import concourse.mybir as mybir
from concourse.bass_utils import run_bass_kernel_spmd

F32 = mybir.dt.float32
BF16 = mybir.dt.bfloat16
I32 = mybir.dt.int32
ALU = mybir.AluOpType
AF = mybir.ActivationFunctionType
AX = mybir.AxisListType

D = 1024
L = 8192
C = 256
NKL = 1024
NKC = 32
NK = NKL + NKC
EPS = 1e-6
TWO_PI = float(2 * np.pi)
PI_SAFE = 3.1415925
STOP_S5_SETUP = [False]


class Op:
    __slots__ = ("eng", "fn", "deps", "is_dma", "stream", "signal", "ticket", "semname")


class Tracker:
    ENGS = ["pe", "act", "dve", "pool", "sp"]

    def __init__(self, nc, es):
        self.nc = nc
        self.es = es
        self.sem = {}
        self.count = {}
        self.ops = []
        self.last_w = {}
        self.readers = {}
        self.barrier = {}
        self.waited = {e: {} for e in self.ENGS}

    def _sem(self, name):
        if name not in self.sem:
            self.sem[name] = self.es.enter_context(self.nc.semaphore(name))
            self.count[name] = 0
        return self.sem[name]

    def add(self, eng, fn, r=(), w=(), dma=False, stream=None):
        op = Op()
        op.eng, op.fn, op.is_dma, op.stream = eng, fn, dma, stream
        op.signal, op.ticket, op.semname = dma, 0, None
        deps = set()
        for k in r:
            if k in self.last_w:
                deps.add(self.last_w[k])
        for k in w:
            if k in self.last_w:
                deps.add(self.last_w[k])
            deps.update(self.readers.get(k, ()))
        i = len(self.ops)
        op.deps = deps
        self.ops.append(op)
        for k in r:
            self.readers.setdefault(k, []).append(i)
        for k in w:
            self.last_w[k] = i
            self.readers[k] = []
        return i

    def dma(self, out, in_, r=(), w=(), stream=None, eng="sp", **kw):
        assert stream is not None
        return self.add(eng, lambda e: e.dma_start(out=out, in_=in_, **kw), r=r, w=w, dma=True, stream=stream)

    def emit(self, block):
        ops = self.ops
        for op in ops:
            for d in op.deps:
                dep = ops[d]
                if dep.is_dma or dep.eng != op.eng or op.eng != "pe":
                    dep.signal = True
        last = {}
        for op in ops:
            if not op.is_dma and op.fn is not None:
                last[op.eng] = op
        for op in last.values():
            op.signal = True
        for op in ops:
            if op.signal and op.fn is not None:
                name = ("D_" + op.stream) if op.is_dma else ("E_" + op.eng)
                self._sem(name)
                self.count[name] += 16 if op.is_dma else 1
                op.ticket = self.count[name]
                op.semname = name
        reg = {"pe": block.tensor, "act": block.scalar, "dve": block.vector, "pool": block.gpsimd, "sp": block.sync}
        for eng in self.ENGS:
            eops = [op for op in ops if op.eng == eng]
            if not eops:
                continue

            def body(e, eops=eops, eng=eng):
                waited = self.waited[eng]
                first = True
                for op in eops:
                    waits = {}
                    if first:
                        waits.update(self.barrier)
                        first = False
                    for d in op.deps:
                        dep = ops[d]
                        if dep.is_dma or dep.eng != eng or eng != "pe":
                            if dep.semname is not None:
                                waits[dep.semname] = max(waits.get(dep.semname, 0), dep.ticket)
                    for s, v in waits.items():
                        if v > 0 and waited.get(s, 0) < v:
                            e.wait_ge(self.sem[s], v)
                            waited[s] = v
                    if op.fn is not None:
                        ins = op.fn(e)
                        if op.signal:
                            ins.then_inc(self.sem[op.semname], 16 if op.is_dma else 1)

            reg[eng](body)
        self.barrier = dict(self.count)
        self.ops = []
        self.last_w = {}
        self.readers = {}


def bc(ap, shape):
    return ap.to_broadcast(shape)


def build_program(stop_after=None, debug=False):
    nc = bass.Bass("TRN2", target_bir_lowering=False)
    dt_in = {}

    def din(name, shape, dt=F32):
        dt_in[name] = nc.dram_tensor(name, list(shape), dt, kind="ExternalInput").ap()
        return dt_in[name]

    x = din("x", [L, D]); ctx = din("ctx", [C, D]); cvec = din("cvec", [2, D])
    w_mod = din("w_mod", [2, D, 3 * D]); b_mod = din("b_mod", [2, 3 * D]); norm_g = din("norm_g", [2, D])
    ssm_w_in = din("ssm_w_in", [D, 2 * D])
    a_re = din("ssm_a_re", [2, 64, 64]); a_im = din("ssm_a_im", [2, 64, 64]); log_dt = din("ssm_log_dt", [2, 64])
    b_re = din("ssm_b_re", [2, 64, 64, 16]); b_im = din("ssm_b_im", [2, 64, 64, 16])
    c_re = din("ssm_c_re", [2, 64, 16, 64]); c_im = din("ssm_c_im", [2, 64, 16, 64])
    ssm_d = din("ssm_d", [D]); w_glu = din("ssm_w_glu", [D, D]); b_glu = din("ssm_b_glu", [D])
    ssm_w_out = din("ssm_w_out", [D, D])
    attn_w_in = din("attn_w_in", [D, 2560]); q_norm = din("attn_q_norm", [64]); k_norm = din("attn_k_norm", [64])
    attn_w_out = din("attn_w_out", [D, D]); fin_g = din("final_norm_g", [D])
    ident_in = din("ident", [128, 128]); maskf_in = din("maskf", [128, 128]); maskb_in = din("maskb", [128, 128])
    kio_in = din("kio", [128, NK]); jv_in = din("jvals", [128, 16])
    posr_in = din("posr", [128, 8]); posc_in = din("posc", [128, 8]); fidx_in = din("fidx", [128, 16])

    out = nc.dram_tensor("out", [L, D], F32, kind="ExternalOutput").ap()
    dbg = {}

    def dout(name, shape, dt=F32):
        dbg[name] = nc.dram_tensor(name, list(shape), dt, kind="ExternalOutput" if debug else "Internal").ap()
        return dbg[name]

    VEC = dout("VEC", [12, D])
    DSCR = dout("DSCR", [64, 128, NK], BF16)
    SZ = dout("SZ", [C + L, D], BF16)
    YSC = dout("YSC", [C + L, D], BF16)
    X1 = dout("X1", [L, D])
    CTX1 = dout("CTX1", [C, D])

    blocks = [(NKC, ctx, 0, 0, True, 0)]
    for b in range(8):
        blocks.append((128, x[b * 1024:(b + 1) * 1024, :], NKC + 128 * b, C + 1024 * b, False, b))

    with ExitStack() as ges:
        T = Tracker(nc, ges)
        identf = ges.enter_context(nc.sbuf_tensor("identf", [128, 128], F32))
        identb = ges.enter_context(nc.sbuf_tensor("identb", [128, 128], BF16))

        with ExitStack() as es:
            sb = lambda n, s, d=F32: es.enter_context(nc.sbuf_tensor(n, s, d))
            scraw = sb("scraw", [128, 2, 8]); sc = sb("sc", [128, 8, 2])
            wst = sb("wst", [128, 8, 3 * D])
            bcol = sb("bcol", [128, 2, 24]); gcol = sb("gcol", [128, 2, 8])
            modc = sb("modc", [128, 2, 24, 2]); gs = sb("gs", [128, 2, 8, 2])
            psm = es.enter_context(nc.psum_tensor("psm", [128, 512], F32))
            block = es.enter_context(nc.Block())
            T.dma(identf[:], ident_in[:, :], w=["identf"], stream="c0")
            T.add("dve", lambda e: e.tensor_copy(out=identb[:], in_=identf[:]), r=["identf"], w=["identb"])
            T.dma(scraw[:], cvec.rearrange("n (kt p) -> p n kt", p=128), w=["scraw"], stream="c1",
                  allow_slow_non_contiguous=True)
            T.dma(bcol[:], b_mod.rearrange("i (j p) -> p i j", p=128), w=["bcol"], stream="c2",
                  allow_slow_non_contiguous=True)
            T.dma(gcol[:], norm_g.rearrange("i (j p) -> p i j", p=128), w=["gcol"], stream="c3",
                  allow_slow_non_contiguous=True)
            T.add("act", lambda e: e.activation(out=sc[:].rearrange("p kt n -> p n kt"), in_=scraw[:], func=AF.Silu),
                  r=["scraw"], w=["sc"])
            for i in range(2):
                psv = psm[:, i * 48:(i + 1) * 48].rearrange("p (j n) -> p j n", n=2)
                for kt in range(8):
                    T.dma(wst[:, kt, :], w_mod[i, kt * 128:(kt + 1) * 128, :], w=[("wst", kt)], stream="wst%d" % kt)
                for j in range(24):
                    for kt in range(8):
                        T.add("pe", lambda e, j=j, kt=kt, psv=psv: e.matmul(
                            psv[:, j, :], lhsT=wst[:, kt, j * 128:(j + 1) * 128], rhs=sc[:, kt, :],
                            start=(kt == 0), stop=(kt == 7)),
                            r=[("wst", kt), "sc"], w=[("psm", i)])
                T.add("dve", lambda e, i=i, psv=psv: e.tensor_tensor(
                    out=modc[:, i], in0=psv, in1=bc(bcol[:, i, :].unsqueeze(2), [128, 24, 2]), op=ALU.add),
                    r=[("psm", i), "bcol"], w=[("modc", i)])
                T.add("dve", lambda e, i=i: e.scalar_tensor_tensor(
                    out=gs[:, i], in0=modc[:, i, 8:16, :], scalar=1.0,
                    in1=bc(gcol[:, i, :].unsqueeze(2), [128, 8, 2]), op0=ALU.add, op1=ALU.mult),
                    r=[("modc", i), "gcol"], w=[("gs", i)])
                for n in range(2):
                    for which, src in ((0, gs[:, i, :, n]), (1, modc[:, i, 0:8, n]), (2, modc[:, i, 16:24, n])):
                        v = i * 6 + n * 3 + which
                        T.dma(VEC[v, :].rearrange("(ft p) -> p ft", p=128), src, r=[("gs", i), ("modc", i)],
                              w=["VEC"], stream="vec", allow_slow_non_contiguous=True)
            T.emit(block)
        if stop_after == 0:
            return nc, finish(nc, T, out)

        phase_front(nc, T, blocks, VEC, 0, ssm_w_in, 2 * D, identb, DSCR, SZ, mode="ssm")
        if stop_after == 1:
            return nc, finish(nc, T, out)
        phase_s5(nc, T, dict(a_re=a_re, a_im=a_im, log_dt=log_dt, b_re=b_re, b_im=b_im, c_re=c_re, c_im=c_im,
                             ssm_d=ssm_d, kio=kio_in, jv=jv_in, maskf=maskf_in, maskb=maskb_in),
                 identf, DSCR, YSC)
        if stop_after == 2:
            return nc, finish(nc, T, out)
        phase_post0(nc, T, blocks, VEC, YSC, SZ, w_glu, b_glu, ssm_w_out, identb, X1, CTX1)
        if stop_after == 3:
            return nc, finish(nc, T, out)
        phase_attn(nc, T, VEC, X1, CTX1, attn_w_in, q_norm, k_norm, attn_w_out, fin_g, identf, identb,
                   posr_in, posc_in, fidx_in, out, stop_after)
    return nc, None


def finish(nc, T, out):
    with ExitStack() as es:
        z = es.enter_context(nc.sbuf_tensor("zfin", [128, D], F32))
        block = es.enter_context(nc.Block())
        T.add("dve", lambda e: e.memset(z[:], 0.0), w=["z"])
        T.dma(out[0:128, :], z[:], r=["z"], w=["out"], stream="fin")
        T.add("sp", None, r=["out"])
        T.emit(block)
    return None


def load_cast_weight(nc, T, es, name, w_ap, ncols, stage):
    wt = es.enter_context(nc.sbuf_tensor(name, [128, 8, ncols], BF16))
    for ft in range(8):
        st = stage[ft % 2]
        T.dma(st[:, 0:ncols], w_ap[ft * 128:(ft + 1) * 128, :], w=[("stage", ft % 2)], stream="stage%d" % (ft % 2))
        T.add("pool", lambda e, st=st, ft=ft: e.tensor_copy(out=wt[:, ft, :], in_=st[:, 0:ncols]),
              r=[("stage", ft % 2)], w=[name])
    return wt


def phase_front(nc, T, blocks, VEC, layer, w_in_ap, ncols, identb, DSCR, SZ, mode):
    with ExitStack() as es:
        sb = lambda n, s, d=F32: es.enter_context(nc.sbuf_tensor(n, s, d))
        stage = [sb("stg0", [128, 2560]), sb("stg1", [128, 2560])]
        block = es.enter_context(nc.Block())
        rep = {}
        for n in range(2):
            for which, nm in ((0, "gs"), (1, "sh")):
                t = sb("rep_%s%d" % (nm, n), [128, D])
                v = layer * 6 + n * 3 + which
                T.dma(t[:], VEC[v, :].partition_broadcast(128), w=[("rep", nm, n)], stream="rep%s%d" % (nm, n))
                rep[(nm, n)] = t
        wt = load_cast_weight(nc, T, es, "w_in_bf", w_in_ap, ncols, stage)
        xh = [sb("xh0", [128, 4, D]), sb("xh1", [128, 4, D])]
        junk = sb("junk", [128, D], BF16)
        tmp32 = [sb("tmp32a", [128, D]), sb("tmp32b", [128, D])]
        ss = sb("ss", [128, 8]); ms = sb("ms", [128, 8]); rstd = sb("rstd", [128, 8])
        h = sb("h", [128, 8, D], BF16)
        hT = sb("hT", [128, 8, 8, 128], BF16)
        ucat = sb("ucat", [128, 64, 8, 16], BF16)
        sz = sb("sz", [128, 8, D], BF16)
        dst = [sb("dst0", [128, 8, 128], BF16), sb("dst1", [128, 8, 128], BF16)]
        psT = [es.enter_context(nc.psum_tensor("psT%d" % i, [128, 8, 128], BF16)) for i in range(2)]
        psU = [es.enter_context(nc.psum_tensor("psU%d" % i, [128, 512], F32)) for i in range(4)]
        psD = [es.enter_context(nc.psum_tensor("psD%d" % i, [128, 8, 128], BF16)) for i in range(2)]
        evac_i = [0]
        ucnt = [0]
        for (nk, xap, kg0, row0, is_ctx, bidx) in blocks:
            n = 1 if is_ctx else 0
            xv = xap.rearrange("(k s) f -> k s f", s=8)
            T.add("dve", lambda e: e.memset(ss[:], 0.0), w=["ss"])
            for half in range(2):
                T.dma(xh[half][:nk], xv[:, half * 4:(half + 1) * 4, :], w=[("xh", half)], stream="xh%d" % half)
                for s4 in range(4):
                    s = half * 4 + s4
                    T.add("act", lambda e, half=half, s4=s4, s=s, nk=nk: e.activation(
                        out=junk[:nk], in_=xh[half][:nk, s4, :], func=AF.Square, accum_out=ss[:nk, s:s + 1]),
                        r=[("xh", half)], w=["junk", "ss"])
            T.add("dve", lambda e, nk=nk: e.tensor_scalar(out=ms[:nk], in0=ss[:nk], scalar1=1.0 / D, scalar2=EPS,
                                                          op0=ALU.mult, op1=ALU.add), r=["ss"], w=["ms"])
            T.add("act", lambda e, nk=nk: e.sqrt(out=ms[:nk], in_=ms[:nk]), r=["ms"], w=["ms"])
            T.add("dve", lambda e, nk=nk: e.reciprocal(out=rstd[:nk], in_=ms[:nk]), r=["ms"], w=["rstd"])
            for s in range(8):
                half, s4 = divmod(s, 4)
                tm = tmp32[s % 2]
                T.add("dve", lambda e, half=half, s4=s4, s=s, nk=nk, tm=tm, n=n: e.scalar_tensor_tensor(
                    out=tm[:nk], in0=xh[half][:nk, s4, :], scalar=rstd[:nk, s:s + 1], in1=rep[("gs", n)][:nk],
                    op0=ALU.mult, op1=ALU.mult), r=[("xh", half), "rstd", ("rep", "gs", n)], w=[("tmp32", s % 2)])
                T.add("pool", lambda e, s=s, nk=nk, tm=tm, n=n: e.tensor_tensor(
                    out=h[:nk, s, :], in0=tm[:nk], in1=rep[("sh", n)][:nk], op=ALU.add),
                    r=[("tmp32", s % 2), ("rep", "sh", n)], w=[("h", s)])
            for ft in range(8):
                pt = psT[ft % 2]
                for s in range(8):
                    T.add("pe", lambda e, pt=pt, s=s, ft=ft, nk=nk: e.transpose(
                        out=pt[:, s, :nk], in_=h[:nk, s, ft * 128:(ft + 1) * 128], identity=identb[:nk, :nk]),
                        r=[("h", s), "identb"], w=[("psT", ft % 2)])
                eng = "act" if (evac_i[0] % 2 == 0) else "dve"
                evac_i[0] += 1
                if eng == "act":
                    T.add("act", lambda e, pt=pt, ft=ft, nk=nk: e.copy(out=hT[:, ft, :, :nk], in_=pt[:, :, :nk]),
                          r=[("psT", ft % 2)], w=[("hT", ft)])
                else:
                    T.add("dve", lambda e, pt=pt, ft=ft, nk=nk: e.tensor_copy(out=hT[:, ft, :, :nk], in_=pt[:, :, :nk]),
                          r=[("psT", ft % 2)], w=[("hT", ft)])
            for s in range(8):
                for nb in range(4):
                    pu = psU[ucnt[0] % 4]
                    pkey = ("psU", ucnt[0] % 4)
                    ucnt[0] += 1
                    for ft in range(8):
                        T.add("pe", lambda e, pu=pu, s=s, nb=nb, ft=ft, nk=nk: e.matmul(
                            pu[:nk, :], lhsT=hT[:, ft, s, :nk], rhs=wt[:, ft, nb * 512:(nb + 1) * 512],
                            start=(ft == 0), stop=(ft == 7)),
                            r=[("hT", ft), "w_in_bf"], w=[pkey])
                    if nb < 2:
                        T.add("dve", lambda e, pu=pu, s=s, nb=nb, nk=nk: e.tensor_copy(
                            out=ucat[:nk, nb * 32:(nb + 1) * 32, s, :],
                            in_=pu[:nk, :].rearrange("k (g h) -> k g h", h=16)),
                            r=[pkey], w=["ucat"])
                    else:
                        T.add("act", lambda e, pu=pu, s=s, nb=nb, nk=nk: e.activation(
                            out=sz[:nk, s, (nb - 2) * 512:(nb - 1) * 512], in_=pu[:nk, :], func=AF.Silu),
                            r=[pkey], w=["sz"])
            T.dma(SZ[row0:row0 + nk * 8, :].rearrange("(k s) f -> k s f", s=8), sz[:nk], r=["sz"], w=["SZ"],
                  stream="szst", eng="pool")
            for gq in range(8):
                pd = psD[gq % 2]
                for gi in range(8):
                    g = gq * 8 + gi
                    T.add("pe", lambda e, pd=pd, gi=gi, g=g, nk=nk: e.transpose(
                        out=pd[:, gi, :nk], in_=ucat[:nk, g, :, :].rearrange("k s h -> k (s h)"),
                        identity=identb[:nk, :nk]),
                        r=["ucat", "identb"], w=[("psD", gq % 2)])
                T.add("dve", lambda e, pd=pd, gq=gq, nk=nk: e.tensor_copy(out=dst[gq % 2][:, :, :nk], in_=pd[:, :, :nk]),
                      r=[("psD", gq % 2)], w=[("dst", gq % 2)])
                T.dma(DSCR[gq * 8:(gq + 1) * 8, :, kg0:kg0 + nk].rearrange("g p k -> p g k"), dst[gq % 2][:, :, :nk],
                      r=[("dst", gq % 2)], w=["DSCR"], stream="dscr%d" % (gq % 2), eng="pool")
        T.emit(block)


def phase_s5(nc, T, P, identf, DSCR, YSC):
    PI2 = TWO_PI
    with ExitStack() as es:
        sb = lambda n, s, d=F32: es.enter_context(nc.sbuf_tensor("s5_" + n, s, d))
        es1 = ExitStack()
        sbt = lambda n, s, d=F32: es1.enter_context(nc.sbuf_tensor("s5t_" + n, s, d))
        kio = sb("kio", [128, NK]); jv = sb("jv", [128, 16])
        maskf = sb("maskf", [128, 128]); maskb = sb("maskb", [128, 128])
        phi = sb("phi", [128, 128]); phi2pi = sb("phi2pi", [128, 128]); rho = sb("rho", [128, 128])
        ctr = sb("ctr", [128, 128, 16]); cti = sb("cti", [128, 128, 16])
        ejr = sb("ejr", [128, 16, 128]); eji = sb("eji", [128, 16, 128])
        bbr = sb("bbr", [128, 128, 16]); bbi = sb("bbi", [128, 128, 16])
        dcol = sb("dcol", [128, 64])
        sm5 = sb("sm5", [128, 128])
        an_r = sbt("an_r", [64, 2, 2, 64]); an_i = sbt("an_i", [64, 2, 2, 64])
        are = sbt("are", [128, 128]); aim = sbt("aim", [128, 128]); ldt = sbt("ldt", [128, 128])
        dtt = sbt("dtt", [128, 128]); alpha = sbt("alpha", [128, 128]); theta = sbt("theta", [128, 128])
        btr = sbt("btr", [128, 128, 16]); bti = sbt("bti", [128, 128, 16])
        cn_r = sbt("cn_r", [128, 8, 2, 2, 64]); cn_i = sbt("cn_i", [128, 8, 2, 2, 64])
        tmpA = sbt("tmpA", [128, 16, 128]); tmpB = sbt("tmpB", [128, 16, 128]); tmpC = sbt("tmpC", [128, 16, 128])
        tni = sbt("tni", [128, 16, 128], I32)
        sm = [sbt("sm%d" % i, [128, 128]) for i in range(5)] + [sm5]
        pss = es.enter_context(nc.psum_tensor("pss", [128, 4, 128], F32))
        psZA = es.enter_context(nc.psum_tensor("psZA", [128, 2, 512], F32))
        psZB = es.enter_context(nc.psum_tensor("psZB", [128, 2, 512], F32))
        psZc = es.enter_context(nc.psum_tensor("psZc", [128, 512], F32))
        psY = [es.enter_context(nc.psum_tensor("psY%d" % i, [128, 4, 128], F32)) for i in range(2)]
        block = es1.enter_context(nc.Block())
        A = T.add
        A("pool", lambda e: e.memset(sm[5][:], float(np.pi / 2)), w=["sm5"])
        T.dma(kio[:], P["kio"][:, :], w=["kio"], stream="p0")
        T.dma(jv[:], P["jv"][:, :], w=["jv"], stream="p1")
        T.dma(maskf[:], P["maskf"][:, :], w=["maskf"], stream="p2")
        T.dma(maskb[:], P["maskb"][:, :], w=["maskb"], stream="p3")
        for c2 in range(2):
            T.dma(an_r[:, :, c2, :], P["a_re"].rearrange("d g p -> g d p"), w=["an_r"], stream="p4")
            T.dma(an_i[:, :, c2, :], P["a_im"].rearrange("d g p -> g d p"), w=["an_i"], stream="p5")
        T.dma(ldt[:], P["log_dt"].rearrange("d g -> (d g)").partition_broadcast(128), w=["ldt"], stream="p6")
        for c2 in range(2):
            for d in range(2):
                for gq in range(4):
                    T.dma(btr[c2 * 64:(c2 + 1) * 64, d * 64 + gq * 16:d * 64 + gq * 16 + 16, :],
                          P["b_re"][d, gq * 16:(gq + 1) * 16].rearrange("g p h -> p g h"), w=["btr"], stream="p7")
                    T.dma(bti[c2 * 64:(c2 + 1) * 64, d * 64 + gq * 16:d * 64 + gq * 16 + 16, :],
                          P["b_im"][d, gq * 16:(gq + 1) * 16].rearrange("g p h -> p g h"), w=["bti"], stream="p8")
            for d in range(2):
                T.dma(cn_r[:, :, d, c2, :], P["c_re"][d].rearrange("(gb g8) h p -> (g8 h) gb p", g8=8),
                      w=["cn_r"], stream="p9")
                T.dma(cn_i[:, :, d, c2, :], P["c_im"][d].rearrange("(gb g8) h p -> (g8 h) gb p", g8=8),
                      w=["cn_i"], stream="p10")
        for s8 in range(8):
            T.dma(dcol[s8 * 16:(s8 + 1) * 16, :], P["ssm_d"].rearrange("(g h) -> h g", h=16), w=["dcol"],
                  stream="p11", allow_slow_non_contiguous=True)
        for (src, dstt, nm) in ((an_r, are, "are"), (an_i, aim, "aim")):
            for d in range(2):
                A("pe", lambda e, src=src, d=d: e.transpose(
                    out=pss[:, d, 0:64], in_=src[:, d, :, :].rearrange("g c p -> g (c p)"), identity=identf[0:64, 0:64]),
                    r=[nm.replace("a", "an_", 1) if False else ("an_r" if nm == "are" else "an_i"), "identf"], w=["pss"])
            A("dve", lambda e, dstt=dstt: e.tensor_copy(out=dstt[:].rearrange("p (d g) -> p d g", d=2), in_=pss[:, 0:2, 0:64]),
              r=["pss"], w=[nm])
        for (src, dstt, nm, snm) in ((cn_r, ctr, "ctr", "cn_r"), (cn_i, cti, "cti", "cn_i")):
            for d in range(2):
                for gq in range(2):
                    for g4 in range(4):
                        gb = gq * 4 + g4
                        A("pe", lambda e, src=src, d=d, gb=gb, g4=g4: e.transpose(
                            out=pss[:, g4, :], in_=src[:, gb, d, :, :].rearrange("q c p -> q (c p)"), identity=identf[:]),
                            r=[snm, "identf"], w=["pss"])
                    A("dve", lambda e, dstt=dstt, d=d, gq=gq: e.tensor_copy(
                        out=dstt[:, d * 64 + gq * 32:d * 64 + gq * 32 + 32, :].rearrange("p (a b) h -> p a b h", a=4),
                        in_=pss[:].rearrange("p a (b h) -> p a b h", h=16)),
                        r=["pss"], w=[nm])
        A("act", lambda e: e.activation(out=dtt[:], in_=ldt[:], func=AF.Exp), r=["ldt"], w=["dtt"])
        A("dve", lambda e: e.tensor_tensor(out=alpha[:], in0=are[:], in1=dtt[:], op=ALU.mult), r=["are", "dtt"], w=["alpha"])
        A("dve", lambda e: e.tensor_tensor(out=theta[:], in0=aim[:], in1=dtt[:], op=ALU.mult), r=["aim", "dtt"], w=["theta"])
        A("dve", lambda e: e.tensor_scalar(out=phi[:], in0=theta[:], scalar1=8.0, scalar2=0.0, op0=ALU.mult, op1=ALU.add),
          r=["theta"], w=["phi"])
        A("dve", lambda e: e.tensor_scalar(out=phi2pi[:], in0=theta[:], scalar1=8.0 / PI2, scalar2=0.0, op0=ALU.mult,
                                           op1=ALU.add), r=["theta"], w=["phi2pi"])
        A("act", lambda e: e.activation(out=rho[:], in_=alpha[:], func=AF.Exp, scale=8.0), r=["alpha"], w=["rho"])
        th_b = bc(theta[:].unsqueeze(1), [128, 16, 128]); al_b = bc(alpha[:].unsqueeze(1), [128, 16, 128])
        jv_b = bc(jv[:].unsqueeze(2), [128, 16, 128])
        A("dve", lambda e: e.tensor_tensor(out=tmpA[:], in0=th_b, in1=jv_b, op=ALU.mult), r=["theta", "jv"], w=["tmpA"])
        A("dve", lambda e: e.tensor_scalar(out=tni[:], in0=tmpA[:], scalar1=1.0 / PI2, scalar2=0.0, op0=ALU.mult,
                                           op1=ALU.add), r=["tmpA"], w=["tni"])
        A("dve", lambda e: e.scalar_tensor_tensor(out=tmpB[:], in0=tni[:], scalar=-PI2, in1=tmpA[:], op0=ALU.mult,
                                                  op1=ALU.add), r=["tni", "tmpA"], w=["tmpB"])
        A("dve", lambda e: e.tensor_scalar(out=tmpB[:], in0=tmpB[:], scalar1=-PI_SAFE, scalar2=PI_SAFE, op0=ALU.max,
                                           op1=ALU.min), r=["tmpB"], w=["tmpB"])
        A("dve", lambda e: e.scalar_tensor_tensor(out=tmpA[:], in0=tmpB[:], scalar=-1.0, in1=tmpB[:], op0=ALU.mult, op1=ALU.max),
          r=["tmpB"], w=["tmpA"])
        A("act", lambda e: e.activation(out=eji[:], in_=tmpB[:], func=AF.Sin), r=["tmpB"], w=["eji"])
        A("act", lambda e: e.activation(out=ejr[:], in_=tmpA[:], func=AF.Sin, scale=-1.0, bias=sm[5][:, 0:1]),
          r=["tmpA", "sm5"], w=["ejr"])
        A("dve", lambda e: e.tensor_tensor(out=tmpC[:], in0=al_b, in1=jv_b, op=ALU.mult), r=["alpha", "jv"], w=["tmpC"])
        A("act", lambda e: e.activation(out=tmpC[:], in_=tmpC[:], func=AF.Exp), r=["tmpC"], w=["tmpC"])
        A("dve", lambda e: e.tensor_tensor(out=ejr[:], in0=ejr[:], in1=tmpC[:], op=ALU.mult), r=["ejr", "tmpC"], w=["ejr"])
        A("dve", lambda e: e.tensor_tensor(out=eji[:], in0=eji[:], in1=tmpC[:], op=ALU.mult), r=["eji", "tmpC"], w=["eji"])
        e1r, e1i = ejr[:, 8, :], eji[:, 8, :]
        A("dve", lambda e: e.tensor_scalar(out=sm[0][:], in0=e1r, scalar1=-1.0, scalar2=0.0, op0=ALU.add, op1=ALU.add),
          r=["ejr"], w=["sm0"])
        A("dve", lambda e: e.tensor_tensor(out=sm[1][:], in0=sm[0][:], in1=are[:], op=ALU.mult), r=["sm0", "are"], w=["sm1"])
        A("dve", lambda e: e.tensor_tensor(out=sm[2][:], in0=e1i, in1=aim[:], op=ALU.mult), r=["eji", "aim"], w=["sm2"])
        A("dve", lambda e: e.tensor_tensor(out=sm[1][:], in0=sm[1][:], in1=sm[2][:], op=ALU.add), r=["sm1", "sm2"], w=["sm1"])
        A("dve", lambda e: e.tensor_tensor(out=sm[2][:], in0=e1i, in1=are[:], op=ALU.mult), r=["eji", "are", "sm1"], w=["sm2"])
        A("dve", lambda e: e.tensor_tensor(out=sm[3][:], in0=sm[0][:], in1=aim[:], op=ALU.mult), r=["sm0", "aim"], w=["sm3"])
        A("dve", lambda e: e.tensor_tensor(out=sm[2][:], in0=sm[2][:], in1=sm[3][:], op=ALU.subtract), r=["sm2", "sm3"], w=["sm2"])
        A("dve", lambda e: e.tensor_tensor(out=sm[3][:], in0=are[:], in1=are[:], op=ALU.mult), r=["are", "sm2"], w=["sm3"])
        A("dve", lambda e: e.tensor_tensor(out=sm[4][:], in0=aim[:], in1=aim[:], op=ALU.mult), r=["aim"], w=["sm4"])
        A("dve", lambda e: e.tensor_tensor(out=sm[3][:], in0=sm[3][:], in1=sm[4][:], op=ALU.add), r=["sm3", "sm4"], w=["sm3"])
        A("dve", lambda e: e.reciprocal(out=sm[3][:], in_=sm[3][:]), r=["sm3"], w=["sm3"])
        A("dve", lambda e: e.tensor_tensor(out=sm[1][:], in0=sm[1][:], in1=sm[3][:], op=ALU.mult), r=["sm1", "sm3"], w=["sm1"])
        A("dve", lambda e: e.tensor_tensor(out=sm[2][:], in0=sm[2][:], in1=sm[3][:], op=ALU.mult), r=["sm2", "sm3"], w=["sm2"])
        br_b = bc(sm[1][:].unsqueeze(2), [128, 128, 16]); bi_b = bc(sm[2][:].unsqueeze(2), [128, 128, 16])
        tA3 = tmpA[:].rearrange("p a b -> p b a"); tB3 = tmpB[:].rearrange("p a b -> p b a")
        A("dve", lambda e: e.tensor_tensor(out=tA3, in0=br_b, in1=btr[:], op=ALU.mult), r=["sm1", "btr", "ejr", "eji"], w=["tmpA"])
        A("dve", lambda e: e.tensor_tensor(out=tB3, in0=bi_b, in1=bti[:], op=ALU.mult), r=["sm2", "bti", "ejr", "eji"], w=["tmpB"])
        A("dve", lambda e: e.tensor_tensor(out=bbr[:], in0=tA3, in1=tB3, op=ALU.subtract), r=["tmpA", "tmpB"], w=["bbr"])
        A("dve", lambda e: e.tensor_tensor(out=tA3, in0=br_b, in1=bti[:], op=ALU.mult), r=["sm1", "bti", "bbr"], w=["tmpA"])
        A("dve", lambda e: e.tensor_tensor(out=tB3, in0=bi_b, in1=btr[:], op=ALU.mult), r=["sm2", "btr", "bbr"], w=["tmpB"])
        A("dve", lambda e: e.tensor_tensor(out=bbi[:], in0=tA3, in1=tB3, op=ALU.add), r=["tmpA", "tmpB"], w=["bbi"])
        T.emit(block)
        es1.close()
        if STOP_S5_SETUP[0]:
            return
        block = es.enter_context(nc.Block())
        s5_groups(nc, T, es, sb, locals())
        T.emit(block)


def s5_groups(nc, T, es, sb, V):
    A = T.add
    kio, identf, maskf, maskb, dcol = V["kio"], V["identf"], V["maskf"], V["maskb"], V["dcol"]
    ejr, eji, bbr, bbi, ctr, cti = V["ejr"], V["eji"], V["bbr"], V["bbi"], V["ctr"], V["cti"]
    phi, phi2pi, rho = V["phi"], V["phi2pi"], V["rho"]
    pss, psZA, psZB, psZc, psY = V["pss"], V["psZA"], V["psZB"], V["psZc"], V["psY"]
    DSCR, YSC, sm = V["DSCR"], V["YSC"], V["sm"]
    PI2 = TWO_PI
    ga = sb("ga", [128, 2, 8, 128]); hm = sb("hm", [128, 2, 8, 128])
    t1 = sb("gt1", [128, 8, 128]); t2 = sb("gt2", [128, 8, 128])
    winjA = sb("winjA", [128, 2, 8, 128], BF16); winjB = sb("winjB", [128, 2, 8, 128], BF16)
    wA = sb("wA", [128, 2, 8, 128], BF16); wB = sb("wB", [128, 2, 8, 128], BF16)
    toep = sb("toep", [128, 8, 128], BF16)
    tt1 = sb("tt1", [128, 4, 128]); tt2 = sb("tt2", [128, 4, 128])
    dg = [sb("dg0", [128, NK], BF16), sb("dg1", [128, NK], BF16)]
    ni = sb("ni", [128, NK], I32); ang = sb("ang", [128, NK]); rr = sb("rr", [128, NK]); ra = sb("ra", [128, NK])
    ts = sb("ts", [128, NK]); tc = sb("tc", [128, NK])
    m1 = sb("m1", [128, NK]); m2 = sb("m2", [128, NK]); q = sb("q", [128, NK])
    FC = [sb("FC%d" % d, [128, NK + 1], BF16) for d in range(2)]
    FS = [sb("FS%d" % d, [128, NK + 1], BF16) for d in range(2)]
    ytile = [sb("ytile%d" % i, [128, 9, 8, 128], BF16) for i in range(2)]
    halfpi = sm[5]
    for d in range(2):
        A("pool", lambda e, d=d: e.memset(FC[d][:], 0.0), w=[("FC", d)])
        A("pool", lambda e, d=d: e.memset(FS[d][:], 0.0), w=[("FS", d)])

    def eview(t, lo, hi, rev, d, gb, parts):
        sl = t[parts, lo:hi, d * 64 + gb * 8:d * 64 + gb * 8 + 8]
        if rev:
            sl = t[parts, hi - 1:(lo - 1 if lo > 0 else None):-1, d * 64 + gb * 8:d * 64 + gb * 8 + 8]
        return bc(sl.rearrange("p s g -> p g s").unsqueeze(3), [64, 8, 8, 16])

    def gview(t, d, gb, parts):
        return bc(t[parts, d * 64 + gb * 8:d * 64 + gb * 8 + 8, :].unsqueeze(2), [64, 8, 8, 16])

    def cplx(eng, outv, parts, Er, Ei, Xr, Xi, kind, rk, wk):
        a = t1[parts].rearrange("p g (s h) -> p g s h", h=16); b = t2[parts].rearrange("p g (s h) -> p g s h", h=16)
        pk = "lo" if parts.start == 0 else "hi"
        X1, X2 = (Xr, Xi) if kind in ("re", "nre") else (Xi, Xr)
        A(eng, lambda e: e.tensor_tensor(out=a, in0=Er, in1=X1, op=ALU.mult), r=rk, w=[("t1", pk)])
        A(eng, lambda e: e.tensor_tensor(out=b, in0=Ei, in1=X2, op=ALU.mult), r=rk, w=[("t2", pk)])
        if kind == "re":
            A(eng, lambda e: e.tensor_tensor(out=outv, in0=a, in1=b, op=ALU.subtract), r=[("t1", pk), ("t2", pk)], w=wk)
        elif kind == "im":
            A(eng, lambda e: e.tensor_tensor(out=outv, in0=a, in1=b, op=ALU.add), r=[("t1", pk), ("t2", pk)], w=wk)
        elif kind == "nre":
            A(eng, lambda e: e.tensor_tensor(out=outv, in0=b, in1=a, op=ALU.subtract), r=[("t1", pk), ("t2", pk)], w=wk)
        else:
            A(eng, lambda e: e.tensor_tensor(out=a, in0=a, in1=b, op=ALU.add), r=[("t1", pk), ("t2", pk)], w=[("t1", pk)])
            A(eng, lambda e: e.tensor_scalar(out=outv, in0=a, scalar1=-1.0, scalar2=0.0, op0=ALU.mult, op1=ALU.add),
              r=[("t1", pk)], w=wk)

    lo, hi = slice(0, 64), slice(64, 128)
    pkeys = ["ejr", "eji", "bbr", "bbi", "ctr", "cti"]
    for gb in range(8):
        yt = ytile[gb % 2]
        ykey = ("ytile", gb % 2)
        for d in range(2):
            args = (7, 15, d == 0, d, gb)
            for parts, kind, eng in ((lo, "re", "dve"), (hi, "im", "pool")):
                ov = ga[parts, d].rearrange("p g (s h) -> p g s h", h=16)
                cplx(eng, ov, parts, eview(ejr, *args, parts), eview(eji, *args, parts),
                     gview(bbr, d, gb, parts), gview(bbi, d, gb, parts), kind, pkeys, [("ga", d)])
            args = (0, 8, d == 1, d, gb)
            for parts, kind, eng in ((lo, "re", "dve"), (hi, "nim", "pool")):
                ov = hm[parts, d].rearrange("p g (s h) -> p g s h", h=16)
                cplx(eng, ov, parts, eview(ejr, *args, parts), eview(eji, *args, parts),
                     gview(ctr, d, gb, parts), gview(cti, d, gb, parts), kind, pkeys, [("hm", d)])
            args = (8, 16, d == 1, d, gb)
            for (wt_, wn, kinds) in ((wA, "wA", ("re", "nim")), (wB, "wB", ("nim", "nre"))):
                for parts, kind, eng in ((lo, kinds[0], "dve"), (hi, kinds[1], "pool")):
                    ov = wt_[parts, d].rearrange("p g (s h) -> p g s h", h=16)
                    cplx(eng, ov, parts, eview(ejr, *args, parts), eview(eji, *args, parts),
                         gview(ctr, d, gb, parts), gview(cti, d, gb, parts), kind, pkeys + ["Ymm"], [(wn, d)])
        for d in range(2):
            for gq in range(2):
                for g4 in range(4):
                    g8 = gq * 4 + g4
                    A("pe", lambda e, d=d, g8=g8, g4=g4: e.transpose(out=pss[:, g4, :], in_=ga[:, d, g8, :], identity=identf[:]),
                      r=[("ga", d), "identf"], w=["pss"])
                A("dve", lambda e, d=d, gq=gq: e.tensor_copy(out=winjA[:, d, gq * 4:gq * 4 + 4, :], in_=pss[:]),
                  r=["pss", "Zmm"], w=["winjA"])
                A("dve", lambda e, d=d, gq=gq: e.tensor_copy(out=winjB[:, d, gq * 4:gq * 4 + 4, 0:64], in_=pss[:, :, 64:128]),
                  r=["pss", "Zmm"], w=["winjB"])
                A("dve", lambda e, d=d, gq=gq: e.tensor_scalar(out=winjB[:, d, gq * 4:gq * 4 + 4, 64:128], in0=pss[:, :, 0:64],
                                                               scalar1=-1.0, scalar2=0.0, op0=ALU.mult, op1=ALU.add),
                  r=["pss", "Zmm"], w=["winjB"])
        for gq in range(2):
            for d in range(2):
                for g4 in range(4):
                    g8 = gq * 4 + g4
                    A("pe", lambda e, d=d, g8=g8, g4=g4: e.matmul(pss[:, g4, :], lhsT=ga[:, d, g8, :], rhs=hm[:, d, g8, :],
                                                                 start=True, stop=True),
                      r=[("ga", d), ("hm", d)], w=["pss"])
                mk = maskf if d == 0 else maskb
                tt = tt1 if d == 0 else tt2
                A("dve", lambda e, mk=mk, tt=tt: e.tensor_tensor(out=tt[:], in0=pss[:], in1=bc(mk[:].unsqueeze(1), [128, 4, 128]),
                                                                op=ALU.mult), r=["pss", "maskf", "maskb"], w=["tt%d" % d])
            A("dve", lambda e: e.tensor_tensor(out=tt1[:], in0=tt1[:], in1=tt2[:], op=ALU.add), r=["tt0", "tt1"], w=["tt0"])
            for g4 in range(4):
                g8 = gq * 4 + g4
                g = gb * 8 + g8
                A("dve", lambda e, g4=g4, g8=g8, g=g: e.scalar_tensor_tensor(
                    out=toep[:, g8, :], in0=identf[:], scalar=dcol[:, g:g + 1], in1=tt1[:, g4, :], op0=ALU.mult, op1=ALU.add),
                    r=["tt0", "dcol", "identf", "Ymm"], w=["toep"])
        for g8 in range(8):
            g = gb * 8 + g8
            dgt = dg[g % 2]
            dkey = ("dg", g % 2)
            T.dma(dgt[:], DSCR[g], w=[dkey], stream="dg%d" % (g % 2))
            for d in range(2):
                col = d * 64 + g
                for (pz, wj, zk) in ((psZA, winjA, "psZA"), (psZB, winjB, "psZB")):
                    for hf in range(2):
                        A("pe", lambda e, pz=pz, wj=wj, hf=hf, d=d, g8=g8, dgt=dgt: e.matmul(
                            pz[:, hf, :], lhsT=wj[:, d, g8, :], rhs=dgt[:, NKC + 512 * hf:NKC + 512 * (hf + 1)],
                            start=True, stop=True), r=[dkey, wj is winjA and "winjA" or "winjB"], w=[zk, "Zmm"])
                A("pe", lambda e, d=d, g8=g8, dgt=dgt: e.matmul(psZc[:, 0:32], lhsT=winjA[:, d, g8, :], rhs=dgt[:, 0:NKC],
                                                               start=True, stop=True), r=[dkey, "winjA"], w=["psZc", "Zmm"])
                A("pe", lambda e, d=d, g8=g8, dgt=dgt: e.matmul(psZc[:, 32:64], lhsT=winjB[:, d, g8, :], rhs=dgt[:, 0:NKC],
                                                               start=True, stop=True), r=[dkey, "winjB"], w=["psZc", "Zmm"])
                A("dve", lambda e, col=col: e.tensor_scalar(out=ni[:], in0=kio[:], scalar1=phi2pi[:, col:col + 1], scalar2=0.0,
                                                            op0=ALU.mult, op1=ALU.add), r=["kio", "phi2pi"], w=["ni"])
                A("pool", lambda e, col=col: e.tensor_scalar(out=ang[:], in0=kio[:], scalar1=phi[:, col:col + 1], scalar2=0.0,
                                                             op0=ALU.mult, op1=ALU.add), r=["kio", "phi"], w=["ang"])
                A("dve", lambda e: e.scalar_tensor_tensor(out=rr[:], in0=ni[:], scalar=-PI2, in1=ang[:], op0=ALU.mult,
                                                          op1=ALU.add), r=["ni", "ang"], w=["rr"])
                A("dve", lambda e: e.tensor_scalar(out=rr[:], in0=rr[:], scalar1=-PI_SAFE, scalar2=PI_SAFE, op0=ALU.max,
                                                   op1=ALU.min), r=["rr"], w=["rr"])
                A("dve", lambda e: e.scalar_tensor_tensor(out=ra[:], in0=rr[:], scalar=-1.0, in1=rr[:], op0=ALU.mult, op1=ALU.max),
                  r=["rr"], w=["ra"])
                A("act", lambda e: e.activation(out=ts[:], in_=rr[:], func=AF.Sin), r=["rr"], w=["ts"])
                A("act", lambda e: e.activation(out=tc[:], in_=ra[:], func=AF.Sin, scale=-1.0, bias=halfpi[:, 0:1]),
                  r=["ra", "sm5"], w=["tc"])
                zaf = psZA[:].rearrange("p a b -> p (a b)"); zbf = psZB[:].rearrange("p a b -> p (a b)")
                if d == 0:
                    segs = [(slice(0, NKC), psZc[:, 0:32], psZc[:, 32:64]), (slice(NKC, NK), zaf, zbf)]
                else:
                    segs = [(slice(0, NKC), psZc[:, 31::-1], psZc[:, 63:31:-1]), (slice(NKC, NK), zaf[:, ::-1], zbf[:, ::-1])]
                for (js, za, zb) in segs:
                    A("dve", lambda e, js=js, za=za: e.tensor_tensor(out=m1[:, js], in0=za, in1=tc[:, js], op=ALU.mult),
                      r=["psZA", "psZc", "tc"], w=["m1"])
                    A("dve", lambda e, js=js, zb=zb: e.tensor_tensor(out=m2[:, js], in0=zb, in1=ts[:, js], op=ALU.mult),
                      r=["psZB", "psZc", "ts"], w=["m2"])
                A("pool", lambda e: e.tensor_tensor(out=m1[:], in0=m1[:], in1=m2[:], op=ALU.add), r=["m1", "m2"], w=["m1"])
                A("dve", lambda e, col=col: e.tensor_tensor_scan(out=q[:], data0=bc(rho[:, col:col + 1], [128, NK]), data1=m1[:],
                                                                 initial=0.0, op0=ALU.mult, op1=ALU.add),
                  r=["m1", "rho"], w=["q"])
                for (Fb, tab, fk) in ((FC[d], tc, ("FC", d)), (FS[d], ts, ("FS", d))):
                    if d == 0:
                        A("pool", lambda e, Fb=Fb, tab=tab: e.tensor_tensor(out=Fb[:, 1:NK + 1], in0=q[:], in1=tab[:], op=ALU.mult),
                          r=["q", "tc", "ts", "Ymm"], w=[fk])
                    else:
                        A("pool", lambda e, Fb=Fb, tab=tab: e.tensor_tensor(out=Fb[:, 31:0:-1], in0=q[:, 0:31], in1=tab[:, 0:31],
                                                                           op=ALU.mult), r=["q", "tc", "ts", "Ymm"], w=[fk])
                        A("pool", lambda e, Fb=Fb, tab=tab: e.tensor_tensor(out=Fb[:, NK - 1:32:-1], in0=q[:, 32:NK - 1],
                                                                           in1=tab[:, 32:NK - 1], op=ALU.mult),
                          r=["q", "tc", "ts", "Ymm"], w=[fk])
                        A("pool", lambda e, Fb=Fb, tab=tab: e.tensor_tensor(out=Fb[:, NK:NK + 1], in0=q[:, 31:32], in1=tab[:, 31:32],
                                                                           op=ALU.mult), r=["q", "tc", "ts", "Ymm"], w=[fk])
            rounds = [(0, [(0, NKC, 0)] + [(1 + i, 128, NKC + 128 * i) for i in range(3)]),
                      (1, [(4 + i, 128, NKC + 128 * (3 + i)) for i in range(4)]),
                      (0, [(8, 128, NKC + 128 * 7)])]
            for (pb, blks) in rounds:
                py = psY[pb]
                for slot, (kb, nk, p0) in enumerate(blks):
                    ops = [(dgt[:, p0:p0 + nk], toep[:, g8, :], [dkey, "toep"]),
                           (FC[0][:, p0:p0 + nk], wA[:, 0, g8, :], [("FC", 0), ("wA", 0)]),
                           (FS[0][:, p0:p0 + nk], wB[:, 0, g8, :], [("FS", 0), ("wB", 0)]),
                           (FC[1][:, p0 + 1:p0 + 1 + nk], wA[:, 1, g8, :], [("FC", 1), ("wA", 1)]),
                           (FS[1][:, p0 + 1:p0 + 1 + nk], wB[:, 1, g8, :], [("FS", 1), ("wB", 1)])]
                    for oi, (lt, rh, rk) in enumerate(ops):
                        A("pe", lambda e, py=py, slot=slot, nk=nk, lt=lt, rh=rh, oi=oi: e.matmul(
                            py[:nk, slot, :], lhsT=lt, rhs=rh, start=(oi == 0), stop=(oi == 4)),
                            r=rk, w=[("psY", pb), "Ymm"])
                if blks[0][1] == NKC:
                    A("act", lambda e, py=py, yt=yt, g8=g8: e.activation(
                        out=yt[:NKC, 0, :, g8 * 16:(g8 + 1) * 16], in_=py[:NKC, 0, :].rearrange("k (t h) -> k t h", h=16),
                        func=AF.Gelu), r=[("psY", pb)], w=[ykey])
                    rest = blks[1:]
                    s0 = 1
                else:
                    rest = blks
                    s0 = 0
                kb0 = rest[0][0]
                nb = len(rest)
                A("act", lambda e, py=py, yt=yt, g8=g8, kb0=kb0, nb=nb, s0=s0: e.activation(
                    out=yt[:, kb0:kb0 + nb, :, g8 * 16:(g8 + 1) * 16],
                    in_=py[:, s0:s0 + nb, :].rearrange("k b (t h) -> k b t h", h=16), func=AF.Gelu),
                    r=[("psY", pb)], w=[ykey])
        T.dma(YSC[0:C, gb * 128:(gb + 1) * 128].rearrange("(k t) f -> k t f", t=8), yt[:NKC, 0], r=[ykey], w=["YSC"],
              stream="ysc", eng="pool")
        for kb in range(8):
            T.dma(YSC[C + kb * 1024:C + (kb + 1) * 1024, gb * 128:(gb + 1) * 128].rearrange("(k t) f -> k t f", t=8),
                  yt[:, 1 + kb], r=[ykey], w=["YSC"], stream="ysc", eng="pool")


def phase_post0(nc, T, blocks, VEC, YSC, SZ, w_glu, b_glu, w_out, identb, X1, CTX1):
    with ExitStack() as es:
        sb = lambda n, s, d=F32: es.enter_context(nc.sbuf_tensor("p0_" + n, s, d))
        stage = [sb("stg0", [128, D]), sb("stg1", [128, D])]
        block = es.enter_context(nc.Block())
        A = T.add
        grep = []
        for n in range(2):
            t = sb("gate%d" % n, [128, D])
            T.dma(t[:], VEC[n * 3 + 2, :].partition_broadcast(128), w=[("gate", n)], stream="gate%d" % n)
            grep.append(t)
        wg = load_cast_weight(nc, T, es, "p0_wglu", w_glu, D, stage)
        wo = load_cast_weight(nc, T, es, "p0_wout", w_out, D, stage)
        bgf = sb("bgf", [1, D]); bgb = sb("bgb", [1, D], BF16); ones = sb("ones", [1, 128], BF16)
        T.dma(bgf[:], b_glu.rearrange("(o f) -> o f", o=1), w=["bgf"], stream="bgf")
        A("dve", lambda e: e.tensor_copy(out=bgb[:], in_=bgf[:]), r=["bgf"], w=["bgb"])
        A("dve", lambda e: e.memset(ones[:], 1.0), w=["ones"])
        yg = sb("yg", [128, 8, D], BF16); szt = sb("szt", [128, 8, D], BF16)
        ygT = sb("ygT", [128, 8, 8, 128], BF16)
        sg = [sb("sg0", [128, D], BF16), sb("sg1", [128, D], BF16)]
        y3 = sb("y3", [128, 8, D], BF16); y3T = sb("y3T", [128, 8, 8, 128], BF16)
        xh = [sb("xh0", [128, 4, D]), sb("xh1", [128, 4, D])]
        tmp = [sb("tmp0", [128, 512]), sb("tmp1", [128, 512])]
        psT = [es.enter_context(nc.psum_tensor("p0psT%d" % i, [128, 8, 128], BF16)) for i in range(2)]
        psU = [es.enter_context(nc.psum_tensor("p0psU%d" % i, [128, 512], F32)) for i in range(4)]
        cnt = {"ev": 0, "u": 0, "tmp": 0}

        def transposes(src, dstT, skey, dkey, nk):
            for ft in range(8):
                pt = psT[ft % 2]
                for t in range(8):
                    A("pe", lambda e, pt=pt, t=t, ft=ft: e.transpose(
                        out=pt[:, t, :nk], in_=src[:nk, t, ft * 128:(ft + 1) * 128], identity=identb[:nk, :nk]),
                        r=[(skey, t), "identb"], w=[("psT", ft % 2)])
                if cnt["ev"] % 2 == 0:
                    A("act", lambda e, pt=pt, ft=ft: e.copy(out=dstT[:, ft, :, :nk], in_=pt[:, :, :nk]),
                      r=[("psT", ft % 2)], w=[(dkey, ft)])
                else:
                    A("dve", lambda e, pt=pt, ft=ft: e.tensor_copy(out=dstT[:, ft, :, :nk], in_=pt[:, :, :nk]),
                      r=[("psT", ft % 2)], w=[(dkey, ft)])
                cnt["ev"] += 1

        def do_block(nk, xap, kg0, row0, is_ctx, bidx):
            n = 1 if is_ctx else 0
            xv = xap.rearrange("(k s) f -> k s f", s=8)
            T.dma(yg[:nk], YSC[row0:row0 + nk * 8, :].rearrange("(k t) f -> k t f", t=8), w=[("yg", t) for t in range(8)],
                  stream="yg")
            T.dma(szt[:nk], SZ[row0:row0 + nk * 8, :].rearrange("(k t) f -> k t f", t=8), w=["szt"], stream="szt")
            for half in range(2):
                T.dma(xh[half][:nk], xv[:, half * 4:(half + 1) * 4, :], w=[("xh", half)], stream="xh%d" % half)
            transposes(yg, ygT, "yg", "ygT", nk)
            for t in range(8):
                sgt = sg[t % 2]
                for hf in range(2):
                    pu = psU[cnt["u"] % 4]; pkey = ("psU", cnt["u"] % 4); cnt["u"] += 1
                    A("pe", lambda e, pu=pu, hf=hf: e.matmul(pu[:nk, :], lhsT=ones[0:1, :nk], rhs=bgb[0:1, hf * 512:(hf + 1) * 512],
                                                            start=True, stop=False), r=["ones", "bgb"], w=[pkey])
                    for ft in range(8):
                        A("pe", lambda e, pu=pu, hf=hf, ft=ft, t=t: e.matmul(
                            pu[:nk, :], lhsT=ygT[:, ft, t, :nk], rhs=wg[:, ft, hf * 512:(hf + 1) * 512],
                            start=False, stop=(ft == 7)), r=[("ygT", ft), "p0_wglu"], w=[pkey])
                    A("act", lambda e, pu=pu, hf=hf, sgt=sgt: e.activation(out=sgt[:nk, hf * 512:(hf + 1) * 512], in_=pu[:nk, :],
                                                                          func=AF.Sigmoid), r=[pkey], w=[("sg", t % 2)])
                A("dve", lambda e, t=t, sgt=sgt: e.tensor_tensor(out=sgt[:nk], in0=sgt[:nk], in1=yg[:nk, t, :], op=ALU.mult),
                  r=[("sg", t % 2), ("yg", t)], w=[("sg", t % 2)])
                A("pool", lambda e, t=t, sgt=sgt: e.tensor_tensor(out=y3[:nk, t, :], in0=sgt[:nk], in1=szt[:nk, t, :], op=ALU.mult),
                  r=[("sg", t % 2), "szt"], w=[("y3", t)])
            transposes(y3, y3T, "y3", "y3T", nk)
            for t in range(8):
                half, s4 = divmod(t, 4)
                for hf in range(2):
                    pu = psU[cnt["u"] % 4]; pkey = ("psU", cnt["u"] % 4); cnt["u"] += 1
                    for ft in range(8):
                        A("pe", lambda e, pu=pu, hf=hf, ft=ft, t=t: e.matmul(
                            pu[:nk, :], lhsT=y3T[:, ft, t, :nk], rhs=wo[:, ft, hf * 512:(hf + 1) * 512],
                            start=(ft == 0), stop=(ft == 7)), r=[("y3T", ft), "p0_wout"], w=[pkey])
                    tm = tmp[cnt["tmp"] % 2]; tkey = ("tmp", cnt["tmp"] % 2); cnt["tmp"] += 1
                    A("dve", lambda e, pu=pu, hf=hf, tm=tm, n=n: e.tensor_tensor(
                        out=tm[:nk], in0=pu[:nk, :], in1=grep[n][:nk, hf * 512:(hf + 1) * 512], op=ALU.mult),
                        r=[pkey, ("gate", n)], w=[tkey])
                    A("pool", lambda e, hf=hf, tm=tm, half=half, s4=s4: e.tensor_tensor(
                        out=xh[half][:nk, s4, hf * 512:(hf + 1) * 512], in0=xh[half][:nk, s4, hf * 512:(hf + 1) * 512],
                        in1=tm[:nk], op=ALU.add), r=[tkey, ("xh", half)], w=[("xh", half)])
            dst = CTX1 if is_ctx else X1[bidx * 1024:(bidx + 1) * 1024, :]
            dv = dst.rearrange("(k s) f -> k s f", s=8)
            for half in range(2):
                T.dma(dv[:, half * 4:(half + 1) * 4, :], xh[half][:nk], r=[("xh", half)], w=["X1"], stream="x1st", eng="pool")
        for blk in blocks:
            do_block(*blk)
        T.emit(block)


def phase_attn(nc, T, VEC, X1, CTX1, w_in, q_norm, k_norm, w_out, fin_g, identf, identb, posr_in, posc_in, fidx_in,
               out, stop_after):
    NKT = 66
    A = T.add
    with ExitStack() as es:
        sb = lambda n, s, d=F32: es.enter_context(nc.sbuf_tensor("a_" + n, s, d))
        KT = sb("KT", [128, 2, NKT * 128], BF16)
        Vt = sb("V", [128, NKT, 4, 65], BF16)
        cosr = sb("cosr", [128, 8, 16]); sinr = sb("sinr", [128, 8, 16])
        cosc = sb("cosc", [128, 8, 16]); sinc = sb("sinc", [128, 8, 16])
        rep = {}
        for nm in ("gs0", "sh0"):
            rep[nm] = sb("rep_" + nm, [128, D])
        qn_rep = sb("qn_rep", [128, 64]); kn_rep = sb("kn_rep", [128, 64])
        halfpi = sb("halfpi", [128, 1])
        xh = [sb("xh0", [128, 4, D])] * 2
        tmp32 = [sb("tmp32a", [128, D])] * 2
        ss = sb("ss", [128, 8]); ms = sb("ms", [128, 8]); rstd = sb("rstd", [128, 8])
        h = sb("h", [128, 4, D], BF16)
        hT = sb("hT", [128, 8, 4, 128], BF16)
        sq = sb("sq", [128, 512]); qn = [sb("qn0", [128, 512])] * 2
        junk = sq[:].bitcast(BF16)
        ssh = sb("ssh", [128, 8]); msh = sb("msh", [128, 8]); rsh = sb("rsh", [128, 8])
        ra_ = sb("ra", [128, 512]); rb_ = sb("rb", [128, 512])
        cc = sb("cc", [128, 8, 2, 16]); cs_ = sb("cs", [128, 8, 2, 16])
        krope = sb("krope", [128, 4, 64], BF16)
        psU = [es.enter_context(nc.psum_tensor("apsU0", [128, 512], F32))] * 2
        psS = [es.enter_context(nc.psum_tensor("apsS%d" % i, [128, 2, 512], F32)) for i in range(2)]
        psO = [es.enter_context(nc.psum_tensor("apsO%d" % i, [128, 512], F32)) for i in range(2)]
        psN = es.enter_context(nc.psum_tensor("apsN", [128, 4, 128], F32))
        psTb = psN[:].rearrange("p a b -> p (a b)").bitcast(BF16).rearrange("p (a b) -> p a b", b=128)
        cnt = {"u": 0, "ev": 0, "s": 0}
        pu_bufs = [(psU[0][:, :], ("pu", 0))] + [(psS[i_][:, c_, :], ("bank", i_, c_)) for i_ in range(2) for c_ in range(2)]

        def next_pu():
            r_ = pu_bufs[cnt["u"] % len(pu_bufs)]
            cnt["u"] += 1
            return r_

        def norm_unit(nk, half, n, ns=4):
            xt = xh[half]
            A("dve", lambda e: e.memset(ss[:], 0.0), w=["ss"])
            for s4 in range(ns):
                A("act", lambda e, s4=s4: e.activation(out=junk[:nk, :], in_=xt[:nk, s4, :], func=AF.Square,
                                                       accum_out=ss[:nk, s4:s4 + 1]), r=[("xh", 0)], w=["sq", "ss"])
            A("dve", lambda e: e.tensor_scalar(out=ms[:nk, 0:ns], in0=ss[:nk, 0:ns], scalar1=1.0 / D, scalar2=EPS, op0=ALU.mult,
                                               op1=ALU.add), r=["ss"], w=["ms"])
            A("act", lambda e: e.sqrt(out=ms[:nk, 0:ns], in_=ms[:nk, 0:ns]), r=["ms"], w=["ms"])
            A("dve", lambda e: e.reciprocal(out=rstd[:nk, 0:ns], in_=ms[:nk, 0:ns]), r=["ms"], w=["rstd"])
            for s4 in range(ns):
                tm = tmp32[s4 % 2]
                A("dve", lambda e, s4=s4, tm=tm: e.scalar_tensor_tensor(
                    out=tm[:nk], in0=xt[:nk, s4, :], scalar=rstd[:nk, s4:s4 + 1], in1=rep["gs%d" % n][:nk],
                    op0=ALU.mult, op1=ALU.mult), r=[("xh", 0), "rstd", "rep"], w=[("tmp32", 0)])
                A("pool", lambda e, s4=s4, tm=tm: e.tensor_tensor(out=h[:nk, s4, :], in0=tm[:nk], in1=rep["sh%d" % n][:nk],
                                                                 op=ALU.add), r=[("tmp32", 0), "rep"], w=["h"])
            tok_transposes(h, nk, ns)

        def tok_transposes(src, nk, ns=4):
            for ft in range(8):
                for s4 in range(ns):
                    A("pe", lambda e, s4=s4, ft=ft: e.transpose(out=psTb[:, s4, :nk], in_=src[:nk, s4, ft * 128:(ft + 1) * 128],
                                                                identity=identb[:nk, :nk]), r=["h", "identb"], w=["psNT"])
                if nk == 128:
                    ov_ = hT[:, ft, 0:ns, :]
                else:
                    ov_ = hT[:, ft].rearrange("p s k -> p (s k)")[:, 0:ns * nk].rearrange("p (s k) -> p s k", k=nk)
                if cnt["ev"] % 2 == 0:
                    A("act", lambda e, ft=ft, ov_=ov_: e.copy(out=ov_, in_=psTb[:, 0:ns, :nk]), r=["psNT"], w=["hT"])
                else:
                    A("dve", lambda e, ft=ft, ov_=ov_: e.tensor_copy(out=ov_, in_=psTb[:, 0:ns, :nk]), r=["psNT"], w=["hT"])
                cnt["ev"] += 1

        def head_norm(pu, pkey, nh, norm_rep_t, outf):
            w = nh * 64
            A("act", lambda e: e.activation(out=sq[:, 0:w], in_=pu[:, 0:w], func=AF.Square), r=[pkey], w=["sq"])
            A("dve", lambda e: e.tensor_reduce(out=ssh[:, 0:nh], in_=sq[:, 0:w].rearrange("p (a d) -> p a d", d=64), axis=AX.X,
                                               op=ALU.add), r=["sq"], w=["ssh"])
            A("dve", lambda e: e.tensor_scalar(out=msh[:, 0:nh], in0=ssh[:, 0:nh], scalar1=1.0 / 64, scalar2=EPS, op0=ALU.mult,
                                               op1=ALU.add), r=["ssh"], w=["msh"])
            A("act", lambda e: e.sqrt(out=msh[:, 0:nh], in_=msh[:, 0:nh]), r=["msh"], w=["msh"])
            A("dve", lambda e: e.reciprocal(out=rsh[:, 0:nh], in_=msh[:, 0:nh]), r=["msh"], w=["rsh"])
            A("dve", lambda e: e.tensor_tensor(out=outf[:, 0:w].rearrange("p (a d) -> p a d", d=64),
                                               in0=pu[:, 0:w].rearrange("p (a d) -> p a d", d=64),
                                               in1=bc(rsh[:, 0:nh].unsqueeze(2), [128, nh, 64]), op=ALU.mult),
              r=[pkey, "rsh"], w=["qnf"])
            A("pool", lambda e: e.tensor_tensor(out=outf[:, 0:w].rearrange("p (a d) -> p a d", d=64),
                                                in0=outf[:, 0:w].rearrange("p (a d) -> p a d", d=64),
                                                in1=bc(norm_rep_t[:].unsqueeze(1), [128, nh, 64]), op=ALU.mult),
              r=["qnf", "nrep"], w=["qnf"])

        def rope(src, nh, s, outv):
            sv = src[:, 0:nh * 64].rearrange("p (a x t f) -> p a x t f", x=2, t=2, f=16)
            ov = outv.rearrange("p a (x t f) -> p a x t f", x=2, t=2, f=16)
            x1, x2 = sv[:, :, :, 0, :], sv[:, :, :, 1, :]
            cb = bc(cc[:, s].unsqueeze(1), [128, nh, 2, 16]); sbb = bc(cs_[:, s].unsqueeze(1), [128, nh, 2, 16])
            n2 = nh * 32
            av = ra_[:, 0:n2].rearrange("p (a x f) -> p a x f", x=2, f=16)
            bv = rb_[:, 0:n2].rearrange("p (a x f) -> p a x f", x=2, f=16)
            av2 = ra_[:, n2:2 * n2].rearrange("p (a x f) -> p a x f", x=2, f=16)
            bv2 = rb_[:, n2:2 * n2].rearrange("p (a x f) -> p a x f", x=2, f=16)
            A("dve", lambda e: e.tensor_tensor(out=av, in0=x1, in1=cb, op=ALU.mult), r=["qnf", "cc"], w=["ra"])
            A("pool", lambda e: e.tensor_tensor(out=bv, in0=x2, in1=sbb, op=ALU.mult), r=["qnf", "cc"], w=["rb"])
            A("dve", lambda e: e.tensor_tensor(out=av2, in0=x2, in1=cb, op=ALU.mult), r=["qnf", "cc"], w=["ra"])
            A("pool", lambda e: e.tensor_tensor(out=bv2, in0=x1, in1=sbb, op=ALU.mult), r=["qnf", "cc"], w=["rb"])
            A("dve", lambda e: e.tensor_tensor(out=ov[:, :, :, 0, :], in0=av, in1=bv, op=ALU.subtract), r=["ra", "rb"], w=["roped"])
            A("pool", lambda e: e.tensor_tensor(out=ov[:, :, :, 1, :], in0=av2, in1=bv2, op=ALU.add), r=["ra", "rb"], w=["roped"])

        def block_tables(b):
            for (tab, rsrc, csrc) in ((cc, cosr, cosc), (cs_, sinr, sinc)):
                A("pool", lambda e, tab=tab, rsrc=rsrc: e.tensor_copy(out=tab[:, :, 0, :], in_=bc(rsrc[:, b, :].unsqueeze(1), [128, 8, 16])),
                  r=["tabs", "roped"], w=["cc"])
                A("pool", lambda e, tab=tab, csrc=csrc: e.tensor_copy(out=tab[:, :, 1, :], in_=csrc[:]), r=["tabs", "roped"], w=["cc"])

        with ExitStack() as es1:
            sbt = lambda n, s, d=F32: es1.enter_context(nc.sbuf_tensor("a1_" + n, s, d))
            stage = [sbt("stg0", [128, 512]), sbt("stg1", [128, 512])]
            posr = sbt("posr", [128, 8]); posc = sbt("posc", [128, 8]); fidx = sbt("fidx", [128, 16]); freq = sbt("freq", [128, 16])
            ta = sbt("ta", [128, 8, 16]); tb = sbt("tb", [128, 8, 16]); tn = sbt("tn", [128, 8, 16], I32)
            rep["gs1"] = sbt("rep_gs1", [128, D]); rep["sh1"] = sbt("rep_sh1", [128, D])
            block = es1.enter_context(nc.Block())
            for nm, v in (("gs0", 6), ("sh0", 7), ("gs1", 9), ("sh1", 10)):
                T.dma(rep[nm][:], VEC[v, :].partition_broadcast(128), w=["rep"], stream="rep_" + nm)
            T.dma(qn_rep[:], q_norm.partition_broadcast(128), w=["nrep"], stream="qnr")
            T.dma(kn_rep[:], k_norm.partition_broadcast(128), w=["nrep"], stream="knr")
            T.dma(posr[:], posr_in[:, :], w=["posr"], stream="posr")
            T.dma(posc[:], posc_in[:, :], w=["posc"], stream="posc")
            T.dma(fidx[:], fidx_in[:, :], w=["fidx"], stream="fidx")
            A("pool", lambda e: e.memset(halfpi[:], float(np.pi / 2)), w=["halfpi"])
            A("pool", lambda e: e.memset(Vt[:], 1.0), w=["V"])
            A("act", lambda e: e.activation(out=freq[:], in_=fidx[:], func=AF.Exp, scale=float(-np.log(10000.0) / 16.0)),
              r=["fidx"], w=["freq"])
            for (pos, ct, st) in ((posr, cosr, sinr), (posc, cosc, sinc)):
                A("dve", lambda e, pos=pos: e.tensor_tensor(out=ta[:], in0=bc(pos[:].unsqueeze(2), [128, 8, 16]),
                                                           in1=bc(freq[:].unsqueeze(1), [128, 8, 16]), op=ALU.mult),
                  r=["posr", "posc", "freq"], w=["ta"])
                A("dve", lambda e: e.tensor_scalar(out=tn[:], in0=ta[:], scalar1=1.0 / TWO_PI, scalar2=0.0, op0=ALU.mult, op1=ALU.add),
                  r=["ta"], w=["tn"])
                A("dve", lambda e: e.scalar_tensor_tensor(out=tb[:], in0=tn[:], scalar=-TWO_PI, in1=ta[:], op0=ALU.mult, op1=ALU.add),
                  r=["tn", "ta"], w=["tb"])
                A("dve", lambda e: e.tensor_scalar(out=tb[:], in0=tb[:], scalar1=-PI_SAFE, scalar2=PI_SAFE, op0=ALU.max, op1=ALU.min),
                  r=["tb"], w=["tb"])
                A("dve", lambda e: e.scalar_tensor_tensor(out=ta[:], in0=tb[:], scalar=-1.0, in1=tb[:], op0=ALU.mult, op1=ALU.max),
                  r=["tb"], w=["ta"])
                A("act", lambda e, st=st: e.activation(out=st[:], in_=tb[:], func=AF.Sin), r=["tb"], w=["tabs"])
                A("act", lambda e, ct=ct: e.activation(out=ct[:], in_=ta[:], func=AF.Sin, scale=-1.0, bias=halfpi[:, 0:1]),
                  r=["ta", "halfpi"], w=["tabs"])
            wkv = es1.enter_context(nc.sbuf_tensor("a1_wkv", [128, 8, 512], BF16))
            for ft in range(8):
                st_ = stage[ft % 2]
                T.dma(st_[:], w_in[ft * 128:(ft + 1) * 128, 1024:1536], w=[("stage", ft % 2)], stream="stage%d" % (ft % 2))
                A("pool", lambda e, st_=st_, ft=ft: e.tensor_copy(
                    out=wkv[:, ft, 0:256].rearrange("p (gp hf d) -> p gp hf d", gp=2, hf=2),
                    in_=st_[:, 0:256].rearrange("p (hf gp d) -> p gp hf d", gp=2, hf=2)), r=[("stage", ft % 2)], w=["wkv"])
                A("pool", lambda e, st_=st_, ft=ft: e.tensor_copy(out=wkv[:, ft, 256:512], in_=st_[:, 256:512]),
                  r=[("stage", ft % 2)], w=["wkv"])

            def kv_tile(kt, lhs_fn, s, is_ctx):
                pu, pkey = next_pu()
                for ft in range(8):
                    A("pe", lambda e, pu=pu, ft=ft: e.matmul(pu[:, :], lhsT=lhs_fn(ft), rhs=wkv[:, ft, :], start=(ft == 0), stop=(ft == 7)),
                      r=["hT", "wkv"], w=[pkey])
                A("act", lambda e, pu=pu: e.copy(out=Vt[:, kt, :, 0:64], in_=pu[:, 256:512].rearrange("p (g d) -> p g d", d=64)),
                  r=[pkey], w=["V"])
                head_norm(pu, pkey, 4, kn_rep, qn[0])
                if is_ctx:
                    A("dve", lambda e: e.tensor_copy(out=krope[:].rearrange("p a d -> p (a d)"), in_=qn[0][:, 0:256]),
                      r=["qnf"], w=["roped"])
                else:
                    rope(qn[0], 4, s, krope[:])
                for gp in range(2):
                    A("pe", lambda e, gp=gp: e.transpose(out=psTb[:, gp, :], in_=krope[:, 2 * gp:2 * gp + 2, :].rearrange("p a d -> p (a d)"), identity=identb[:]),
                      r=["roped", "identb"], w=["psNT"])
                A("dve", lambda e: e.tensor_copy(out=KT[:, :, kt * 128:(kt + 1) * 128], in_=psTb[:, 0:2, :]), r=["psNT"], w=["KT"])

            cv = CTX1.rearrange("(k s) f -> k s f", s=8)
            for j in range(2):
                T.dma(xh[0][:NKC], cv[:, j * 4:(j + 1) * 4, :], w=[("xh", 0)], stream="xh0")
                norm_unit(NKC, 0, 1)
                kv_tile(j, lambda ft: hT[:, ft].rearrange("p s k -> p (s k)")[:, 0:4 * NKC], None, True)
            for b in range(8):
                block_tables(b)
                xv = X1[b * 1024:(b + 1) * 1024, :].rearrange("(k s) f -> k s f", s=8)
                for hf in range(2):
                    T.dma(xh[0][:], xv[:, hf * 4:(hf + 1) * 4, :], w=[("xh", 0)], stream="xh0")
                    norm_unit(128, 0, 0)
                    for s4 in range(4):
                        kv_tile(2 + b * 8 + hf * 4 + s4, lambda ft, s4=s4: hT[:, ft, s4, :], hf * 4 + s4, False)
            T.emit(block)
        if stop_after == 4:
            return finish(nc, T, out)

        with ExitStack() as es2:
            sbt = lambda n, s, d=F32: es2.enter_context(nc.sbuf_tensor("a2_" + n, s, d))
            stg = tmp32[0]
            rep["gate"] = sbt("rep_gate", [128, D]); rep["fin"] = sbt("rep_fin", [128, D])
            block = es2.enter_context(nc.Block())
            T.dma(rep["gate"][:], VEC[8, :].partition_broadcast(128), w=["rep"], stream="rep_gate")
            T.dma(rep["fin"][:], fin_g.partition_broadcast(128), w=["rep"], stream="rep_fin")
            wqz = es2.enter_context(nc.sbuf_tensor("a2_wqz", [128, 8, 2048], BF16))
            wo = es2.enter_context(nc.sbuf_tensor("a2_wo", [128, 8, D], BF16))
            for ft in range(8):
                for (c0, o0) in ((0, 0), (1536, 1024)):
                    T.dma(stg[:], w_in[ft * 128:(ft + 1) * 128, c0:c0 + 1024], w=[("tmp32", 0)], stream="stg")
                    if o0 == 0:
                        A("pool", lambda e, ft=ft: e.tensor_copy(
                            out=wqz[:, ft, 0:1024].rearrange("p (hp hf d) -> p hp hf d", hp=8, hf=2),
                            in_=stg[:].rearrange("p (hf hp d) -> p hp hf d", hp=8, hf=2)), r=[("tmp32", 0)], w=["wqz"])
                    else:
                        A("pool", lambda e, ft=ft, o0=o0: e.tensor_copy(out=wqz[:, ft, o0:o0 + 1024], in_=stg[:]),
                          r=[("tmp32", 0)], w=["wqz"])
                T.dma(stg[:], w_out[ft * 128:(ft + 1) * 128, :], w=[("tmp32", 0)], stream="stg")
                A("pool", lambda e, ft=ft: e.tensor_copy(out=wo[:, ft, :], in_=stg[:]), r=[("tmp32", 0)], w=["wo"])
            qrope = sbt("qrope", [128, 16, 64], BF16)
            QT = sbt("QT", [128, 8, 512], BF16)
            szq = sbt("szq", [128, 4, D], BF16)
            PT = [sbt("PT%d" % i, [128, 2, 512], BF16) for i in range(2)]
            oT = [sbt("oT0", [65, 512]), sbt("oT1", [65, 512])]
            rec = sbt("rec", [128, 4])
            tmpo = [tmp32[0][:, 0:512], tmp32[0][:, 512:1024]]
            ucount = 0
            for b in range(8):
                block_tables(b)
                xv = X1[b * 1024:(b + 1) * 1024, :].rearrange("(k s) f -> k s f", s=8)
                ov = out[b * 1024:(b + 1) * 1024, :].rearrange("(k s) f -> k s f", s=8)
                for hf in range(2):
                    xt = xh[0]; xkey = ("xh", 0); uh = 0; ucount += 1
                    T.dma(xt[:], xv[:, hf * 4:(hf + 1) * 4, :], w=[xkey], stream="xh%d" % uh)
                    norm_unit(128, uh, 0)
                    for s4 in range(4):
                        s = hf * 4 + s4
                        for nb in range(4):
                            pu, pkey = next_pu()
                            for ft in range(8):
                                A("pe", lambda e, pu=pu, ft=ft, s4=s4, nb=nb: e.matmul(
                                    pu[:, :], lhsT=hT[:, ft, s4, :], rhs=wqz[:, ft, nb * 512:(nb + 1) * 512],
                                    start=(ft == 0), stop=(ft == 7)), r=["hT", "wqz"], w=[pkey])
                            if nb < 2:
                                qf = qn[0]
                                head_norm(pu, pkey, 8, qn_rep, qf)
                                rope(qf, 8, s, qrope[:, nb * 8:(nb + 1) * 8, :])
                            else:
                                A("act", lambda e, pu=pu, s4=s4, nb=nb: e.activation(
                                    out=szq[:, s4, (nb - 2) * 512:(nb - 1) * 512], in_=pu[:, :], func=AF.Silu), r=[pkey], w=["szq"])
                        for hp in range(8):
                            A("pe", lambda e, hp=hp, s4=s4: e.transpose(out=psTb[:, hp, :], in_=qrope[:, 2 * hp:2 * hp + 2, :].rearrange("p a d -> p (a d)"), identity=identb[:]),
                              r=["roped", "identb"], w=["psNT"])
                        A("dve", lambda e, s4=s4: e.tensor_copy(out=QT[:, :, s4 * 128:(s4 + 1) * 128], in_=psTb[:]), r=["psNT"], w=["QT"])
                    stream = [(hp, kt) for hp in range(8) for kt in range(NKT)]
                    pend = {}

                    def finalize(hd, ot):
                        for j in range(4):
                            A("pe", lambda e, j=j, ot=ot: e.transpose(out=psN[:, j, 0:65], in_=ot[:, j * 128:(j + 1) * 128],
                                                                     identity=identf[0:65, 0:65]), r=[("oT", hd // 8), "identf"], w=["psNT"])
                        A("dve", lambda e: e.reciprocal(out=rec[:], in_=psN[:, :, 64]), r=["psNT"], w=["rec"])
                        A("dve", lambda e, hd=hd: e.tensor_tensor(out=h[:, :, hd * 64:(hd + 1) * 64], in0=psN[:, :, 0:64],
                                                                 in1=bc(rec[:].unsqueeze(2), [128, 4, 64]), op=ALU.mult),
                          r=["psNT", "rec"], w=["h"])

                    def pv(idx):
                        hp, kt = stream[idx]
                        g = hp // 4
                        pt = PT[idx % 2]
                        for c in range(2):
                            A("pe", lambda e, pt=pt, kt=kt, g=g, c=c: e.matmul(
                                psO[c][0:65, :], lhsT=Vt[:, kt, g + 2 * c, :], rhs=pt[:, c, :], start=(kt == 0), stop=(kt == NKT - 1)),
                                r=[("PT", idx % 2), "V"], w=[("psO", c)])
                        if kt == NKT - 1:
                            for c in range(2):
                                A("dve", lambda e, c=c: e.tensor_copy(out=oT[c][:], in_=psO[c][0:65, :]), r=[("psO", c)], w=[("oT", c)])
                            pend[idx + 2] = hp

                    for idx, (hp, kt) in enumerate(stream):
                        gp = hp // 4
                        ps = psS[idx % 2]; pt = PT[idx % 2]
                        for c in range(2):
                            rows = slice(c * 64, c * 64 + 64)
                            A("pe", lambda e, ps=ps, rows=rows, gp=gp, kt=kt, hp=hp, c=c: e.matmul(
                                ps[:, c, :], lhsT=KT[rows, gp, kt * 128:(kt + 1) * 128], rhs=QT[rows, hp, :], start=True, stop=True,
                                tile_position=(64 * c, 0)),
                                r=["KT", "QT"], w=[("bank", idx % 2, c)])
                        A("act", lambda e, ps=ps, pt=pt: e.activation(out=pt[:], in_=ps[:], func=AF.Exp, scale=0.125),
                          r=[("bank", idx % 2, 0), ("bank", idx % 2, 1)], w=[("PT", idx % 2)])
                        if idx >= 1:
                            pv(idx - 1)
                        if idx in pend:
                            hp_ = pend.pop(idx)
                            finalize(hp_, oT[0]); finalize(hp_ + 8, oT[1])
                    pv(len(stream) - 1)
                    for k_ in sorted(pend):
                        finalize(pend[k_], oT[0]); finalize(pend[k_] + 8, oT[1])
                    A("dve", lambda e: e.tensor_tensor(out=h[:], in0=h[:], in1=szq[:], op=ALU.mult), r=["h", "szq"], w=["h"])
                    tok_transposes(h, 128)
                    for s4 in range(4):
                        for nb in range(2):
                            pu, pkey = next_pu()
                            for ft in range(8):
                                A("pe", lambda e, pu=pu, ft=ft, s4=s4, nb=nb: e.matmul(
                                    pu[:, :], lhsT=hT[:, ft, s4, :], rhs=wo[:, ft, nb * 512:(nb + 1) * 512],
                                    start=(ft == 0), stop=(ft == 7)), r=["hT", "wo"], w=[pkey])
                            tm = tmpo[nb]
                            A("dve", lambda e, pu=pu, tm=tm, nb=nb: e.tensor_tensor(out=tm, in0=pu[:, :],
                                                                                   in1=rep["gate"][:, nb * 512:(nb + 1) * 512], op=ALU.mult),
                              r=[pkey, "rep"], w=[("tmp32", 0)])
                            A("pool", lambda e, tm=tm, nb=nb, s4=s4, xt=xt: e.tensor_tensor(
                                out=xt[:, s4, nb * 512:(nb + 1) * 512], in0=xt[:, s4, nb * 512:(nb + 1) * 512], in1=tm, op=ALU.add),
                                r=[("tmp32", 0), xkey], w=[xkey])
                    A("dve", lambda e: e.memset(ss[:], 0.0), w=["ss"])
                    for s4 in range(4):
                        A("act", lambda e, s4=s4, xt=xt: e.activation(out=junk[:, :], in_=xt[:, s4, :], func=AF.Square,
                                                                     accum_out=ss[:, s4:s4 + 1]), r=[xkey], w=["sq", "ss"])
                    A("dve", lambda e: e.tensor_scalar(out=ms[:, 0:4], in0=ss[:, 0:4], scalar1=1.0 / D, scalar2=EPS, op0=ALU.mult,
                                                       op1=ALU.add), r=["ss"], w=["ms"])
                    A("act", lambda e: e.sqrt(out=ms[:, 0:4], in_=ms[:, 0:4]), r=["ms"], w=["ms"])
                    A("dve", lambda e: e.reciprocal(out=rstd[:, 0:4], in_=ms[:, 0:4]), r=["ms"], w=["rstd"])
                    for s4 in range(4):
                        A("dve", lambda e, s4=s4, xt=xt: e.scalar_tensor_tensor(
                            out=xt[:, s4, :], in0=xt[:, s4, :], scalar=rstd[:, s4:s4 + 1], in1=rep["fin"][:], op0=ALU.mult, op1=ALU.mult),
                            r=[xkey, "rstd", "rep"], w=[xkey])
                    T.dma(ov[:, hf * 4:(hf + 1) * 4, :], xt[:], r=[xkey], w=["out"], stream="outst", eng="pool")
            A("sp", None, r=["out"])
            T.emit(block)
    return None


def _host_consts():
    ident = np.eye(128, dtype=np.float32)
    sidx = np.arange(128) // 16
    maskf = (sidx[None, :] >= sidx[:, None]).astype(np.float32)
    maskb = (sidx[None, :] <= sidx[:, None]).astype(np.float32)
    kio = np.tile(np.arange(NK, dtype=np.float32)[None, :], (128, 1))
    jv = np.tile(np.arange(-7, 9, dtype=np.float32)[None, :], (128, 1))
    k = np.arange(128)
    posr = (16 * np.arange(8)[None, :] + (k // 8)[:, None]).astype(np.float32)
    posc = (8 * (k % 8)[:, None] + np.arange(8)[None, :]).astype(np.float32)
    fidx = np.tile(np.arange(16, dtype=np.float32)[None, :], (128, 1))
    return dict(ident=ident, maskf=maskf, maskb=maskb, kio=kio, jvals=jv, posr=posr, posc=posc, fidx=fidx)


def make_in_maps(inputs):
    consts = _host_consts()
    f = lambda a: np.ascontiguousarray(np.asarray(a, dtype=np.float32))
    shared = dict(
        w_mod=f(inputs["w_mod"]), b_mod=f(inputs["b_mod"]), norm_g=f(inputs["norm_g"]),
        ssm_w_in=f(inputs["ssm_w_in"][0]), ssm_a_re=f(inputs["ssm_a_re"][0]), ssm_a_im=f(inputs["ssm_a_im"][0]),
        ssm_log_dt=f(inputs["ssm_log_dt"][0]), ssm_b_re=f(inputs["ssm_b_re"][0]), ssm_b_im=f(inputs["ssm_b_im"][0]),
        ssm_c_re=f(inputs["ssm_c_re"][0]), ssm_c_im=f(inputs["ssm_c_im"][0]), ssm_d=f(inputs["ssm_d"][0]),
        ssm_w_glu=f(inputs["ssm_w_glu"][0]), ssm_b_glu=f(inputs["ssm_b_glu"][0]), ssm_w_out=f(inputs["ssm_w_out"][0]),
        attn_w_in=f(inputs["attn_w_in"][0]), attn_q_norm=f(inputs["attn_q_norm"][0]),
        attn_k_norm=f(inputs["attn_k_norm"][0]), attn_w_out=f(inputs["attn_w_out"][0]),
        final_norm_g=f(inputs["final_norm_g"]), **consts)
    maps = []
    for b in range(8):
        m = dict(shared)
        m["x"] = f(inputs["x"][b]); m["ctx"] = f(inputs["ctx"][b])
        m["cvec"] = f(np.stack([np.asarray(inputs["c"][b]), np.asarray(inputs["c_ctx"])], 0))
        maps.append(m)
    return maps


def kernel(**inputs):
    nc, _ = build_program()
    maps = make_in_maps(inputs)
    res = run_bass_kernel_spmd(nc, maps, core_ids=list(range(8)))
    return np.stack([np.asarray(r["out"], dtype=np.float32) for r in res.results], 0)
```

```python
import numpy as np
from contextlib import ExitStack
import concourse.bass as bass
import concourse.mybir as mybir
from concourse.bass_utils import run_bass_kernel_spmd

F32 = mybir.dt.float32
BF16 = mybir.dt.bfloat16
I32 = mybir.dt.int32
ALU = mybir.AluOpType
AF = mybir.ActivationFunctionType
AX = mybir.AxisListType

D = 1024
L = 8192
C = 256
NKL = 1024
NKC = 32
NK = NKL + NKC
EPS = 1e-6
TWO_PI = float(2 * np.pi)
PI_SAFE = 3.1415925
STOP_S5_SETUP = [False]


class Op:
    __slots__ = ("eng", "fn", "deps", "is_dma", "stream", "signal", "ticket", "semname")


class Tracker:
    ENGS = ["pe", "act", "dve", "pool", "sp"]

    def __init__(self, nc, es):
        self.nc = nc
        self.es = es
        self.sem = {}
        self.count = {}
        self.ops = []
        self.last_w = {}
        self.readers = {}
        self.barrier = {}
        self.waited = {e: {} for e in self.ENGS}

    def _sem(self, name):
        if name not in self.sem:
            self.sem[name] = self.es.enter_context(self.nc.semaphore(name))
            self.count[name] = 0
        return self.sem[name]

    def add(self, eng, fn, r=(), w=(), dma=False, stream=None):
        op = Op()
        op.eng, op.fn, op.is_dma, op.stream = eng, fn, dma, stream
        op.signal, op.ticket, op.semname = dma, 0, None
        deps = set()
        for k in r:
            if k in self.last_w:
                deps.add(self.last_w[k])
        for k in w:
            if k in self.last_w:
                deps.add(self.last_w[k])
            deps.update(self.readers.get(k, ()))
        i = len(self.ops)
        op.deps = deps
        self.ops.append(op)
        for k in r:
            self.readers.setdefault(k, []).append(i)
        for k in w:
            self.last_w[k] = i
            self.readers[k] = []
        return i

    def dma(self, out, in_, r=(), w=(), stream=None, eng="sp", **kw):
        assert stream is not None
        return self.add(eng, lambda e: e.dma_start(out=out, in_=in_, **kw), r=r, w=w, dma=True, stream=stream)

    def emit(self, block):
        ops = self.ops
        for op in ops:
            for d in op.deps:
                dep = ops[d]
                if dep.is_dma or dep.eng != op.eng or op.eng != "pe":
                    dep.signal = True
        last = {}
        for op in ops:
            if not op.is_dma and op.fn is not None:
                last[op.eng] = op
        for op in last.values():
            op.signal = True
        for op in ops:
            if op.signal and op.fn is not None:
                name = ("D_" + op.stream) if op.is_dma else ("E_" + op.eng)
                self._sem(name)
                self.count[name] += 16 if op.is_dma else 1
                op.ticket = self.count[name]
                op.semname = name
        reg = {"pe": block.tensor, "act": block.scalar, "dve": block.vector, "pool": block.gpsimd, "sp": block.sync}
        for eng in self.ENGS:
            eops = [op for op in ops if op.eng == eng]
            if not eops:
                continue

            def body(e, eops=eops, eng=eng):
                waited = self.waited[eng]
                first = True
                for op in eops:
                    waits = {}
                    if first:
                        waits.update(self.barrier)
                        first = False
                    for d in op.deps:
                        dep = ops[d]
                        if dep.is_dma or dep.eng != eng or eng != "pe":
                            if dep.semname is not None:
                                waits[dep.semname] = max(waits.get(dep.semname, 0), dep.ticket)
                    for s, v in waits.items():
                        if v > 0 and waited.get(s, 0) < v:
                            e.wait_ge(self.sem[s], v)
                            waited[s] = v
                    if op.fn is not None:
                        ins = op.fn(e)
                        if op.signal:
                            ins.then_inc(self.sem[op.semname], 16 if op.is_dma else 1)

            reg[eng](body)
        self.barrier = dict(self.count)
        self.ops = []
        self.last_w = {}
        self.readers = {}


def bc(ap, shape):
    return ap.to_broadcast(shape)


def build_program(stop_after=None, debug=False):
    nc = bass.Bass("TRN2", target_bir_lowering=False)
    dt_in = {}

    def din(name, shape, dt=F32):
        dt_in[name] = nc.dram_tensor(name, list(shape), dt, kind="ExternalInput").ap()
        return dt_in[name]

    x = din("x", [L, D]); ctx = din("ctx", [C, D]); cvec = din("cvec", [2, D])
    w_mod = din("w_mod", [2, D, 3 * D]); b_mod = din("b_mod", [2, 3 * D]); norm_g = din("norm_g", [2, D])
    ssm_w_in = din("ssm_w_in", [D, 2 * D])
    a_re = din("ssm_a_re", [2, 64, 64]); a_im = din("ssm_a_im", [2, 64, 64]); log_dt = din("ssm_log_dt", [2, 64])
    b_re = din("ssm_b_re", [2, 64, 64, 16]); b_im = din("ssm_b_im", [2, 64, 64, 16])
    c_re = din("ssm_c_re", [2, 64, 16, 64]); c_im = din("ssm_c_im", [2, 64, 16, 64])
    ssm_d = din("ssm_d", [D]); w_glu = din("ssm_w_glu", [D, D]); b_glu = din("ssm_b_glu", [D])
    ssm_w_out = din("ssm_w_out", [D, D])
    attn_w_in = din("attn_w_in", [D, 2560]); q_norm = din("attn_q_norm", [64]); k_norm = din("attn_k_norm", [64])
    attn_w_out = din("attn_w_out", [D, D]); fin_g = din("final_norm_g", [D])
    ident_in = din("ident", [128, 128]); maskf_in = din("maskf", [128, 128]); maskb_in = din("maskb", [128, 128])
    kio_in = din("kio", [128, NK]); jv_in = din("jvals", [128, 16])
    posr_in = din("posr", [128, 8]); posc_in = din("posc", [128, 8]); fidx_in = din("fidx", [128, 16])

    out = nc.dram_tensor("out", [L, D], F32, kind="ExternalOutput").ap()
    dbg = {}

    def dout(name, shape, dt=F32):
        dbg[name] = nc.dram_tensor(name, list(shape), dt, kind="ExternalOutput" if debug else "Internal").ap()
        return dbg[name]

    VEC = dout("VEC", [12, D])
    DSCR = dout("DSCR", [64, 128, NK], BF16)
    SZ = dout("SZ", [C + L, D], BF16)
    YSC = dout("YSC", [C + L, D], BF16)
    X1 = dout("X1", [L, D])
    CTX1 = dout("CTX1", [C, D])

    blocks = [(NKC, ctx, 0, 0, True, 0)]
    for b in range(8):
        blocks.append((128, x[b * 1024:(b + 1) * 1024, :], NKC + 128 * b, C + 1024 * b, False, b))

    with ExitStack() as ges:
        T = Tracker(nc, ges)
        identf = ges.enter_context(nc.sbuf_tensor("identf", [128, 128], F32))
        identb = ges.enter_context(nc.sbuf_tensor("identb", [128, 128], BF16))

        with ExitStack() as es:
            sb = lambda n, s, d=F32: es.enter_context(nc.sbuf_tensor(n, s, d))
            scraw = sb("scraw", [128, 2, 8]); sc = sb("sc", [128, 8, 2])
            wst = sb("wst", [128, 8, 3 * D])
            bcol = sb("bcol", [128, 2, 24]); gcol = sb("gcol", [128, 2, 8])
            modc = sb("modc", [128, 2, 24, 2]); gs = sb("gs", [128, 2, 8, 2])
            psm = es.enter_context(nc.psum_tensor("psm", [128, 512], F32))
            block = es.enter_context(nc.Block())
            T.dma(identf[:], ident_in[:, :], w=["identf"], stream="c0")
            T.add("dve", lambda e: e.tensor_copy(out=identb[:], in_=identf[:]), r=["identf"], w=["identb"])
            T.dma(scraw[:], cvec.rearrange("n (kt p) -> p n kt", p=128), w=["scraw"], stream="c1",
                  allow_slow_non_contiguous=True)
            T.dma(bcol[:], b_mod.rearrange("i (j p) -> p i j", p=128), w=["bcol"], stream="c2",
                  allow_slow_non_contiguous=True)
            T.dma(gcol[:], norm_g.rearrange("i (j p) -> p i j", p=128), w=["gcol"], stream="c3",
                  allow_slow_non_contiguous=True)
            T.add("act", lambda e: e.activation(out=sc[:].rearrange("p kt n -> p n kt"), in_=scraw[:], func=AF.Silu),
                  r=["scraw"], w=["sc"])
            for i in range(2):
                psv = psm[:, i * 48:(i + 1) * 48].rearrange("p (j n) -> p j n", n=2)
                for kt in range(8):
                    T.dma(wst[:, kt, :], w_mod[i, kt * 128:(kt + 1) * 128, :], w=[("wst", kt)], stream="wst%d" % kt)
                for j in range(24):
                    for kt in range(8):
                        T.add("pe", lambda e, j=j, kt=kt, psv=psv: e.matmul(
                            psv[:, j, :], lhsT=wst[:, kt, j * 128:(j + 1) * 128], rhs=sc[:, kt, :],
                            start=(kt == 0), stop=(kt == 7)),
                            r=[("wst", kt), "sc"], w=[("psm", i)])
                T.add("dve", lambda e, i=i, psv=psv: e.tensor_tensor(
                    out=modc[:, i], in0=psv, in1=bc(bcol[:, i, :].unsqueeze(2), [128, 24, 2]), op=ALU.add),
                    r=[("psm", i), "bcol"], w=[("modc", i)])
                T.add("dve", lambda e, i=i: e.scalar_tensor_tensor(
                    out=gs[:, i], in0=modc[:, i, 8:16, :], scalar=1.0,
                    in1=bc(gcol[:, i, :].unsqueeze(2), [128, 8, 2]), op0=ALU.add, op1=ALU.mult),
                    r=[("modc", i), "gcol"], w=[("gs", i)])
                for n in range(2):
                    for which, src in ((0, gs[:, i, :, n]), (1, modc[:, i, 0:8, n]), (2, modc[:, i, 16:24, n])):
                        v = i * 6 + n * 3 + which
                        T.dma(VEC[v, :].rearrange("(ft p) -> p ft", p=128), src, r=[("gs", i), ("modc", i)],
                              w=["VEC"], stream="vec", allow_slow_non_contiguous=True)
            T.emit(block)
        if stop_after == 0:
            return nc, finish(nc, T, out)

        phase_front(nc, T, blocks, VEC, 0, ssm_w_in, 2 * D, identb, DSCR, SZ, mode="ssm")
        if stop_after == 1:
            return nc, finish(nc, T, out)
        phase_s5(nc, T, dict(a_re=a_re, a_im=a_im, log_dt=log_dt, b_re=b_re, b_im=b_im, c_re=c_re, c_im=c_im,
                             ssm_d=ssm_d, kio=kio_in, jv=jv_in, maskf=maskf_in, maskb=maskb_in),
                 identf, DSCR, YSC)
        if stop_after == 2:
            return nc, finish(nc, T, out)
        phase_post0(nc, T, blocks, VEC, YSC, SZ, w_glu, b_glu, ssm_w_out, identb, X1, CTX1)
        if stop_after == 3:
            return nc, finish(nc, T, out)
        phase_attn(nc, T, VEC, X1, CTX1, attn_w_in, q_norm, k_norm, attn_w_out, fin_g, identf, identb,
                   posr_in, posc_in, fidx_in, out, stop_after)
    return nc, None


def finish(nc, T, out):
    with ExitStack() as es:
        z = es.enter_context(nc.sbuf_tensor("zfin", [128, D], F32))
        block = es.enter_context(nc.Block())
        T.add("dve", lambda e: e.memset(z[:], 0.0), w=["z"])
        T.dma(out[0:128, :], z[:], r=["z"], w=["out"], stream="fin")
        T.add("sp", None, r=["out"])
        T.emit(block)
    return None


def load_cast_weight(nc, T, es, name, w_ap, ncols, stage):
    wt = es.enter_context(nc.sbuf_tensor(name, [128, 8, ncols], BF16))
    for ft in range(8):
        st = stage[ft % 2]
        T.dma(st[:, 0:ncols], w_ap[ft * 128:(ft + 1) * 128, :], w=[("stage", ft % 2)], stream="stage%d" % (ft % 2))
        T.add("pool", lambda e, st=st, ft=ft: e.tensor_copy(out=wt[:, ft, :], in_=st[:, 0:ncols]),
              r=[("stage", ft % 2)], w=[name])
    return wt


def phase_front(nc, T, blocks, VEC, layer, w_in_ap, ncols, identb, DSCR, SZ, mode):
    with ExitStack() as es:
        sb = lambda n, s, d=F32: es.enter_context(nc.sbuf_tensor(n, s, d))
        stage = [sb("stg0", [128, 2560]), sb("stg1", [128, 2560])]
        block = es.enter_context(nc.Block())
        rep = {}
        for n in range(2):
            for which, nm in ((0, "gs"), (1, "sh")):
                t = sb("rep_%s%d" % (nm, n), [128, D])
                v = layer * 6 + n * 3 + which
                T.dma(t[:], VEC[v, :].partition_broadcast(128), w=[("rep", nm, n)], stream="rep%s%d" % (nm, n))
                rep[(nm, n)] = t
        wt = load_cast_weight(nc, T, es, "w_in_bf", w_in_ap, ncols, stage)
        xh = [sb("xh0", [128, 4, D]), sb("xh1", [128, 4, D])]
        junk = sb("junk", [128, D], BF16)
        tmp32 = [sb("tmp32a", [128, D]), sb("tmp32b", [128, D])]
        ss = sb("ss", [128, 8]); ms = sb("ms", [128, 8]); rstd = sb("rstd", [128, 8])
        h = sb("h", [128, 8, D], BF16)
        hT = sb("hT", [128, 8, 8, 128], BF16)
        ucat = sb("ucat", [128, 64, 8, 16], BF16)
        sz = sb("sz", [128, 8, D], BF16)
        dst = [sb("dst0", [128, 8, 128], BF16), sb("dst1", [128, 8, 128], BF16)]
        psT = [es.enter_context(nc.psum_tensor("psT%d" % i, [128, 8, 128], BF16)) for i in range(2)]
        psU = [es.enter_context(nc.psum_tensor("psU%d" % i, [128, 512], F32)) for i in range(4)]
        psD = [es.enter_context(nc.psum_tensor("psD%d" % i, [128, 8, 128], BF16)) for i in range(2)]
        evac_i = [0]
        ucnt = [0]
        for (nk, xap, kg0, row0, is_ctx, bidx) in blocks:
            n = 1 if is_ctx else 0
            xv = xap.rearrange("(k s) f -> k s f", s=8)
            T.add("dve", lambda e: e.memset(ss[:], 0.0), w=["ss"])
            for half in range(2):
                T.dma(xh[half][:nk], xv[:, half * 4:(half + 1) * 4, :], w=[("xh", half)], stream="xh%d" % half)
                for s4 in range(4):
                    s = half * 4 + s4
                    T.add("act", lambda e, half=half, s4=s4, s=s, nk=nk: e.activation(
                        out=junk[:nk], in_=xh[half][:nk, s4, :], func=AF.Square, accum_out=ss[:nk, s:s + 1]),
                        r=[("xh", half)], w=["junk", "ss"])
            T.add("dve", lambda e, nk=nk: e.tensor_scalar(out=ms[:nk], in0=ss[:nk], scalar1=1.0 / D, scalar2=EPS,
                                                          op0=ALU.mult, op1=ALU.add), r=["ss"], w=["ms"])
            T.add("act", lambda e, nk=nk: e.sqrt(out=ms[:nk], in_=ms[:nk]), r=["ms"], w=["ms"])
            T.add("dve", lambda e, nk=nk: e.reciprocal(out=rstd[:nk], in_=ms[:nk]), r=["ms"], w=["rstd"])
            for s in range(8):
                half, s4 = divmod(s, 4)
                tm = tmp32[s % 2]
                T.add("dve", lambda e, half=half, s4=s4, s=s, nk=nk, tm=tm, n=n: e.scalar_tensor_tensor(
                    out=tm[:nk], in0=xh[half][:nk, s4, :], scalar=rstd[:nk, s:s + 1], in1=rep[("gs", n)][:nk],
                    op0=ALU.mult, op1=ALU.mult), r=[("xh", half), "rstd", ("rep", "gs", n)], w=[("tmp32", s % 2)])
                T.add("pool", lambda e, s=s, nk=nk, tm=tm, n=n: e.tensor_tensor(
                    out=h[:nk, s, :], in0=tm[:nk], in1=rep[("sh", n)][:nk], op=ALU.add),
                    r=[("tmp32", s % 2), ("rep", "sh", n)], w=[("h", s)])
            for ft in range(8):
                pt = psT[ft % 2]
                for s in range(8):
                    T.add("pe", lambda e, pt=pt, s=s, ft=ft, nk=nk: e.transpose(
                        out=pt[:, s, :nk], in_=h[:nk, s, ft * 128:(ft + 1) * 128], identity=identb[:nk, :nk]),
                        r=[("h", s), "identb"], w=[("psT", ft % 2)])
                eng = "act" if (evac_i[0] % 2 == 0) else "dve"
                evac_i[0] += 1
                if eng == "act":
                    T.add("act", lambda e, pt=pt, ft=ft, nk=nk: e.copy(out=hT[:, ft, :, :nk], in_=pt[:, :, :nk]),
                          r=[("psT", ft % 2)], w=[("hT", ft)])
                else:
                    T.add("dve", lambda e, pt=pt, ft=ft, nk=nk: e.tensor_copy(out=hT[:, ft, :, :nk], in_=pt[:, :, :nk]),
                          r=[("psT", ft % 2)], w=[("hT", ft)])
            for s in range(8):
                for nb in range(4):
                    pu = psU[ucnt[0] % 4]
                    pkey = ("psU", ucnt[0] % 4)
                    ucnt[0] += 1
                    for ft in range(8):
                        T.add("pe", lambda e, pu=pu, s=s, nb=nb, ft=ft, nk=nk: e.matmul(
                            pu[:nk, :], lhsT=hT[:, ft, s, :nk], rhs=wt[:, ft, nb * 512:(nb + 1) * 512],
                            start=(ft == 0), stop=(ft == 7)),
                            r=[("hT", ft), "w_in_bf"], w=[pkey])
                    if nb < 2:
                        T.add("dve", lambda e, pu=pu, s=s, nb=nb, nk=nk: e.tensor_copy(
                            out=ucat[:nk, nb * 32:(nb + 1) * 32, s, :],
                            in_=pu[:nk, :].rearrange("k (g h) -> k g h", h=16)),
                            r=[pkey], w=["ucat"])
                    else:
                        T.add("act", lambda e, pu=pu, s=s, nb=nb, nk=nk: e.activation(
                            out=sz[:nk, s, (nb - 2) * 512:(nb - 1) * 512], in_=pu[:nk, :], func=AF.Silu),
                            r=[pkey], w=["sz"])
            T.dma(SZ[row0:row0 + nk * 8, :].rearrange("(k s) f -> k s f", s=8), sz[:nk], r=["sz"], w=["SZ"],
                  stream="szst", eng="pool")
            for gq in range(8):
                pd = psD[gq % 2]
                for gi in range(8):
                    g = gq * 8 + gi
                    T.add("pe", lambda e, pd=pd, gi=gi, g=g, nk=nk: e.transpose(
                        out=pd[:, gi, :nk], in_=ucat[:nk, g, :, :].rearrange("k s h -> k (s h)"),
                        identity=identb[:nk, :nk]),
                        r=["ucat", "identb"], w=[("psD", gq % 2)])
                T.add("dve", lambda e, pd=pd, gq=gq, nk=nk: e.tensor_copy(out=dst[gq % 2][:, :, :nk], in_=pd[:, :, :nk]),
                      r=[("psD", gq % 2)], w=[("dst", gq % 2)])
                T.dma(DSCR[gq * 8:(gq + 1) * 8, :, kg0:kg0 + nk].rearrange("g p k -> p g k"), dst[gq % 2][:, :, :nk],
                      r=[("dst", gq % 2)], w=["DSCR"], stream="dscr%d" % (gq % 2), eng="pool")
        T.emit(block)


def phase_s5(nc, T, P, identf, DSCR, YSC):
    PI2 = TWO_PI
    with ExitStack() as es:
        sb = lambda n, s, d=F32: es.enter_context(nc.sbuf_tensor("s5_" + n, s, d))
        es1 = ExitStack()
        sbt = lambda n, s, d=F32: es1.enter_context(nc.sbuf_tensor("s5t_" + n, s, d))
        kio = sb("kio", [128, NK]); jv = sb("jv", [128, 16])
        maskf = sb("maskf", [128, 128]); maskb = sb("maskb", [128, 128])
        phi = sb("phi", [128, 128]); phi2pi = sb("phi2pi", [128, 128]); rho = sb("rho", [128, 128])
        ctr = sb("ctr", [128, 128, 16]); cti = sb("cti", [128, 128, 16])
        ejr = sb("ejr", [128, 16, 128]); eji = sb("eji", [128, 16, 128])
        bbr = sb("bbr", [128, 128, 16]); bbi = sb("bbi", [128, 128, 16])
        dcol = sb("dcol", [128, 64])
        sm5 = sb("sm5", [128, 128])
        an_r = sbt("an_r", [64, 2, 2, 64]); an_i = sbt("an_i", [64, 2, 2, 64])
        are = sbt("are", [128, 128]); aim = sbt("aim", [128, 128]); ldt = sbt("ldt", [128, 128])
        dtt = sbt("dtt", [128, 128]); alpha = sbt("alpha", [128, 128]); theta = sbt("theta", [128, 128])
        btr = sbt("btr", [128, 128, 16]); bti = sbt("bti", [128, 128, 16])
        cn_r = sbt("cn_r", [128, 8, 2, 2, 64]); cn_i = sbt("cn_i", [128, 8, 2, 2, 64])
        tmpA = sbt("tmpA", [128, 16, 128]); tmpB = sbt("tmpB", [128, 16, 128]); tmpC = sbt("tmpC", [128, 16, 128])
        tni = sbt("tni", [128, 16, 128], I32)
        sm = [sbt("sm%d" % i, [128, 128]) for i in range(5)] + [sm5]
        pss = es.enter_context(nc.psum_tensor("pss", [128, 4, 128], F32))
        psZA = es.enter_context(nc.psum_tensor("psZA", [128, 2, 512], F32))
        psZB = es.enter_context(nc.psum_tensor("psZB", [128, 2, 512], F32))
        psZc = es.enter_context(nc.psum_tensor("psZc", [128, 512], F32))
        psY = [es.enter_context(nc.psum_tensor("psY%d" % i, [128, 4, 128], F32)) for i in range(2)]
        block = es1.enter_context(nc.Block())
        A = T.add
        A("pool", lambda e: e.memset(sm[5][:], float(np.pi / 2)), w=["sm5"])
        T.dma(kio[:], P["kio"][:, :], w=["kio"], stream="p0")
        T.dma(jv[:], P["jv"][:, :], w=["jv"], stream="p1")
        T.dma(maskf[:], P["maskf"][:, :], w=["maskf"], stream="p2")
        T.dma(maskb[:], P["maskb"][:, :], w=["maskb"], stream="p3")
        for c2 in range(2):
            T.dma(an_r[:, :, c2, :], P["a_re"].rearrange("d g p -> g d p"), w=["an_r"], stream="p4")
            T.dma(an_i[:, :, c2, :], P["a_im"].rearrange("d g p -> g d p"), w=["an_i"], stream="p5")
        T.dma(ldt[:], P["log_dt"].rearrange("d g -> (d g)").partition_broadcast(128), w=["ldt"], stream="p6")
        for c2 in range(2):
            for d in range(2):
                for gq in range(4):
                    T.dma(btr[c2 * 64:(c2 + 1) * 64, d * 64 + gq * 16:d * 64 + gq * 16 + 16, :],
                          P["b_re"][d, gq * 16:(gq + 1) * 16].rearrange("g p h -> p g h"), w=["btr"], stream="p7")
                    T.dma(bti[c2 * 64:(c2 + 1) * 64, d * 64 + gq * 16:d * 64 + gq * 16 + 16, :],
                          P["b_im"][d, gq * 16:(gq + 1) * 16].rearrange("g p h -> p g h"), w=["bti"], stream="p8")
            for d in range(2):
                T.dma(cn_r[:, :, d, c2, :], P["c_re"][d].rearrange("(gb g8) h p -> (g8 h) gb p", g8=8),
                      w=["cn_r"], stream="p9")
                T.dma(cn_i[:, :, d, c2, :], P["c_im"][d].rearrange("(gb g8) h p -> (g8 h) gb p", g8=8),
                      w=["cn_i"], stream="p10")
        for s8 in range(8):
            T.dma(dcol[s8 * 16:(s8 + 1) * 16, :], P["ssm_d"].rearrange("(g h) -> h g", h=16), w=["dcol"],
                  stream="p11", allow_slow_non_contiguous=True)
        for (src, dstt, nm) in ((an_r, are, "are"), (an_i, aim, "aim")):
            for d in range(2):
                A("pe", lambda e, src=src, d=d: e.transpose(
                    out=pss[:, d, 0:64], in_=src[:, d, :, :].rearrange("g c p -> g (c p)"), identity=identf[0:64, 0:64]),
                    r=[nm.replace("a", "an_", 1) if False else ("an_r" if nm == "are" else "an_i"), "identf"], w=["pss"])
            A("dve", lambda e, dstt=dstt: e.tensor_copy(out=dstt[:].rearrange("p (d g) -> p d g", d=2), in_=pss[:, 0:2, 0:64]),
              r=["pss"], w=[nm])
        for (src, dstt, nm, snm) in ((cn_r, ctr, "ctr", "cn_r"), (cn_i, cti, "cti", "cn_i")):
            for d in range(2):
                for gq in range(2):
                    for g4 in range(4):
                        gb = gq * 4 + g4
                        A("pe", lambda e, src=src, d=d, gb=gb, g4=g4: e.transpose(
                            out=pss[:, g4, :], in_=src[:, gb, d, :, :].rearrange("q c p -> q (c p)"), identity=identf[:]),
                            r=[snm, "identf"], w=["pss"])
                    A("dve", lambda e, dstt=dstt, d=d, gq=gq: e.tensor_copy(
                        out=dstt[:, d * 64 + gq * 32:d * 64 + gq * 32 + 32, :].rearrange("p (a b) h -> p a b h", a=4),
                        in_=pss[:].rearrange("p a (b h) -> p a b h", h=16)),
                        r=["pss"], w=[nm])
        A("act", lambda e: e.activation(out=dtt[:], in_=ldt[:], func=AF.Exp), r=["ldt"], w=["dtt"])
        A("dve", lambda e: e.tensor_tensor(out=alpha[:], in0=are[:], in1=dtt[:], op=ALU.mult), r=["are", "dtt"], w=["alpha"])
        A("dve", lambda e: e.tensor_tensor(out=theta[:], in0=aim[:], in1=dtt[:], op=ALU.mult), r=["aim", "dtt"], w=["theta"])
        A("dve", lambda e: e.tensor_scalar(out=phi[:], in0=theta[:], scalar1=8.0, scalar2=0.0, op0=ALU.mult, op1=ALU.add),
          r=["theta"], w=["phi"])
        A("dve", lambda e: e.tensor_scalar(out=phi2pi[:], in0=theta[:], scalar1=8.0 / PI2, scalar2=0.0, op0=ALU.mult,
                                           op1=ALU.add), r=["theta"], w=["phi2pi"])
        A("act", lambda e: e.activation(out=rho[:], in_=alpha[:], func=AF.Exp, scale=8.0), r=["alpha"], w=["rho"])
        th_b = bc(theta[:].unsqueeze(1), [128, 16, 128]); al_b = bc(alpha[:].unsqueeze(1), [128, 16, 128])
        jv_b = bc(jv[:].unsqueeze(2), [128, 16, 128])
        A("dve", lambda e: e.tensor_tensor(out=tmpA[:], in0=th_b, in1=jv_b, op=ALU.mult), r=["theta", "jv"], w=["tmpA"])
        A("dve", lambda e: e.tensor_scalar(out=tni[:], in0=tmpA[:], scalar1=1.0 / PI2, scalar2=0.0, op0=ALU.mult,
                                           op1=ALU.add), r=["tmpA"], w=["tni"])
        A("dve", lambda e: e.scalar_tensor_tensor(out=tmpB[:], in0=tni[:], scalar=-PI2, in1=tmpA[:], op0=ALU.mult,
                                                  op1=ALU.add), r=["tni", "tmpA"], w=["tmpB"])
        A("dve", lambda e: e.tensor_scalar(out=tmpB[:], in0=tmpB[:], scalar1=-PI_SAFE, scalar2=PI_SAFE, op0=ALU.max,
                                           op1=ALU.min), r=["tmpB"], w=["tmpB"])
        A("dve", lambda e: e.scalar_tensor_tensor(out=tmpA[:], in0=tmpB[:], scalar=-1.0, in1=tmpB[:], op0=ALU.mult, op1=ALU.max),
          r=["tmpB"], w=["tmpA"])
        A("act", lambda e: e.activation(out=eji[:], in_=tmpB[:], func=AF.Sin), r=["tmpB"], w=["eji"])
        A("act", lambda e: e.activation(out=ejr[:], in_=tmpA[:], func=AF.Sin, scale=-1.0, bias=sm[5][:, 0:1]),
          r=["tmpA", "sm5"], w=["ejr"])
        A("dve", lambda e: e.tensor_tensor(out=tmpC[:], in0=al_b, in1=jv_b, op=ALU.mult), r=["alpha", "jv"], w=["tmpC"])
        A("act", lambda e: e.activation(out=tmpC[:], in_=tmpC[:], func=AF.Exp), r=["tmpC"], w=["tmpC"])
        A("dve", lambda e: e.tensor_tensor(out=ejr[:], in0=ejr[:], in1=tmpC[:], op=ALU.mult), r=["ejr", "tmpC"], w=["ejr"])
        A("dve", lambda e: e.tensor_tensor(out=eji[:], in0=eji[:], in1=tmpC[:], op=ALU.mult), r=["eji", "tmpC"], w=["eji"])
        e1r, e1i = ejr[:, 8, :], eji[:, 8, :]
        A("dve", lambda e: e.tensor_scalar(out=sm[0][:], in0=e1r, scalar1=-1.0, scalar2=0.0, op0=ALU.add, op1=ALU.add),
          r=["ejr"], w=["sm0"])
        A("dve", lambda e: e.tensor_tensor(out=sm[1][:], in0=sm[0][:], in1=are[:], op=ALU.mult), r=["sm0", "are"], w=["sm1"])
        A("dve", lambda e: e.tensor_tensor(out=sm[2][:], in0=e1i, in1=aim[:], op=ALU.mult), r=["eji", "aim"], w=["sm2"])
        A("dve", lambda e: e.tensor_tensor(out=sm[1][:], in0=sm[1][:], in1=sm[2][:], op=ALU.add), r=["sm1", "sm2"], w=["sm1"])
        A("dve", lambda e: e.tensor_tensor(out=sm[2][:], in0=e1i, in1=are[:], op=ALU.mult), r=["eji", "are", "sm1"], w=["sm2"])
        A("dve", lambda e: e.tensor_tensor(out=sm[3][:], in0=sm[0][:], in1=aim[:], op=ALU.mult), r=["sm0", "aim"], w=["sm3"])
        A("dve", lambda e: e.tensor_tensor(out=sm[2][:], in0=sm[2][:], in1=sm[3][:], op=ALU.subtract), r=["sm2", "sm3"], w=["sm2"])
        A("dve", lambda e: e.tensor_tensor(out=sm[3][:], in0=are[:], in1=are[:], op=ALU.mult), r=["are", "sm2"], w=["sm3"])
        A("dve", lambda e: e.tensor_tensor(out=sm[4][:], in0=aim[:], in1=aim[:], op=ALU.mult), r=["aim"], w=["sm4"])
        A("dve", lambda e: e.tensor_tensor(out=sm[3][:], in0=sm[3][:], in1=sm[4][:], op=ALU.add), r=["sm3", "sm4"], w=["sm3"])
        A("dve", lambda e: e.reciprocal(out=sm[3][:], in_=sm[3][:]), r=["sm3"], w=["sm3"])
        A("dve", lambda e: e.tensor_tensor(out=sm[1][:], in0=sm[1][:], in1=sm[3][:], op=ALU.mult), r=["sm1", "sm3"], w=["sm1"])
        A("dve", lambda e: e.tensor_tensor(out=sm[2][:], in0=sm[2][:], in1=sm[3][:], op=ALU.mult), r=["sm2", "sm3"], w=["sm2"])
        br_b = bc(sm[1][:].unsqueeze(2), [128, 128, 16]); bi_b = bc(sm[2][:].unsqueeze(2), [128, 128, 16])
        tA3 = tmpA[:].rearrange("p a b -> p b a"); tB3 = tmpB[:].rearrange("p a b -> p b a")
        A("dve", lambda e: e.tensor_tensor(out=tA3, in0=br_b, in1=btr[:], op=ALU.mult), r=["sm1", "btr", "ejr", "eji"], w=["tmpA"])
        A("dve", lambda e: e.tensor_tensor(out=tB3, in0=bi_b, in1=bti[:], op=ALU.mult), r=["sm2", "bti", "ejr", "eji"], w=["tmpB"])
        A("dve", lambda e: e.tensor_tensor(out=bbr[:], in0=tA3, in1=tB3, op=ALU.subtract), r=["tmpA", "tmpB"], w=["bbr"])
        A("dve", lambda e: e.tensor_tensor(out=tA3, in0=br_b, in1=bti[:], op=ALU.mult), r=["sm1", "bti", "bbr"], w=["tmpA"])
        A("dve", lambda e: e.tensor_tensor(out=tB3, in0=bi_b, in1=btr[:], op=ALU.mult), r=["sm2", "btr", "bbr"], w=["tmpB"])
        A("dve", lambda e: e.tensor_tensor(out=bbi[:], in0=tA3, in1=tB3, op=ALU.add), r=["tmpA", "tmpB"], w=["bbi"])
        T.emit(block)
        es1.close()
        if STOP_S5_SETUP[0]:
            return
        block = es.enter_context(nc.Block())
        s5_groups(nc, T, es, sb, locals())
        T.emit(block)


def s5_groups(nc, T, es, sb, V):
    A = T.add
    kio, identf, maskf, maskb, dcol = V["kio"], V["identf"], V["maskf"], V["maskb"], V["dcol"]
    ejr, eji, bbr, bbi, ctr, cti = V["ejr"], V["eji"], V["bbr"], V["bbi"], V["ctr"], V["cti"]
    phi, phi2pi, rho = V["phi"], V["phi2pi"], V["rho"]
    pss, psZA, psZB, psZc, psY = V["pss"], V["psZA"], V["psZB"], V["psZc"], V["psY"]
    DSCR, YSC, sm = V["DSCR"], V["YSC"], V["sm"]
    PI2 = TWO_PI
    ga = sb("ga", [128, 2, 8, 128]); hm = sb("hm", [128, 2, 8, 128])
    t1 = sb("gt1", [128, 8, 128]); t2 = sb("gt2", [128, 8, 128])
    winjA = sb("winjA", [128, 2, 8, 128], BF16); winjB = sb("winjB", [128, 2, 8, 128], BF16)
    wA = sb("wA", [128, 2, 8, 128], BF16); wB = sb("wB", [128, 2, 8, 128], BF16)
    toep = sb("toep", [128, 8, 128], BF16)
    tt1 = sb("tt1", [128, 4, 128]); tt2 = sb("tt2", [128, 4, 128])
    dg = [sb("dg0", [128, NK], BF16), sb("dg1", [128, NK], BF16)]
    ni = [sb("ni0", [128, NK], I32)] * 2; ang = [sb("ang0", [128, NK])] * 2
    rr = [sb("rr0", [128, NK])] * 2; ra = [sb("ra0", [128, NK])] * 2
    ts4 = [sb("ts%d" % i, [128, NK]) for i in range(4)]; tc4 = [sb("tc%d" % i, [128, NK]) for i in range(4)]
    m1 = [sb("m1%d" % d, [128, NK]) for d in range(2)]; m2 = [sb("m20", [128, NK])] * 2
    q = [sb("q%d" % d, [128, NK]) for d in range(2)]
    FC = [sb("FC%d" % d, [128, NK + 1], BF16) for d in range(2)]
    FS = [sb("FS%d" % d, [128, NK + 1], BF16) for d in range(2)]
    ytile = [sb("ytile0", [128, 9, 8, 128], BF16)] * 2
    halfpi = sm[5]
    for d in range(2):
        A("pool", lambda e, d=d: e.memset(FC[d][:], 0.0), w=[("FC", d)])
        A("pool", lambda e, d=d: e.memset(FS[d][:], 0.0), w=[("FS", d)])

    def eview(t, lo, hi, rev, d, gb, parts):
        sl = t[parts, lo:hi, d * 64 + gb * 8:d * 64 + gb * 8 + 8]
        if rev:
            sl = t[parts, hi - 1:(lo - 1 if lo > 0 else None):-1, d * 64 + gb * 8:d * 64 + gb * 8 + 8]
        return bc(sl.rearrange("p s g -> p g s").unsqueeze(3), [64, 8, 8, 16])

    def gview(t, d, gb, parts):
        return bc(t[parts, d * 64 + gb * 8:d * 64 + gb * 8 + 8, :].unsqueeze(2), [64, 8, 8, 16])

    def cplx(eng, outv, parts, Er, Ei, Xr, Xi, kind, rk, wk):
        a = t1[parts].rearrange("p g (s h) -> p g s h", h=16); b = t2[parts].rearrange("p g (s h) -> p g s h", h=16)
        pk = "lo" if parts.start == 0 else "hi"
        X1, X2 = (Xr, Xi) if kind in ("re", "nre") else (Xi, Xr)
        A(eng, lambda e: e.tensor_tensor(out=a, in0=Er, in1=X1, op=ALU.mult), r=rk, w=[("t1", pk)])
        A(eng, lambda e: e.tensor_tensor(out=b, in0=Ei, in1=X2, op=ALU.mult), r=rk, w=[("t2", pk)])
        if kind == "re":
            A(eng, lambda e: e.tensor_tensor(out=outv, in0=a, in1=b, op=ALU.subtract), r=[("t1", pk), ("t2", pk)], w=wk)
        elif kind == "im":
            A(eng, lambda e: e.tensor_tensor(out=outv, in0=a, in1=b, op=ALU.add), r=[("t1", pk), ("t2", pk)], w=wk)
        elif kind == "nre":
            A(eng, lambda e: e.tensor_tensor(out=outv, in0=b, in1=a, op=ALU.subtract), r=[("t1", pk), ("t2", pk)], w=wk)
        else:
            A(eng, lambda e: e.tensor_tensor(out=a, in0=a, in1=b, op=ALU.add), r=[("t1", pk), ("t2", pk)], w=[("t1", pk)])
            A(eng, lambda e: e.tensor_scalar(out=outv, in0=a, scalar1=-1.0, scalar2=0.0, op0=ALU.mult, op1=ALU.add),
              r=[("t1", pk)], w=wk)

    lo, hi = slice(0, 64), slice(64, 128)
    pkeys = ["ejr", "eji", "bbr", "bbi", "ctr", "cti"]
    yt = ytile[0]
    ykey = ("ytile", 0)

    def gen_weights(gb):
        for d in range(2):
            args = (7, 15, d == 0, d, gb)
            for parts, kind, eng in ((lo, "re", "dve"), (hi, "im", "pool")):
                ov = ga[parts, d].rearrange("p g (s h) -> p g s h", h=16)
                cplx(eng, ov, parts, eview(ejr, *args, parts), eview(eji, *args, parts),
                     gview(bbr, d, gb, parts), gview(bbi, d, gb, parts), kind, pkeys, [("ga", d)])
            args = (0, 8, d == 1, d, gb)
            for parts, kind, eng in ((lo, "re", "dve"), (hi, "nim", "pool")):
                ov = hm[parts, d].rearrange("p g (s h) -> p g s h", h=16)
                cplx(eng, ov, parts, eview(ejr, *args, parts), eview(eji, *args, parts),
                     gview(ctr, d, gb, parts), gview(cti, d, gb, parts), kind, pkeys, [("hm", d)])
            args = (8, 16, d == 1, d, gb)
            for (wt_, wn, kinds) in ((wA, "wA", ("re", "nim")), (wB, "wB", ("nim", "nre"))):
                for parts, kind, eng in ((lo, kinds[0], "dve"), (hi, kinds[1], "pool")):
                    ov = wt_[parts, d].rearrange("p g (s h) -> p g s h", h=16)
                    cplx(eng, ov, parts, eview(ejr, *args, parts), eview(eji, *args, parts),
                         gview(ctr, d, gb, parts), gview(cti, d, gb, parts), kind, pkeys + ["Ymm"], [(wn, d)])
        for d in range(2):
            for gq in range(2):
                for g4 in range(4):
                    g8 = gq * 4 + g4
                    A("pe", lambda e, d=d, g8=g8, g4=g4: e.transpose(out=pss[:, g4, :], in_=ga[:, d, g8, :], identity=identf[:]),
                      r=[("ga", d), "identf"], w=["pss"])
                A("dve", lambda e, d=d, gq=gq: e.tensor_copy(out=winjA[:, d, gq * 4:gq * 4 + 4, :], in_=pss[:]),
                  r=["pss", "Zmm"], w=["winjA"])
                A("dve", lambda e, d=d, gq=gq: e.tensor_copy(out=winjB[:, d, gq * 4:gq * 4 + 4, 0:64], in_=pss[:, :, 64:128]),
                  r=["pss", "Zmm"], w=["winjB"])
                A("dve", lambda e, d=d, gq=gq: e.tensor_scalar(out=winjB[:, d, gq * 4:gq * 4 + 4, 64:128], in0=pss[:, :, 0:64],
                                                               scalar1=-1.0, scalar2=0.0, op0=ALU.mult, op1=ALU.add),
                  r=["pss", "Zmm"], w=["winjB"])
        for gq in range(2):
            for d in range(2):
                for g4 in range(4):
                    g8 = gq * 4 + g4
                    A("pe", lambda e, d=d, g8=g8, g4=g4: e.matmul(pss[:, g4, :], lhsT=ga[:, d, g8, :], rhs=hm[:, d, g8, :],
                                                                 start=True, stop=True),
                      r=[("ga", d), ("hm", d)], w=["pss"])
                mk = maskf if d == 0 else maskb
                tt = tt1 if d == 0 else tt2
                A("dve", lambda e, mk=mk, tt=tt: e.tensor_tensor(out=tt[:], in0=pss[:], in1=bc(mk[:].unsqueeze(1), [128, 4, 128]),
                                                                op=ALU.mult), r=["pss", "maskf", "maskb"], w=["tt%d" % d])
            A("dve", lambda e: e.tensor_tensor(out=tt1[:], in0=tt1[:], in1=tt2[:], op=ALU.add), r=["tt0", "tt1"], w=["tt0"])
            for g4 in range(4):
                g8 = gq * 4 + g4
                g = gb * 8 + g8
                A("dve", lambda e, g4=g4, g8=g8, g=g: e.scalar_tensor_tensor(
                    out=toep[:, g8, :], in0=identf[:], scalar=dcol[:, g:g + 1], in1=tt1[:, g4, :], op0=ALU.mult, op1=ALU.add),
                    r=["tt0", "dcol", "identf", "Ymm"], w=["toep"])
    def tables(g, d):
        col = d * 64 + g
        ti = (g % 2) * 2 + d
        ts = {d: ts4[ti]}; tc = {d: tc4[ti]}
        A("dve", lambda e: e.tensor_scalar(out=ni[d][:], in0=kio[:], scalar1=phi2pi[:, col:col + 1], scalar2=0.0,
                                           op0=ALU.mult, op1=ALU.add), r=["kio", "phi2pi"], w=[("ni", 0)])
        A("act", lambda e: e.activation(out=ang[d][:], in_=kio[:], func=AF.Copy, scale=phi[:, col:col + 1]),
          r=["kio", "phi"], w=[("ang", 0)])
        A("dve", lambda e: e.scalar_tensor_tensor(out=rr[d][:], in0=ni[d][:], scalar=-PI2, in1=ang[d][:], op0=ALU.mult,
                                                  op1=ALU.add), r=[("ni", 0), ("ang", 0)], w=[("rr", 0)])
        A("dve", lambda e: e.tensor_scalar(out=rr[d][:], in0=rr[d][:], scalar1=-PI_SAFE, scalar2=PI_SAFE, op0=ALU.max,
                                           op1=ALU.min), r=[("rr", 0)], w=[("rr", 0)])
        A("act", lambda e: e.activation(out=ra[d][:], in_=rr[d][:], func=AF.Abs), r=[("rr", 0)], w=[("ra", 0)])
        A("act", lambda e: e.activation(out=ts[d][:], in_=rr[d][:], func=AF.Sin), r=[("rr", 0)], w=[("ts", ti)])
        A("act", lambda e: e.activation(out=tc[d][:], in_=ra[d][:], func=AF.Sin, scale=-1.0, bias=halfpi[:, 0:1]),
          r=[("ra", 0), "sm5"], w=[("tc", ti)])

    def core(g, g8, d, dgt, dkey):
        col = d * 64 + g
        ti = (g % 2) * 2 + d
        ts = {d: ts4[ti]}; tc = {d: tc4[ti]}
        for (pz, wj, zk, wk_) in ((psZA, winjA, "psZA", "winjA"), (psZB, winjB, "psZB", "winjB")):
            for hf in range(2):
                A("pe", lambda e, pz=pz, wj=wj, hf=hf: e.matmul(
                    pz[:, hf, :], lhsT=wj[:, d, g8, :], rhs=dgt[:, NKC + 512 * hf:NKC + 512 * (hf + 1)],
                    start=True, stop=True), r=[dkey, wk_], w=[zk, "Zmm"])
        A("pe", lambda e: e.matmul(psZc[:, 0:32], lhsT=winjA[:, d, g8, :], rhs=dgt[:, 0:NKC], start=True, stop=True),
          r=[dkey, "winjA"], w=["psZc", "Zmm"])
        A("pe", lambda e: e.matmul(psZc[:, 32:64], lhsT=winjB[:, d, g8, :], rhs=dgt[:, 0:NKC], start=True, stop=True),
          r=[dkey, "winjB"], w=["psZc", "Zmm"])
        zaf = psZA[:].rearrange("p a b -> p (a b)"); zbf = psZB[:].rearrange("p a b -> p (a b)")
        if d == 0:
            segs = [(slice(0, NKC), psZc[:, 0:32], psZc[:, 32:64]), (slice(NKC, NK), zaf, zbf)]
        else:
            segs = [(slice(0, NKC), psZc[:, 31::-1], psZc[:, 63:31:-1]), (slice(NKC, NK), zaf[:, ::-1], zbf[:, ::-1])]
        for (js, za, zb) in segs:
            A("dve", lambda e, js=js, za=za: e.tensor_tensor(out=m1[d][:, js], in0=za, in1=tc[d][:, js], op=ALU.mult),
              r=["psZA", "psZc", ("tc", ti)], w=[("m1", d)])
            A("dve", lambda e, js=js, zb=zb: e.tensor_tensor(out=m2[d][:, js], in0=zb, in1=ts[d][:, js], op=ALU.mult),
              r=["psZB", "psZc", ("ts", ti)], w=[("m2", 0)])
        A("dve", lambda e: e.tensor_tensor(out=m1[d][:], in0=m1[d][:], in1=m2[d][:], op=ALU.add), r=[("m1", d), ("m2", 0)],
          w=[("m1", d)])

    def core2(g, g8, d):
        col = d * 64 + g
        ti = (g % 2) * 2 + d
        ts = {d: ts4[ti]}; tc = {d: tc4[ti]}
        A("dve", lambda e: e.tensor_tensor_scan(out=q[d][:], data0=bc(rho[:, col:col + 1], [128, NK]), data1=m1[d][:],
                                                initial=0.0, op0=ALU.mult, op1=ALU.add), r=[("m1", d), "rho"], w=[("q", d)])
        for (Fb, tab, fk, tk) in ((FC[d], tc[d], ("FC", d), ("tc", ti)), (FS[d], ts[d], ("FS", d), ("ts", ti))):
            if d == 0:
                A("pool", lambda e, Fb=Fb, tab=tab: e.tensor_tensor(out=Fb[:, 1:NK + 1], in0=q[d][:], in1=tab[:], op=ALU.mult),
                  r=[("q", d), tk], w=[fk])
            else:
                A("pool", lambda e, Fb=Fb, tab=tab: e.tensor_tensor(out=Fb[:, 31:0:-1], in0=q[d][:, 0:31], in1=tab[:, 0:31],
                                                                   op=ALU.mult), r=[("q", d), tk], w=[fk])
                A("pool", lambda e, Fb=Fb, tab=tab: e.tensor_tensor(out=Fb[:, NK - 1:32:-1], in0=q[d][:, 32:NK - 1],
                                                                   in1=tab[:, 32:NK - 1], op=ALU.mult), r=[("q", d), tk], w=[fk])
                A("pool", lambda e, Fb=Fb, tab=tab: e.tensor_tensor(out=Fb[:, NK:NK + 1], in0=q[d][:, 31:32], in1=tab[:, 31:32],
                                                                   op=ALU.mult), r=[("q", d), tk], w=[fk])

    def outputs(g, g8, dgt, dkey):
        rounds = [(0, [(0, NKC, 0)] + [(1 + i, 128, NKC + 128 * i) for i in range(3)]),
                  (1, [(4 + i, 128, NKC + 128 * (3 + i)) for i in range(4)]),
                  (0, [(8, 128, NKC + 128 * 7)])]
        for (pb, blks) in rounds:
            py = psY[pb]
            for slot, (kb, nk, p0) in enumerate(blks):
                ops = [(dgt[:, p0:p0 + nk], toep[:, g8, :], [dkey, "toep"]),
                       (FC[0][:, p0:p0 + nk], wA[:, 0, g8, :], [("FC", 0), ("wA", 0)]),
                       (FS[0][:, p0:p0 + nk], wB[:, 0, g8, :], [("FS", 0), ("wB", 0)]),
                       (FC[1][:, p0 + 1:p0 + 1 + nk], wA[:, 1, g8, :], [("FC", 1), ("wA", 1)]),
                       (FS[1][:, p0 + 1:p0 + 1 + nk], wB[:, 1, g8, :], [("FS", 1), ("wB", 1)])]
                for oi, (lt, rh, rk) in enumerate(ops):
                    A("pe", lambda e, py=py, slot=slot, nk=nk, lt=lt, rh=rh, oi=oi: e.matmul(
                        py[:nk, slot, :], lhsT=lt, rhs=rh, start=(oi == 0), stop=(oi == 4)),
                        r=rk, w=[("psY", pb), "Ymm"])
            if blks[0][1] == NKC:
                A("act", lambda e, py=py, yt=yt, g8=g8: e.activation(
                    out=yt[:NKC, 0, :, g8 * 16:(g8 + 1) * 16], in_=py[:NKC, 0, :].rearrange("k (t h) -> k t h", h=16),
                    func=AF.Gelu), r=[("psY", pb)], w=[ykey])
                rest = blks[1:]
                s0 = 1
            else:
                rest = blks
                s0 = 0
            kb0 = rest[0][0]
            nb = len(rest)
            A("act", lambda e, py=py, yt=yt, g8=g8, kb0=kb0, nb=nb, s0=s0: e.activation(
                out=yt[:, kb0:kb0 + nb, :, g8 * 16:(g8 + 1) * 16],
                in_=py[:, s0:s0 + nb, :].rearrange("k b (t h) -> k b t h", h=16), func=AF.Gelu),
                r=[("psY", pb)], w=[ykey])

    def store(gb):
        T.dma(YSC[0:C, gb * 128:(gb + 1) * 128].rearrange("(k t) f -> k t f", t=8), yt[:NKC, 0], r=[ykey], w=["YSC"],
              stream="ysc", eng="pool")
        for kb in range(8):
            T.dma(YSC[C + kb * 1024:C + (kb + 1) * 1024, gb * 128:(gb + 1) * 128].rearrange("(k t) f -> k t f", t=8),
                  yt[:, 1 + kb], r=[ykey], w=["YSC"], stream="ysc", eng="pool")

    def load_dg(g):
        T.dma(dg[g % 2][:], DSCR[g], w=[("dg", g % 2)], stream="dg%d" % (g % 2))

    def stage1(g):
        dgt = dg[g % 2]; dkey = ("dg", g % 2)
        core(g, g % 8, 0, dgt, dkey)
        core(g, g % 8, 1, dgt, dkey)

    gen_weights(0)
    tables(0, 0); tables(0, 1)
    load_dg(0)
    stage1(0)
    for g in range(64):
        g8 = g % 8
        core2(g, g8, 0)
        core2(g, g8, 1)
        last_in_gb = (g8 == 7)
        if g + 1 < 64:
            tables(g + 1, 0); tables(g + 1, 1)
            load_dg(g + 1)
            if not last_in_gb:
                stage1(g + 1)
        outputs(g, g8, dg[g % 2], ("dg", g % 2))
        if last_in_gb:
            store(g // 8)
            if g + 1 < 64:
                gen_weights(g // 8 + 1)
                stage1(g + 1)


def phase_post0(nc, T, blocks, VEC, YSC, SZ, w_glu, b_glu, w_out, identb, X1, CTX1):
    with ExitStack() as es:
        sb = lambda n, s, d=F32: es.enter_context(nc.sbuf_tensor("p0_" + n, s, d))
        stage = [sb("stg0", [128, D]), sb("stg1", [128, D])]
        block = es.enter_context(nc.Block())
        A = T.add
        grep = []
        for n in range(2):
            t = sb("gate%d" % n, [128, D])
            T.dma(t[:], VEC[n * 3 + 2, :].partition_broadcast(128), w=[("gate", n)], stream="gate%d" % n)
            grep.append(t)
        wg = load_cast_weight(nc, T, es, "p0_wglu", w_glu, D, stage)
        wo = load_cast_weight(nc, T, es, "p0_wout", w_out, D, stage)
        bgf = sb("bgf", [1, D]); bgb = sb("bgb", [1, D], BF16); ones = sb("ones", [1, 128], BF16)
        T.dma(bgf[:], b_glu.rearrange("(o f) -> o f", o=1), w=["bgf"], stream="bgf")
        A("dve", lambda e: e.tensor_copy(out=bgb[:], in_=bgf[:]), r=["bgf"], w=["bgb"])
        A("dve", lambda e: e.memset(ones[:], 1.0), w=["ones"])
        yg = sb("yg", [128, 8, D], BF16); szt = sb("szt", [128, 8, D], BF16)
        ygT = sb("ygT", [128, 8, 8, 128], BF16)
        sg = [sb("sg0", [128, D], BF16), sb("sg1", [128, D], BF16)]
        y3 = sb("y3", [128, 8, D], BF16); y3T = sb("y3T", [128, 8, 8, 128], BF16)
        xh = [sb("xh0", [128, 4, D]), sb("xh1", [128, 4, D])]
        tmp = [sb("tmp0", [128, 512]), sb("tmp1", [128, 512])]
        psT = [es.enter_context(nc.psum_tensor("p0psT%d" % i, [128, 8, 128], BF16)) for i in range(2)]
        psU = [es.enter_context(nc.psum_tensor("p0psU%d" % i, [128, 512], F32)) for i in range(4)]
        cnt = {"ev": 0, "u": 0, "tmp": 0}

        def transposes(src, dstT, skey, dkey, nk):
            for ft in range(8):
                pt = psT[ft % 2]
                for t in range(8):
                    A("pe", lambda e, pt=pt, t=t, ft=ft: e.transpose(
                        out=pt[:, t, :nk], in_=src[:nk, t, ft * 128:(ft + 1) * 128], identity=identb[:nk, :nk]),
                        r=[(skey, t), "identb"], w=[("psT", ft % 2)])
                if cnt["ev"] % 2 == 0:
                    A("act", lambda e, pt=pt, ft=ft: e.copy(out=dstT[:, ft, :, :nk], in_=pt[:, :, :nk]),
                      r=[("psT", ft % 2)], w=[(dkey, ft)])
                else:
                    A("dve", lambda e, pt=pt, ft=ft: e.tensor_copy(out=dstT[:, ft, :, :nk], in_=pt[:, :, :nk]),
                      r=[("psT", ft % 2)], w=[(dkey, ft)])
                cnt["ev"] += 1

        def do_block(nk, xap, kg0, row0, is_ctx, bidx):
            n = 1 if is_ctx else 0
            xv = xap.rearrange("(k s) f -> k s f", s=8)
            T.dma(yg[:nk], YSC[row0:row0 + nk * 8, :].rearrange("(k t) f -> k t f", t=8), w=[("yg", t) for t in range(8)],
                  stream="yg")
            T.dma(szt[:nk], SZ[row0:row0 + nk * 8, :].rearrange("(k t) f -> k t f", t=8), w=["szt"], stream="szt")
            for half in range(2):
                T.dma(xh[half][:nk], xv[:, half * 4:(half + 1) * 4, :], w=[("xh", half)], stream="xh%d" % half)
            transposes(yg, ygT, "yg", "ygT", nk)
            for t in range(8):
                sgt = sg[t % 2]
                for hf in range(2):
                    pu = psU[cnt["u"] % 4]; pkey = ("psU", cnt["u"] % 4); cnt["u"] += 1
                    A("pe", lambda e, pu=pu, hf=hf: e.matmul(pu[:nk, :], lhsT=ones[0:1, :nk], rhs=bgb[0:1, hf * 512:(hf + 1) * 512],
                                                            start=True, stop=False), r=["ones", "bgb"], w=[pkey])
                    for ft in range(8):
                        A("pe", lambda e, pu=pu, hf=hf, ft=ft, t=t: e.matmul(
                            pu[:nk, :], lhsT=ygT[:, ft, t, :nk], rhs=wg[:, ft, hf * 512:(hf + 1) * 512],
                            start=False, stop=(ft == 7)), r=[("ygT", ft), "p0_wglu"], w=[pkey])
                    A("act", lambda e, pu=pu, hf=hf, sgt=sgt: e.activation(out=sgt[:nk, hf * 512:(hf + 1) * 512], in_=pu[:nk, :],
                                                                          func=AF.Sigmoid), r=[pkey], w=[("sg", t % 2)])
                A("dve", lambda e, t=t, sgt=sgt: e.tensor_tensor(out=sgt[:nk], in0=sgt[:nk], in1=yg[:nk, t, :], op=ALU.mult),
                  r=[("sg", t % 2), ("yg", t)], w=[("sg", t % 2)])
                A("pool", lambda e, t=t, sgt=sgt: e.tensor_tensor(out=y3[:nk, t, :], in0=sgt[:nk], in1=szt[:nk, t, :], op=ALU.mult),
                  r=[("sg", t % 2), "szt"], w=[("y3", t)])
            transposes(y3, y3T, "y3", "y3T", nk)
            for t in range(8):
                half, s4 = divmod(t, 4)
                for hf in range(2):
                    pu = psU[cnt["u"] % 4]; pkey = ("psU", cnt["u"] % 4); cnt["u"] += 1
                    for ft in range(8):
                        A("pe", lambda e, pu=pu, hf=hf, ft=ft, t=t: e.matmul(
                            pu[:nk, :], lhsT=y3T[:, ft, t, :nk], rhs=wo[:, ft, hf * 512:(hf + 1) * 512],
                            start=(ft == 0), stop=(ft == 7)), r=[("y3T", ft), "p0_wout"], w=[pkey])
                    tm = tmp[cnt["tmp"] % 2]; tkey = ("tmp", cnt["tmp"] % 2); cnt["tmp"] += 1
                    A("dve", lambda e, pu=pu, hf=hf, tm=tm, n=n: e.tensor_tensor(
                        out=tm[:nk], in0=pu[:nk, :], in1=grep[n][:nk, hf * 512:(hf + 1) * 512], op=ALU.mult),
                        r=[pkey, ("gate", n)], w=[tkey])
                    A("pool", lambda e, hf=hf, tm=tm, half=half, s4=s4: e.tensor_tensor(
                        out=xh[half][:nk, s4, hf * 512:(hf + 1) * 512], in0=xh[half][:nk, s4, hf * 512:(hf + 1) * 512],
                        in1=tm[:nk], op=ALU.add), r=[tkey, ("xh", half)], w=[("xh", half)])
            dst = CTX1 if is_ctx else X1[bidx * 1024:(bidx + 1) * 1024, :]
            dv = dst.rearrange("(k s) f -> k s f", s=8)
            for half in range(2):
                T.dma(dv[:, half * 4:(half + 1) * 4, :], xh[half][:nk], r=[("xh", half)], w=["X1"], stream="x1st", eng="pool")
        for blk in blocks:
            do_block(*blk)
        T.emit(block)


def phase_attn(nc, T, VEC, X1, CTX1, w_in, q_norm, k_norm, w_out, fin_g, identf, identb, posr_in, posc_in, fidx_in,
               out, stop_after):
    NKT = 66
    A = T.add
    with ExitStack() as es:
        sb = lambda n, s, d=F32: es.enter_context(nc.sbuf_tensor("a_" + n, s, d))
        KT = sb("KT", [128, 2, NKT * 128], BF16)
        Vt = sb("V", [128, NKT, 4, 65], BF16)
        cosr = sb("cosr", [128, 8, 16]); sinr = sb("sinr", [128, 8, 16])
        cosc = sb("cosc", [128, 8, 16]); sinc = sb("sinc", [128, 8, 16])
        rep = {}
        for nm in ("gs0", "sh0"):
            rep[nm] = sb("rep_" + nm, [128, D])
        qn_rep = sb("qn_rep", [128, 64]); kn_rep = sb("kn_rep", [128, 64])
        halfpi = sb("halfpi", [128, 1])
        xh = [sb("xh0", [128, 4, D])] * 2
        tmp32 = [sb("tmp32a", [128, D])] * 2
        ss = sb("ss", [128, 8]); ms = sb("ms", [128, 8]); rstd = sb("rstd", [128, 8])
        h = sb("h", [128, 4, D], BF16)
        hT = sb("hT", [128, 8, 4, 128], BF16)
        sq = sb("sq", [128, 512]); qn = [sb("qn0", [128, 512])] * 2
        junk = sq[:].bitcast(BF16)
        ssh = sb("ssh", [128, 8]); msh = sb("msh", [128, 8]); rsh = sb("rsh", [128, 8])
        ra_ = sb("ra", [128, 512]); rb_ = sb("rb", [128, 512])
        cc = sb("cc", [128, 8, 2, 16]); cs_ = sb("cs", [128, 8, 2, 16])
        krope = sb("krope", [128, 4, 64], BF16)
        psU = [es.enter_context(nc.psum_tensor("apsU0", [128, 512], F32))] * 2
        psS = [es.enter_context(nc.psum_tensor("apsS%d" % i, [128, 2, 512], F32)) for i in range(2)]
        psO = [es.enter_context(nc.psum_tensor("apsO%d" % i, [128, 512], F32)) for i in range(2)]
        psN = es.enter_context(nc.psum_tensor("apsN", [128, 4, 128], F32))
        psTb = psN[:].rearrange("p a b -> p (a b)").bitcast(BF16).rearrange("p (a b) -> p a b", b=128)
        cnt = {"u": 0, "ev": 0, "s": 0}
        pu_bufs = [(psU[0][:, :], ("pu", 0))] + [(psS[i_][:, c_, :], ("bank", i_, c_)) for i_ in range(2) for c_ in range(2)]

        def next_pu():
            r_ = pu_bufs[cnt["u"] % len(pu_bufs)]
            cnt["u"] += 1
            return r_

        def norm_unit(nk, half, n, ns=4):
            xt = xh[half]
            A("dve", lambda e: e.memset(ss[:], 0.0), w=["ss"])
            for s4 in range(ns):
                A("act", lambda e, s4=s4: e.activation(out=junk[:nk, :], in_=xt[:nk, s4, :], func=AF.Square,
                                                       accum_out=ss[:nk, s4:s4 + 1]), r=[("xh", 0)], w=["sq", "ss"])
            A("dve", lambda e: e.tensor_scalar(out=ms[:nk, 0:ns], in0=ss[:nk, 0:ns], scalar1=1.0 / D, scalar2=EPS, op0=ALU.mult,
                                               op1=ALU.add), r=["ss"], w=["ms"])
            A("act", lambda e: e.sqrt(out=ms[:nk, 0:ns], in_=ms[:nk, 0:ns]), r=["ms"], w=["ms"])
            A("dve", lambda e: e.reciprocal(out=rstd[:nk, 0:ns], in_=ms[:nk, 0:ns]), r=["ms"], w=["rstd"])
            for s4 in range(ns):
                tm = tmp32[s4 % 2]
                A("dve", lambda e, s4=s4, tm=tm: e.scalar_tensor_tensor(
                    out=tm[:nk], in0=xt[:nk, s4, :], scalar=rstd[:nk, s4:s4 + 1], in1=rep["gs%d" % n][:nk],
                    op0=ALU.mult, op1=ALU.mult), r=[("xh", 0), "rstd", "rep"], w=[("tmp32", 0)])
                A("pool", lambda e, s4=s4, tm=tm: e.tensor_tensor(out=h[:nk, s4, :], in0=tm[:nk], in1=rep["sh%d" % n][:nk],
                                                                 op=ALU.add), r=[("tmp32", 0), "rep"], w=["h"])
            tok_transposes(h, nk, ns)

        def tok_transposes(src, nk, ns=4):
            for ft in range(8):
                for s4 in range(ns):
                    A("pe", lambda e, s4=s4, ft=ft: e.transpose(out=psTb[:, s4, :nk], in_=src[:nk, s4, ft * 128:(ft + 1) * 128],
                                                                identity=identb[:nk, :nk]), r=["h", "identb"], w=["psNT"])
                if nk == 128:
                    ov_ = hT[:, ft, 0:ns, :]
                else:
                    ov_ = hT[:, ft].rearrange("p s k -> p (s k)")[:, 0:ns * nk].rearrange("p (s k) -> p s k", k=nk)
                if cnt["ev"] % 2 == 0:
                    A("act", lambda e, ft=ft, ov_=ov_: e.copy(out=ov_, in_=psTb[:, 0:ns, :nk]), r=["psNT"], w=["hT"])
                else:
                    A("dve", lambda e, ft=ft, ov_=ov_: e.tensor_copy(out=ov_, in_=psTb[:, 0:ns, :nk]), r=["psNT"], w=["hT"])
                cnt["ev"] += 1

        def head_norm(pu, pkey, nh, norm_rep_t, outf):
            w = nh * 64
            A("act", lambda e: e.activation(out=sq[:, 0:w], in_=pu[:, 0:w], func=AF.Square), r=[pkey], w=["sq"])
            A("dve", lambda e: e.tensor_reduce(out=ssh[:, 0:nh], in_=sq[:, 0:w].rearrange("p (a d) -> p a d", d=64), axis=AX.X,
                                               op=ALU.add), r=["sq"], w=["ssh"])
            A("dve", lambda e: e.tensor_scalar(out=msh[:, 0:nh], in0=ssh[:, 0:nh], scalar1=1.0 / 64, scalar2=EPS, op0=ALU.mult,
                                               op1=ALU.add), r=["ssh"], w=["msh"])
            A("act", lambda e: e.sqrt(out=msh[:, 0:nh], in_=msh[:, 0:nh]), r=["msh"], w=["msh"])
            A("dve", lambda e: e.reciprocal(out=rsh[:, 0:nh], in_=msh[:, 0:nh]), r=["msh"], w=["rsh"])
            A("dve", lambda e: e.tensor_tensor(out=outf[:, 0:w].rearrange("p (a d) -> p a d", d=64),
                                               in0=pu[:, 0:w].rearrange("p (a d) -> p a d", d=64),
                                               in1=bc(rsh[:, 0:nh].unsqueeze(2), [128, nh, 64]), op=ALU.mult),
              r=[pkey, "rsh"], w=["qnf"])
            A("pool", lambda e: e.tensor_tensor(out=outf[:, 0:w].rearrange("p (a d) -> p a d", d=64),
                                                in0=outf[:, 0:w].rearrange("p (a d) -> p a d", d=64),
                                                in1=bc(norm_rep_t[:].unsqueeze(1), [128, nh, 64]), op=ALU.mult),
              r=["qnf", "nrep"], w=["qnf"])

        def rope(src, nh, s, outv):
            sv = src[:, 0:nh * 64].rearrange("p (a x t f) -> p a x t f", x=2, t=2, f=16)
            ov = outv.rearrange("p a (x t f) -> p a x t f", x=2, t=2, f=16)
            x1, x2 = sv[:, :, :, 0, :], sv[:, :, :, 1, :]
            cb = bc(cc[:, s].unsqueeze(1), [128, nh, 2, 16]); sbb = bc(cs_[:, s].unsqueeze(1), [128, nh, 2, 16])
            n2 = nh * 32
            av = ra_[:, 0:n2].rearrange("p (a x f) -> p a x f", x=2, f=16)
            bv = rb_[:, 0:n2].rearrange("p (a x f) -> p a x f", x=2, f=16)
            av2 = ra_[:, n2:2 * n2].rearrange("p (a x f) -> p a x f", x=2, f=16)
            bv2 = rb_[:, n2:2 * n2].rearrange("p (a x f) -> p a x f", x=2, f=16)
            A("dve", lambda e: e.tensor_tensor(out=av, in0=x1, in1=cb, op=ALU.mult), r=["qnf", "cc"], w=["ra"])
            A("pool", lambda e: e.tensor_tensor(out=bv, in0=x2, in1=sbb, op=ALU.mult), r=["qnf", "cc"], w=["rb"])
            A("dve", lambda e: e.tensor_tensor(out=av2, in0=x2, in1=cb, op=ALU.mult), r=["qnf", "cc"], w=["ra"])
            A("pool", lambda e: e.tensor_tensor(out=bv2, in0=x1, in1=sbb, op=ALU.mult), r=["qnf", "cc"], w=["rb"])
            A("dve", lambda e: e.tensor_tensor(out=ov[:, :, :, 0, :], in0=av, in1=bv, op=ALU.subtract), r=["ra", "rb"], w=["roped"])
            A("pool", lambda e: e.tensor_tensor(out=ov[:, :, :, 1, :], in0=av2, in1=bv2, op=ALU.add), r=["ra", "rb"], w=["roped"])

        def block_tables(b):
            for (tab, rsrc, csrc) in ((cc, cosr, cosc), (cs_, sinr, sinc)):
                A("pool", lambda e, tab=tab, rsrc=rsrc: e.tensor_copy(out=tab[:, :, 0, :], in_=bc(rsrc[:, b, :].unsqueeze(1), [128, 8, 16])),
                  r=["tabs", "roped"], w=["cc"])
                A("pool", lambda e, tab=tab, csrc=csrc: e.tensor_copy(out=tab[:, :, 1, :], in_=csrc[:]), r=["tabs", "roped"], w=["cc"])

        with ExitStack() as es1:
            sbt = lambda n, s, d=F32: es1.enter_context(nc.sbuf_tensor("a1_" + n, s, d))
            stage = [sbt("stg0", [128, 512]), sbt("stg1", [128, 512])]
            posr = sbt("posr", [128, 8]); posc = sbt("posc", [128, 8]); fidx = sbt("fidx", [128, 16]); freq = sbt("freq", [128, 16])
            ta = sbt("ta", [128, 8, 16]); tb = sbt("tb", [128, 8, 16]); tn = sbt("tn", [128, 8, 16], I32)
            rep["gs1"] = sbt("rep_gs1", [128, D]); rep["sh1"] = sbt("rep_sh1", [128, D])
            block = es1.enter_context(nc.Block())
            for nm, v in (("gs0", 6), ("sh0", 7), ("gs1", 9), ("sh1", 10)):
                T.dma(rep[nm][:], VEC[v, :].partition_broadcast(128), w=["rep"], stream="rep_" + nm)
            T.dma(qn_rep[:], q_norm.partition_broadcast(128), w=["nrep"], stream="qnr")
            T.dma(kn_rep[:], k_norm.partition_broadcast(128), w=["nrep"], stream="knr")
            T.dma(posr[:], posr_in[:, :], w=["posr"], stream="posr")
            T.dma(posc[:], posc_in[:, :], w=["posc"], stream="posc")
            T.dma(fidx[:], fidx_in[:, :], w=["fidx"], stream="fidx")
            A("pool", lambda e: e.memset(halfpi[:], float(np.pi / 2)), w=["halfpi"])
            A("pool", lambda e: e.memset(Vt[:], 1.0), w=["V"])
            A("act", lambda e: e.activation(out=freq[:], in_=fidx[:], func=AF.Exp, scale=float(-np.log(10000.0) / 16.0)),
              r=["fidx"], w=["freq"])
            for (pos, ct, st) in ((posr, cosr, sinr), (posc, cosc, sinc)):
                A("dve", lambda e, pos=pos: e.tensor_tensor(out=ta[:], in0=bc(pos[:].unsqueeze(2), [128, 8, 16]),
                                                           in1=bc(freq[:].unsqueeze(1), [128, 8, 16]), op=ALU.mult),
                  r=["posr", "posc", "freq"], w=["ta"])
                A("dve", lambda e: e.tensor_scalar(out=tn[:], in0=ta[:], scalar1=1.0 / TWO_PI, scalar2=0.0, op0=ALU.mult, op1=ALU.add),
                  r=["ta"], w=["tn"])
                A("dve", lambda e: e.scalar_tensor_tensor(out=tb[:], in0=tn[:], scalar=-TWO_PI, in1=ta[:], op0=ALU.mult, op1=ALU.add),
                  r=["tn", "ta"], w=["tb"])
                A("dve", lambda e: e.tensor_scalar(out=tb[:], in0=tb[:], scalar1=-PI_SAFE, scalar2=PI_SAFE, op0=ALU.max, op1=ALU.min),
                  r=["tb"], w=["tb"])
                A("dve", lambda e: e.scalar_tensor_tensor(out=ta[:], in0=tb[:], scalar=-1.0, in1=tb[:], op0=ALU.mult, op1=ALU.max),
                  r=["tb"], w=["ta"])
                A("act", lambda e, st=st: e.activation(out=st[:], in_=tb[:], func=AF.Sin), r=["tb"], w=["tabs"])
                A("act", lambda e, ct=ct: e.activation(out=ct[:], in_=ta[:], func=AF.Sin, scale=-1.0, bias=halfpi[:, 0:1]),
                  r=["ta", "halfpi"], w=["tabs"])
            wkv = es1.enter_context(nc.sbuf_tensor("a1_wkv", [128, 8, 512], BF16))
            for ft in range(8):
                st_ = stage[ft % 2]
                T.dma(st_[:], w_in[ft * 128:(ft + 1) * 128, 1024:1536], w=[("stage", ft % 2)], stream="stage%d" % (ft % 2))
                A("pool", lambda e, st_=st_, ft=ft: e.tensor_copy(
                    out=wkv[:, ft, 0:256].rearrange("p (gp hf d) -> p gp hf d", gp=2, hf=2),
                    in_=st_[:, 0:256].rearrange("p (hf gp d) -> p gp hf d", gp=2, hf=2)), r=[("stage", ft % 2)], w=["wkv"])
                A("pool", lambda e, st_=st_, ft=ft: e.tensor_copy(out=wkv[:, ft, 256:512], in_=st_[:, 256:512]),
                  r=[("stage", ft % 2)], w=["wkv"])

            def kv_tile(kt, lhs_fn, s, is_ctx):
                pu, pkey = next_pu()
                for ft in range(8):
                    A("pe", lambda e, pu=pu, ft=ft: e.matmul(pu[:, :], lhsT=lhs_fn(ft), rhs=wkv[:, ft, :], start=(ft == 0), stop=(ft == 7)),
                      r=["hT", "wkv"], w=[pkey])
                A("act", lambda e, pu=pu: e.copy(out=Vt[:, kt, :, 0:64], in_=pu[:, 256:512].rearrange("p (g d) -> p g d", d=64)),
                  r=[pkey], w=["V"])
                head_norm(pu, pkey, 4, kn_rep, qn[0])
                if is_ctx:
                    A("dve", lambda e: e.tensor_copy(out=krope[:].rearrange("p a d -> p (a d)"), in_=qn[0][:, 0:256]),
                      r=["qnf"], w=["roped"])
                else:
                    rope(qn[0], 4, s, krope[:])
                for gp in range(2):
                    A("pe", lambda e, gp=gp: e.transpose(out=psTb[:, gp, :], in_=krope[:, 2 * gp:2 * gp + 2, :].rearrange("p a d -> p (a d)"), identity=identb[:]),
                      r=["roped", "identb"], w=["psNT"])
                A("dve", lambda e: e.tensor_copy(out=KT[:, :, kt * 128:(kt + 1) * 128], in_=psTb[:, 0:2, :]), r=["psNT"], w=["KT"])

            cv = CTX1.rearrange("(k s) f -> k s f", s=8)
            for j in range(2):
                T.dma(xh[0][:NKC], cv[:, j * 4:(j + 1) * 4, :], w=[("xh", 0)], stream="xh0")
                norm_unit(NKC, 0, 1)
                kv_tile(j, lambda ft: hT[:, ft].rearrange("p s k -> p (s k)")[:, 0:4 * NKC], None, True)
            for b in range(8):
                block_tables(b)
                xv = X1[b * 1024:(b + 1) * 1024, :].rearrange("(k s) f -> k s f", s=8)
                for hf in range(2):
                    T.dma(xh[0][:], xv[:, hf * 4:(hf + 1) * 4, :], w=[("xh", 0)], stream="xh0")
                    norm_unit(128, 0, 0)
                    for s4 in range(4):
                        kv_tile(2 + b * 8 + hf * 4 + s4, lambda ft, s4=s4: hT[:, ft, s4, :], hf * 4 + s4, False)
            T.emit(block)
        if stop_after == 4:
            return finish(nc, T, out)

        with ExitStack() as es2:
            sbt = lambda n, s, d=F32: es2.enter_context(nc.sbuf_tensor("a2_" + n, s, d))
            stg = tmp32[0]
            rep["gate"] = sbt("rep_gate", [128, D]); rep["fin"] = sbt("rep_fin", [128, D])
            block = es2.enter_context(nc.Block())
            T.dma(rep["gate"][:], VEC[8, :].partition_broadcast(128), w=["rep"], stream="rep_gate")
            T.dma(rep["fin"][:], fin_g.partition_broadcast(128), w=["rep"], stream="rep_fin")
            wqz = es2.enter_context(nc.sbuf_tensor("a2_wqz", [128, 8, 2048], BF16))
            wo = es2.enter_context(nc.sbuf_tensor("a2_wo", [128, 8, D], BF16))
            for ft in range(8):
                for (c0, o0) in ((0, 0), (1536, 1024)):
                    T.dma(stg[:], w_in[ft * 128:(ft + 1) * 128, c0:c0 + 1024], w=[("tmp32", 0)], stream="stg")
                    if o0 == 0:
                        A("pool", lambda e, ft=ft: e.tensor_copy(
                            out=wqz[:, ft, 0:1024].rearrange("p (hp hf d) -> p hp hf d", hp=8, hf=2),
                            in_=stg[:].rearrange("p (hf hp d) -> p hp hf d", hp=8, hf=2)), r=[("tmp32", 0)], w=["wqz"])
                    else:
                        A("pool", lambda e, ft=ft, o0=o0: e.tensor_copy(out=wqz[:, ft, o0:o0 + 1024], in_=stg[:]),
                          r=[("tmp32", 0)], w=["wqz"])
                T.dma(stg[:], w_out[ft * 128:(ft + 1) * 128, :], w=[("tmp32", 0)], stream="stg")
                A("pool", lambda e, ft=ft: e.tensor_copy(out=wo[:, ft, :], in_=stg[:]), r=[("tmp32", 0)], w=["wo"])
            qrope = sbt("qrope", [128, 16, 64], BF16)
            QT = sbt("QT", [128, 8, 512], BF16)
            szq = sbt("szq", [128, 4, D], BF16)
            PT = [sbt("PT%d" % i, [128, 2, 512], BF16) for i in range(2)]
            oT = [sbt("oT0", [65, 512]), sbt("oT1", [65, 512])]
            rec = sbt("rec", [128, 4])
            tmpo = [tmp32[0][:, 0:512], tmp32[0][:, 512:1024]]
            ucount = 0
            for b in range(8):
                block_tables(b)
                xv = X1[b * 1024:(b + 1) * 1024, :].rearrange("(k s) f -> k s f", s=8)
                ov = out[b * 1024:(b + 1) * 1024, :].rearrange("(k s) f -> k s f", s=8)
                for hf in range(2):
                    xt = xh[0]; xkey = ("xh", 0); uh = 0; ucount += 1
                    T.dma(xt[:], xv[:, hf * 4:(hf + 1) * 4, :], w=[xkey], stream="xh%d" % uh)
                    norm_unit(128, uh, 0)
                    for s4 in range(4):
                        s = hf * 4 + s4
                        for nb in range(4):
                            pu, pkey = next_pu()
                            for ft in range(8):
                                A("pe", lambda e, pu=pu, ft=ft, s4=s4, nb=nb: e.matmul(
                                    pu[:, :], lhsT=hT[:, ft, s4, :], rhs=wqz[:, ft, nb * 512:(nb + 1) * 512],
                                    start=(ft == 0), stop=(ft == 7)), r=["hT", "wqz"], w=[pkey])
                            if nb < 2:
                                qf = qn[0]
                                head_norm(pu, pkey, 8, qn_rep, qf)
                                rope(qf, 8, s, qrope[:, nb * 8:(nb + 1) * 8, :])
                            else:
                                A("act", lambda e, pu=pu, s4=s4, nb=nb: e.activation(
                                    out=szq[:, s4, (nb - 2) * 512:(nb - 1) * 512], in_=pu[:, :], func=AF.Silu), r=[pkey], w=["szq"])
                        for hp in range(8):
                            A("pe", lambda e, hp=hp, s4=s4: e.transpose(out=psTb[:, hp, :], in_=qrope[:, 2 * hp:2 * hp + 2, :].rearrange("p a d -> p (a d)"), identity=identb[:]),
                              r=["roped", "identb"], w=["psNT"])
                        A("dve", lambda e, s4=s4: e.tensor_copy(out=QT[:, :, s4 * 128:(s4 + 1) * 128], in_=psTb[:]), r=["psNT"], w=["QT"])
                    stream = [(hp, kt) for hp in range(8) for kt in range(NKT)]
                    pend = {}

                    def finalize(hd, ot):
                        for j in range(4):
                            A("pe", lambda e, j=j, ot=ot: e.transpose(out=psN[:, j, 0:65], in_=ot[:, j * 128:(j + 1) * 128],
                                                                     identity=identf[0:65, 0:65]), r=[("oT", hd // 8), "identf"], w=["psNT"])
                        A("dve", lambda e: e.reciprocal(out=rec[:], in_=psN[:, :, 64]), r=["psNT"], w=["rec"])
                        A("dve", lambda e, hd=hd: e.tensor_tensor(out=h[:, :, hd * 64:(hd + 1) * 64], in0=psN[:, :, 0:64],
                                                                 in1=bc(rec[:].unsqueeze(2), [128, 4, 64]), op=ALU.mult),
                          r=["psNT", "rec"], w=["h"])

                    def pv(idx):
                        hp, kt = stream[idx]
                        g = hp // 4
                        pt = PT[idx % 2]
                        for c in range(2):
                            A("pe", lambda e, pt=pt, kt=kt, g=g, c=c: e.matmul(
                                psO[c][0:65, :], lhsT=Vt[:, kt, g + 2 * c, :], rhs=pt[:, c, :], start=(kt == 0), stop=(kt == NKT - 1)),
                                r=[("PT", idx % 2), "V"], w=[("psO", c)])
                        if kt == NKT - 1:
                            for c in range(2):
                                A("dve", lambda e, c=c: e.tensor_copy(out=oT[c][:], in_=psO[c][0:65, :]), r=[("psO", c)], w=[("oT", c)])
                            pend[idx + 2] = hp

                    for idx, (hp, kt) in enumerate(stream):
                        gp = hp // 4
                        ps = psS[idx % 2]; pt = PT[idx % 2]
                        for c in range(2):
                            rows = slice(c * 64, c * 64 + 64)
                            A("pe", lambda e, ps=ps, rows=rows, gp=gp, kt=kt, hp=hp, c=c: e.matmul(
                                ps[:, c, :], lhsT=KT[rows, gp, kt * 128:(kt + 1) * 128], rhs=QT[rows, hp, :], start=True, stop=True,
                                tile_position=(64 * c, 0)),
                                r=["KT", "QT"], w=[("bank", idx % 2, c)])
                        A("act", lambda e, ps=ps, pt=pt: e.activation(out=pt[:], in_=ps[:], func=AF.Exp, scale=0.125),
                          r=[("bank", idx % 2, 0), ("bank", idx % 2, 1)], w=[("PT", idx % 2)])
                        if idx >= 1:
                            pv(idx - 1)
                        if idx in pend:
                            hp_ = pend.pop(idx)
                            finalize(hp_, oT[0]); finalize(hp_ + 8, oT[1])
                    pv(len(stream) - 1)
                    for k_ in sorted(pend):
                        finalize(pend[k_], oT[0]); finalize(pend[k_] + 8, oT[1])
                    A("dve", lambda e: e.tensor_tensor(out=h[:], in0=h[:], in1=szq[:], op=ALU.mult), r=["h", "szq"], w=["h"])
                    tok_transposes(h, 128)
                    for s4 in range(4):
                        for nb in range(2):
                            pu, pkey = next_pu()
                            for ft in range(8):
                                A("pe", lambda e, pu=pu, ft=ft, s4=s4, nb=nb: e.matmul(
                                    pu[:, :], lhsT=hT[:, ft, s4, :], rhs=wo[:, ft, nb * 512:(nb + 1) * 512],
                                    start=(ft == 0), stop=(ft == 7)), r=["hT", "wo"], w=[pkey])
                            tm = tmpo[nb]
                            A("dve", lambda e, pu=pu, tm=tm, nb=nb: e.tensor_tensor(out=tm, in0=pu[:, :],
                                                                                   in1=rep["gate"][:, nb * 512:(nb + 1) * 512], op=ALU.mult),
                              r=[pkey, "rep"], w=[("tmp32", 0)])
                            A("pool", lambda e, tm=tm, nb=nb, s4=s4, xt=xt: e.tensor_tensor(
                                out=xt[:, s4, nb * 512:(nb + 1) * 512], in0=xt[:, s4, nb * 512:(nb + 1) * 512], in1=tm, op=ALU.add),
                                r=[("tmp32", 0), xkey], w=[xkey])
                    A("dve", lambda e: e.memset(ss[:], 0.0), w=["ss"])
                    for s4 in range(4):
                        A("act", lambda e, s4=s4, xt=xt: e.activation(out=junk[:, :], in_=xt[:, s4, :], func=AF.Square,
                                                                     accum_out=ss[:, s4:s4 + 1]), r=[xkey], w=["sq", "ss"])
                    A("dve", lambda e: e.tensor_scalar(out=ms[:, 0:4], in0=ss[:, 0:4], scalar1=1.0 / D, scalar2=EPS, op0=ALU.mult,
                                                       op1=ALU.add), r=["ss"], w=["ms"])
                    A("act", lambda e: e.sqrt(out=ms[:, 0:4], in_=ms[:, 0:4]), r=["ms"], w=["ms"])
                    A("dve", lambda e: e.reciprocal(out=rstd[:, 0:4], in_=ms[:, 0:4]), r=["ms"], w=["rstd"])
                    for s4 in range(4):
                        A("dve", lambda e, s4=s4, xt=xt: e.scalar_tensor_tensor(
                            out=xt[:, s4, :], in0=xt[:, s4, :], scalar=rstd[:, s4:s4 + 1], in1=rep["fin"][:], op0=ALU.mult, op1=ALU.mult),
                            r=[xkey, "rstd", "rep"], w=[xkey])
                    T.dma(ov[:, hf * 4:(hf + 1) * 4, :], xt[:], r=[xkey], w=["out"], stream="outst", eng="pool")
            A("sp", None, r=["out"])
            T.emit(block)
    return None


def _host_consts():
    ident = np.eye(128, dtype=np.float32)
    sidx = np.arange(128) // 16
    maskf = (sidx[None, :] >= sidx[:, None]).astype(np.float32)
    maskb = (sidx[None, :] <= sidx[:, None]).astype(np.float32)
    kio = np.tile(np.arange(NK, dtype=np.float32)[None, :], (128, 1))
    jv = np.tile(np.arange(-7, 9, dtype=np.float32)[None, :], (128, 1))
    k = np.arange(128)
    posr = (16 * np.arange(8)[None, :] + (k // 8)[:, None]).astype(np.float32)
    posc = (8 * (k % 8)[:, None] + np.arange(8)[None, :]).astype(np.float32)
    fidx = np.tile(np.arange(16, dtype=np.float32)[None, :], (128, 1))
    return dict(ident=ident, maskf=maskf, maskb=maskb, kio=kio, jvals=jv, posr=posr, posc=posc, fidx=fidx)


def make_in_maps(inputs):
    consts = _host_consts()
    f = lambda a: np.ascontiguousarray(np.asarray(a, dtype=np.float32))
    shared = dict(
        w_mod=f(inputs["w_mod"]), b_mod=f(inputs["b_mod"]), norm_g=f(inputs["norm_g"]),
        ssm_w_in=f(inputs["ssm_w_in"][0]), ssm_a_re=f(inputs["ssm_a_re"][0]), ssm_a_im=f(inputs["ssm_a_im"][0]),
        ssm_log_dt=f(inputs["ssm_log_dt"][0]), ssm_b_re=f(inputs["ssm_b_re"][0]), ssm_b_im=f(inputs["ssm_b_im"][0]),
        ssm_c_re=f(inputs["ssm_c_re"][0]), ssm_c_im=f(inputs["ssm_c_im"][0]), ssm_d=f(inputs["ssm_d"][0]),
        ssm_w_glu=f(inputs["ssm_w_glu"][0]), ssm_b_glu=f(inputs["ssm_b_glu"][0]), ssm_w_out=f(inputs["ssm_w_out"][0]),
        attn_w_in=f(inputs["attn_w_in"][0]), attn_q_norm=f(inputs["attn_q_norm"][0]),
        attn_k_norm=f(inputs["attn_k_norm"][0]), attn_w_out=f(inputs["attn_w_out"][0]),
        final_norm_g=f(inputs["final_norm_g"]), **consts)
    maps = []
    for b in range(8):
        m = dict(shared)
        m["x"] = f(inputs["x"][b]); m["ctx"] = f(inputs["ctx"][b])
        m["cvec"] = f(np.stack([np.asarray(inputs["c"][b]), np.asarray(inputs["c_ctx"])], 0))
        maps.append(m)
    return maps


def kernel(**inputs):
    nc, _ = build_program()
    maps = make_in_maps(inputs)
    res = run_bass_kernel_spmd(nc, maps, core_ids=list(range(8)))
    return np.stack([np.asarray(r["out"], dtype=np.float32) for r in res.results], 0)
```

```python
import numpy as np
from contextlib import ExitStack
import concourse.bass as bass
import concourse.mybir as mybir
from concourse.bass_utils import run_bass_kernel_spmd

F32 = mybir.dt.float32
BF16 = mybir.dt.bfloat16
I32 = mybir.dt.int32
ALU = mybir.AluOpType
AF = mybir.ActivationFunctionType
AX = mybir.AxisListType

D = 1024
L = 8192
C = 256
NKL = 1024
NKC = 32
NK = NKL + NKC
EPS = 1e-6
TWO_PI = float(2 * np.pi)
PI_SAFE = 3.1415925
STOP_S5_SETUP = [False]


class Op:
    __slots__ = ("eng", "fn", "deps", "is_dma", "stream", "signal", "ticket", "semname")


class Tracker:
    ENGS = ["pe", "act", "dve", "pool", "sp"]

    def __init__(self, nc, es):
        self.nc = nc
        self.es = es
        self.sem = {}
        self.count = {}
        self.ops = []
        self.last_w = {}
        self.readers = {}
        self.barrier = {}
        self.waited = {e: {} for e in self.ENGS}

    def _sem(self, name):
        if name not in self.sem:
            self.sem[name] = self.es.enter_context(self.nc.semaphore(name))
            self.count[name] = 0
        return self.sem[name]

    def add(self, eng, fn, r=(), w=(), dma=False, stream=None):
        op = Op()
        op.eng, op.fn, op.is_dma, op.stream = eng, fn, dma, stream
        op.signal, op.ticket, op.semname = dma, 0, None
        deps = set()
        for k in r:
            if k in self.last_w:
                deps.add(self.last_w[k])
        for k in w:
            if k in self.last_w:
                deps.add(self.last_w[k])
            deps.update(self.readers.get(k, ()))
        i = len(self.ops)
        op.deps = deps
        self.ops.append(op)
        for k in r:
            self.readers.setdefault(k, []).append(i)
        for k in w:
            self.last_w[k] = i
            self.readers[k] = []
        return i

    def dma(self, out, in_, r=(), w=(), stream=None, eng="sp", **kw):
        assert stream is not None
        return self.add(eng, lambda e: e.dma_start(out=out, in_=in_, **kw), r=r, w=w, dma=True, stream=stream)

    def emit(self, block):
        ops = self.ops
        for op in ops:
            for d in op.deps:
                dep = ops[d]
                if dep.is_dma or dep.eng != op.eng or op.eng != "pe":
                    dep.signal = True
        last = {}
        for op in ops:
            if not op.is_dma and op.fn is not None:
                last[op.eng] = op
        for op in last.values():
            op.signal = True
        for op in ops:
            if op.signal and op.fn is not None:
                name = ("D_" + op.stream) if op.is_dma else ("E_" + op.eng)
                self._sem(name)
                self.count[name] += 16 if op.is_dma else 1
                op.ticket = self.count[name]
                op.semname = name
        reg = {"pe": block.tensor, "act": block.scalar, "dve": block.vector, "pool": block.gpsimd, "sp": block.sync}
        for eng in self.ENGS:
            eops = [op for op in ops if op.eng == eng]
            if not eops:
                continue

            def body(e, eops=eops, eng=eng):
                waited = self.waited[eng]
                first = True
                for op in eops:
                    waits = {}
                    if first:
                        waits.update(self.barrier)
                        first = False
                    for d in op.deps:
                        dep = ops[d]
                        if dep.is_dma or dep.eng != eng or eng != "pe":
                            if dep.semname is not None:
                                waits[dep.semname] = max(waits.get(dep.semname, 0), dep.ticket)
                    for s, v in waits.items():
                        if v > 0 and waited.get(s, 0) < v:
                            e.wait_ge(self.sem[s], v)
                            waited[s] = v
                    if op.fn is not None:
                        ins = op.fn(e)
                        if op.signal:
                            ins.then_inc(self.sem[op.semname], 16 if op.is_dma else 1)

            reg[eng](body)
        self.barrier = dict(self.count)
        self.ops = []
        self.last_w = {}
        self.readers = {}


def bc(ap, shape):
    return ap.to_broadcast(shape)


def build_program(stop_after=None, debug=False):
    nc = bass.Bass("TRN2", target_bir_lowering=False)
    dt_in = {}

    def din(name, shape, dt=F32):
        dt_in[name] = nc.dram_tensor(name, list(shape), dt, kind="ExternalInput").ap()
        return dt_in[name]

    x = din("x", [L, D]); ctx = din("ctx", [C, D]); cvec = din("cvec", [2, D])
    w_mod = din("w_mod", [2, D, 3 * D]); b_mod = din("b_mod", [2, 3 * D]); norm_g = din("norm_g", [2, D])
    ssm_w_in = din("ssm_w_in", [D, 2 * D])
    a_re = din("ssm_a_re", [2, 64, 64]); a_im = din("ssm_a_im", [2, 64, 64]); log_dt = din("ssm_log_dt", [2, 64])
    b_re = din("ssm_b_re", [2, 64, 64, 16]); b_im = din("ssm_b_im", [2, 64, 64, 16])
    c_re = din("ssm_c_re", [2, 64, 16, 64]); c_im = din("ssm_c_im", [2, 64, 16, 64])
    ssm_d = din("ssm_d", [D]); w_glu = din("ssm_w_glu", [D, D]); b_glu = din("ssm_b_glu", [D])
    ssm_w_out = din("ssm_w_out", [D, D])
    attn_w_in = din("attn_w_in", [D, 2560]); q_norm = din("attn_q_norm", [64]); k_norm = din("attn_k_norm", [64])
    attn_w_out = din("attn_w_out", [D, D]); fin_g = din("final_norm_g", [D])
    ident_in = din("ident", [128, 128]); maskf_in = din("maskf", [128, 128]); maskb_in = din("maskb", [128, 128])
    kio_in = din("kio", [128, NK]); jv_in = din("jvals", [128, 16])
    posr_in = din("posr", [128, 8]); posc_in = din("posc", [128, 8]); fidx_in = din("fidx", [128, 16])

    out = nc.dram_tensor("out", [L, D], F32, kind="ExternalOutput").ap()
    dbg = {}

    def dout(name, shape, dt=F32):
        dbg[name] = nc.dram_tensor(name, list(shape), dt, kind="ExternalOutput" if debug else "Internal").ap()
        return dbg[name]

    VEC = dout("VEC", [12, D])
    DSCR = dout("DSCR", [64, 128, NK], BF16)
    SZ = dout("SZ", [C + L, D], BF16)
    YSC = dout("YSC", [C + L, D], BF16)
    X1 = dout("X1", [L, D])
    CTX1 = dout("CTX1", [C, D])

    blocks = [(NKC, ctx, 0, 0, True, 0)]
    for b in range(8):
        blocks.append((128, x[b * 1024:(b + 1) * 1024, :], NKC + 128 * b, C + 1024 * b, False, b))

    with ExitStack() as ges:
        T = Tracker(nc, ges)
        identf = ges.enter_context(nc.sbuf_tensor("identf", [128, 128], F32))
        identb = ges.enter_context(nc.sbuf_tensor("identb", [128, 128], BF16))

        with ExitStack() as es:
            sb = lambda n, s, d=F32: es.enter_context(nc.sbuf_tensor(n, s, d))
            scraw = sb("scraw", [128, 2, 8]); sc = sb("sc", [128, 8, 2])
            wst = sb("wst", [128, 8, 3 * D])
            bcol = sb("bcol", [128, 2, 24]); gcol = sb("gcol", [128, 2, 8])
            modc = sb("modc", [128, 2, 24, 2]); gs = sb("gs", [128, 2, 8, 2])
            psm = es.enter_context(nc.psum_tensor("psm", [128, 512], F32))
            block = es.enter_context(nc.Block())
            T.dma(identf[:], ident_in[:, :], w=["identf"], stream="c0")
            T.add("dve", lambda e: e.tensor_copy(out=identb[:], in_=identf[:]), r=["identf"], w=["identb"])
            T.dma(scraw[:], cvec.rearrange("n (kt p) -> p n kt", p=128), w=["scraw"], stream="c1",
                  allow_slow_non_contiguous=True)
            T.dma(bcol[:], b_mod.rearrange("i (j p) -> p i j", p=128), w=["bcol"], stream="c2",
                  allow_slow_non_contiguous=True)
            T.dma(gcol[:], norm_g.rearrange("i (j p) -> p i j", p=128), w=["gcol"], stream="c3",
                  allow_slow_non_contiguous=True)
            T.add("act", lambda e: e.activation(out=sc[:].rearrange("p kt n -> p n kt"), in_=scraw[:], func=AF.Silu),
                  r=["scraw"], w=["sc"])
            for i in range(2):
                psv = psm[:, i * 48:(i + 1) * 48].rearrange("p (j n) -> p j n", n=2)
                for kt in range(8):
                    T.dma(wst[:, kt, :], w_mod[i, kt * 128:(kt + 1) * 128, :], w=[("wst", kt)], stream="wst%d" % kt)
                for j in range(24):
                    for kt in range(8):
                        T.add("pe", lambda e, j=j, kt=kt, psv=psv: e.matmul(
                            psv[:, j, :], lhsT=wst[:, kt, j * 128:(j + 1) * 128], rhs=sc[:, kt, :],
                            start=(kt == 0), stop=(kt == 7)),
                            r=[("wst", kt), "sc"], w=[("psm", i)])
                T.add("dve", lambda e, i=i, psv=psv: e.tensor_tensor(
                    out=modc[:, i], in0=psv, in1=bc(bcol[:, i, :].unsqueeze(2), [128, 24, 2]), op=ALU.add),
                    r=[("psm", i), "bcol"], w=[("modc", i)])
                T.add("dve", lambda e, i=i: e.scalar_tensor_tensor(
                    out=gs[:, i], in0=modc[:, i, 8:16, :], scalar=1.0,
                    in1=bc(gcol[:, i, :].unsqueeze(2), [128, 8, 2]), op0=ALU.add, op1=ALU.mult),
                    r=[("modc", i), "gcol"], w=[("gs", i)])
                for n in range(2):
                    for which, src in ((0, gs[:, i, :, n]), (1, modc[:, i, 0:8, n]), (2, modc[:, i, 16:24, n])):
                        v = i * 6 + n * 3 + which
                        T.dma(VEC[v, :].rearrange("(ft p) -> p ft", p=128), src, r=[("gs", i), ("modc", i)],
                              w=["VEC"], stream="vec", allow_slow_non_contiguous=True)
            T.emit(block)
        if stop_after == 0:
            return nc, finish(nc, T, out)

        phase_front(nc, T, blocks, VEC, 0, ssm_w_in, 2 * D, identb, DSCR, SZ, mode="ssm")
        if stop_after == 1:
            return nc, finish(nc, T, out)
        phase_s5(nc, T, dict(a_re=a_re, a_im=a_im, log_dt=log_dt, b_re=b_re, b_im=b_im, c_re=c_re, c_im=c_im,
                             ssm_d=ssm_d, kio=kio_in, jv=jv_in, maskf=maskf_in, maskb=maskb_in),
                 identf, DSCR, YSC)
        if stop_after == 2:
            return nc, finish(nc, T, out)
        phase_post0(nc, T, blocks, VEC, YSC, SZ, w_glu, b_glu, ssm_w_out, identb, X1, CTX1)
        if stop_after == 3:
            return nc, finish(nc, T, out)
        phase_attn(nc, T, VEC, X1, CTX1, attn_w_in, q_norm, k_norm, attn_w_out, fin_g, identf, identb,
                   posr_in, posc_in, fidx_in, out, stop_after)
    return nc, None


def finish(nc, T, out):
    with ExitStack() as es:
        z = es.enter_context(nc.sbuf_tensor("zfin", [128, D], F32))
        block = es.enter_context(nc.Block())
        T.add("dve", lambda e: e.memset(z[:], 0.0), w=["z"])
        T.dma(out[0:128, :], z[:], r=["z"], w=["out"], stream="fin")
        T.add("sp", None, r=["out"])
        T.emit(block)
    return None


def load_cast_weight(nc, T, es, name, w_ap, ncols, stage):
    wt = es.enter_context(nc.sbuf_tensor(name, [128, 8, ncols], BF16))
    for ft in range(8):
        st = stage[ft % 2]
        T.dma(st[:, 0:ncols], w_ap[ft * 128:(ft + 1) * 128, :], w=[("stage", ft % 2)], stream="stage%d" % (ft % 2))
        T.add("pool", lambda e, st=st, ft=ft: e.tensor_copy(out=wt[:, ft, :], in_=st[:, 0:ncols]),
              r=[("stage", ft % 2)], w=[name])
    return wt


def phase_front(nc, T, blocks, VEC, layer, w_in_ap, ncols, identb, DSCR, SZ, mode):
    with ExitStack() as es:
        sb = lambda n, s, d=F32: es.enter_context(nc.sbuf_tensor(n, s, d))
        stage = [sb("stg0", [128, 2560]), sb("stg1", [128, 2560])]
        block = es.enter_context(nc.Block())
        rep = {}
        for n in range(2):
            for which, nm in ((0, "gs"), (1, "sh")):
                t = sb("rep_%s%d" % (nm, n), [128, D])
                v = layer * 6 + n * 3 + which
                T.dma(t[:], VEC[v, :].partition_broadcast(128), w=[("rep", nm, n)], stream="rep%s%d" % (nm, n))
                rep[(nm, n)] = t
        wt = load_cast_weight(nc, T, es, "w_in_bf", w_in_ap, ncols, stage)
        xh = [sb("xh0", [128, 4, D]), sb("xh1", [128, 4, D])]
        junk = sb("junk", [128, D], BF16)
        tmp32 = [sb("tmp32a", [128, D]), sb("tmp32b", [128, D])]
        ss = sb("ss", [128, 8]); ms = sb("ms", [128, 8]); rstd = sb("rstd", [128, 8])
        h = sb("h", [128, 8, D], BF16)
        hT = sb("hT", [128, 8, 8, 128], BF16)
        ucat = sb("ucat", [128, 64, 8, 16], BF16)
        sz = sb("sz", [128, 8, D], BF16)
        dst = [sb("dst0", [128, 8, 128], BF16), sb("dst1", [128, 8, 128], BF16)]
        psT = [es.enter_context(nc.psum_tensor("psT%d" % i, [128, 8, 128], BF16)) for i in range(2)]
        psU = [es.enter_context(nc.psum_tensor("psU%d" % i, [128, 512], F32)) for i in range(4)]
        psD = [es.enter_context(nc.psum_tensor("psD%d" % i, [128, 8, 128], BF16)) for i in range(2)]
        evac_i = [0]
        ucnt = [0]
        for (nk, xap, kg0, row0, is_ctx, bidx) in blocks:
            n = 1 if is_ctx else 0
            xv = xap.rearrange("(k s) f -> k s f", s=8)
            T.add("dve", lambda e: e.memset(ss[:], 0.0), w=["ss"])
            for half in range(2):
                T.dma(xh[half][:nk], xv[:, half * 4:(half + 1) * 4, :], w=[("xh", half)], stream="xh%d" % half)
                for s4 in range(4):
                    s = half * 4 + s4
                    T.add("act", lambda e, half=half, s4=s4, s=s, nk=nk: e.activation(
                        out=junk[:nk], in_=xh[half][:nk, s4, :], func=AF.Square, accum_out=ss[:nk, s:s + 1]),
                        r=[("xh", half)], w=["junk", "ss"])
            T.add("dve", lambda e, nk=nk: e.tensor_scalar(out=ms[:nk], in0=ss[:nk], scalar1=1.0 / D, scalar2=EPS,
                                                          op0=ALU.mult, op1=ALU.add), r=["ss"], w=["ms"])
            T.add("act", lambda e, nk=nk: e.sqrt(out=ms[:nk], in_=ms[:nk]), r=["ms"], w=["ms"])
            T.add("dve", lambda e, nk=nk: e.reciprocal(out=rstd[:nk], in_=ms[:nk]), r=["ms"], w=["rstd"])
            for s in range(8):
                half, s4 = divmod(s, 4)
                tm = tmp32[s % 2]
                T.add("dve", lambda e, half=half, s4=s4, s=s, nk=nk, tm=tm, n=n: e.scalar_tensor_tensor(
                    out=tm[:nk], in0=xh[half][:nk, s4, :], scalar=rstd[:nk, s:s + 1], in1=rep[("gs", n)][:nk],
                    op0=ALU.mult, op1=ALU.mult), r=[("xh", half), "rstd", ("rep", "gs", n)], w=[("tmp32", s % 2)])
                T.add("pool", lambda e, s=s, nk=nk, tm=tm, n=n: e.tensor_tensor(
                    out=h[:nk, s, :], in0=tm[:nk], in1=rep[("sh", n)][:nk], op=ALU.add),
                    r=[("tmp32", s % 2), ("rep", "sh", n)], w=[("h", s)])
            for ft in range(8):
                pt = psT[ft % 2]
                for s in range(8):
                    T.add("pe", lambda e, pt=pt, s=s, ft=ft, nk=nk: e.transpose(
                        out=pt[:, s, :nk], in_=h[:nk, s, ft * 128:(ft + 1) * 128], identity=identb[:nk, :nk]),
                        r=[("h", s), "identb"], w=[("psT", ft % 2)])
                eng = "act" if (evac_i[0] % 2 == 0) else "dve"
                evac_i[0] += 1
                if eng == "act":
                    T.add("act", lambda e, pt=pt, ft=ft, nk=nk: e.copy(out=hT[:, ft, :, :nk], in_=pt[:, :, :nk]),
                          r=[("psT", ft % 2)], w=[("hT", ft)])
                else:
                    T.add("dve", lambda e, pt=pt, ft=ft, nk=nk: e.tensor_copy(out=hT[:, ft, :, :nk], in_=pt[:, :, :nk]),
                          r=[("psT", ft % 2)], w=[("hT", ft)])
            for s in range(8):
                for nb in range(4):
                    pu = psU[ucnt[0] % 4]
                    pkey = ("psU", ucnt[0] % 4)
                    ucnt[0] += 1
                    for ft in range(8):
                        T.add("pe", lambda e, pu=pu, s=s, nb=nb, ft=ft, nk=nk: e.matmul(
                            pu[:nk, :], lhsT=hT[:, ft, s, :nk], rhs=wt[:, ft, nb * 512:(nb + 1) * 512],
                            start=(ft == 0), stop=(ft == 7)),
                            r=[("hT", ft), "w_in_bf"], w=[pkey])
                    if nb < 2:
                        T.add("dve", lambda e, pu=pu, s=s, nb=nb, nk=nk: e.tensor_copy(
                            out=ucat[:nk, nb * 32:(nb + 1) * 32, s, :],
                            in_=pu[:nk, :].rearrange("k (g h) -> k g h", h=16)),
                            r=[pkey], w=["ucat"])
                    else:
                        T.add("act", lambda e, pu=pu, s=s, nb=nb, nk=nk: e.activation(
                            out=sz[:nk, s, (nb - 2) * 512:(nb - 1) * 512], in_=pu[:nk, :], func=AF.Silu),
                            r=[pkey], w=["sz"])
            T.dma(SZ[row0:row0 + nk * 8, :].rearrange("(k s) f -> k s f", s=8), sz[:nk], r=["sz"], w=["SZ"],
                  stream="szst", eng="pool")
            for gq in range(8):
                pd = psD[gq % 2]
                for gi in range(8):
                    g = gq * 8 + gi
                    T.add("pe", lambda e, pd=pd, gi=gi, g=g, nk=nk: e.transpose(
                        out=pd[:, gi, :nk], in_=ucat[:nk, g, :, :].rearrange("k s h -> k (s h)"),
                        identity=identb[:nk, :nk]),
                        r=["ucat", "identb"], w=[("psD", gq % 2)])
                T.add("dve", lambda e, pd=pd, gq=gq, nk=nk: e.tensor_copy(out=dst[gq % 2][:, :, :nk], in_=pd[:, :, :nk]),
                      r=[("psD", gq % 2)], w=[("dst", gq % 2)])
                T.dma(DSCR[gq * 8:(gq + 1) * 8, :, kg0:kg0 + nk].rearrange("g p k -> p g k"), dst[gq % 2][:, :, :nk],
                      r=[("dst", gq % 2)], w=["DSCR"], stream="dscr%d" % (gq % 2), eng="pool")
        T.emit(block)


def phase_s5(nc, T, P, identf, DSCR, YSC):
    PI2 = TWO_PI
    with ExitStack() as es:
        sb = lambda n, s, d=F32: es.enter_context(nc.sbuf_tensor("s5_" + n, s, d))
        es1 = ExitStack()
        sbt = lambda n, s, d=F32: es1.enter_context(nc.sbuf_tensor("s5t_" + n, s, d))
        kio = sb("kio", [128, NK]); jv = sb("jv", [128, 16])
        maskf = sb("maskf", [128, 128]); maskb = sb("maskb", [128, 128])
        phi = sb("phi", [128, 128]); phi2pi = sb("phi2pi", [128, 128]); rho = sb("rho", [128, 128])
        ctr = sb("ctr", [128, 128, 16]); cti = sb("cti", [128, 128, 16])
        ejr = sb("ejr", [128, 16, 128]); eji = sb("eji", [128, 16, 128])
        bbr = sb("bbr", [128, 128, 16]); bbi = sb("bbi", [128, 128, 16])
        dcol = sb("dcol", [128, 64])
        sm5 = sb("sm5", [128, 128])
        an_r = sbt("an_r", [64, 2, 2, 64]); an_i = sbt("an_i", [64, 2, 2, 64])
        are = sbt("are", [128, 128]); aim = sbt("aim", [128, 128]); ldt = sbt("ldt", [128, 128])
        dtt = sbt("dtt", [128, 128]); alpha = sbt("alpha", [128, 128]); theta = sbt("theta", [128, 128])
        btr = sbt("btr", [128, 128, 16]); bti = sbt("bti", [128, 128, 16])
        cn_r = sbt("cn_r", [128, 8, 2, 2, 64]); cn_i = sbt("cn_i", [128, 8, 2, 2, 64])
        tmpA = sbt("tmpA", [128, 16, 128]); tmpB = sbt("tmpB", [128, 16, 128]); tmpC = sbt("tmpC", [128, 16, 128])
        tni = sbt("tni", [128, 16, 128], I32)
        sm = [sbt("sm%d" % i, [128, 128]) for i in range(5)] + [sm5]
        pss = es.enter_context(nc.psum_tensor("pss", [128, 4, 128], F32))
        psZA = es.enter_context(nc.psum_tensor("psZA", [128, 2, 512], F32))
        psZB = es.enter_context(nc.psum_tensor("psZB", [128, 2, 512], F32))
        psZc = es.enter_context(nc.psum_tensor("psZc", [128, 512], F32))
        psY = [es.enter_context(nc.psum_tensor("psY%d" % i, [128, 4, 128], F32)) for i in range(2)]
        block = es1.enter_context(nc.Block())
        A = T.add
        A("pool", lambda e: e.memset(sm[5][:], float(np.pi / 2)), w=["sm5"])
        T.dma(kio[:], P["kio"][:, :], w=["kio"], stream="p0")
        T.dma(jv[:], P["jv"][:, :], w=["jv"], stream="p1")
        T.dma(maskf[:], P["maskf"][:, :], w=["maskf"], stream="p2")
        T.dma(maskb[:], P["maskb"][:, :], w=["maskb"], stream="p3")
        for c2 in range(2):
            T.dma(an_r[:, :, c2, :], P["a_re"].rearrange("d g p -> g d p"), w=["an_r"], stream="p4")
            T.dma(an_i[:, :, c2, :], P["a_im"].rearrange("d g p -> g d p"), w=["an_i"], stream="p5")
        T.dma(ldt[:], P["log_dt"].rearrange("d g -> (d g)").partition_broadcast(128), w=["ldt"], stream="p6")
        for c2 in range(2):
            for d in range(2):
                for gq in range(4):
                    T.dma(btr[c2 * 64:(c2 + 1) * 64, d * 64 + gq * 16:d * 64 + gq * 16 + 16, :],
                          P["b_re"][d, gq * 16:(gq + 1) * 16].rearrange("g p h -> p g h"), w=["btr"], stream="p7")
                    T.dma(bti[c2 * 64:(c2 + 1) * 64, d * 64 + gq * 16:d * 64 + gq * 16 + 16, :],
                          P["b_im"][d, gq * 16:(gq + 1) * 16].rearrange("g p h -> p g h"), w=["bti"], stream="p8")
            for d in range(2):
                T.dma(cn_r[:, :, d, c2, :], P["c_re"][d].rearrange("(gb g8) h p -> (g8 h) gb p", g8=8),
                      w=["cn_r"], stream="p9")
                T.dma(cn_i[:, :, d, c2, :], P["c_im"][d].rearrange("(gb g8) h p -> (g8 h) gb p", g8=8),
                      w=["cn_i"], stream="p10")
        for s8 in range(8):
            T.dma(dcol[s8 * 16:(s8 + 1) * 16, :], P["ssm_d"].rearrange("(g h) -> h g", h=16), w=["dcol"],
                  stream="p11", allow_slow_non_contiguous=True)
        for (src, dstt, nm) in ((an_r, are, "are"), (an_i, aim, "aim")):
            for d in range(2):
                A("pe", lambda e, src=src, d=d: e.transpose(
                    out=pss[:, d, 0:64], in_=src[:, d, :, :].rearrange("g c p -> g (c p)"), identity=identf[0:64, 0:64]),
                    r=[nm.replace("a", "an_", 1) if False else ("an_r" if nm == "are" else "an_i"), "identf"], w=["pss"])
            A("dve", lambda e, dstt=dstt: e.tensor_copy(out=dstt[:].rearrange("p (d g) -> p d g", d=2), in_=pss[:, 0:2, 0:64]),
              r=["pss"], w=[nm])
        for (src, dstt, nm, snm) in ((cn_r, ctr, "ctr", "cn_r"), (cn_i, cti, "cti", "cn_i")):
            for d in range(2):
                for gq in range(2):
                    for g4 in range(4):
                        gb = gq * 4 + g4
                        A("pe", lambda e, src=src, d=d, gb=gb, g4=g4: e.transpose(
                            out=pss[:, g4, :], in_=src[:, gb, d, :, :].rearrange("q c p -> q (c p)"), identity=identf[:]),
                            r=[snm, "identf"], w=["pss"])
                    A("dve", lambda e, dstt=dstt, d=d, gq=gq: e.tensor_copy(
                        out=dstt[:, d * 64 + gq * 32:d * 64 + gq * 32 + 32, :].rearrange("p (a b) h -> p a b h", a=4),
                        in_=pss[:].rearrange("p a (b h) -> p a b h", h=16)),
                        r=["pss"], w=[nm])
        A("act", lambda e: e.activation(out=dtt[:], in_=ldt[:], func=AF.Exp), r=["ldt"], w=["dtt"])
        A("dve", lambda e: e.tensor_tensor(out=alpha[:], in0=are[:], in1=dtt[:], op=ALU.mult), r=["are", "dtt"], w=["alpha"])
        A("dve", lambda e: e.tensor_tensor(out=theta[:], in0=aim[:], in1=dtt[:], op=ALU.mult), r=["aim", "dtt"], w=["theta"])
        A("dve", lambda e: e.tensor_scalar(out=phi[:], in0=theta[:], scalar1=8.0, scalar2=0.0, op0=ALU.mult, op1=ALU.add),
          r=["theta"], w=["phi"])
        A("dve", lambda e: e.tensor_scalar(out=phi2pi[:], in0=theta[:], scalar1=8.0 / PI2, scalar2=0.0, op0=ALU.mult,
                                           op1=ALU.add), r=["theta"], w=["phi2pi"])
        A("act", lambda e: e.activation(out=rho[:], in_=alpha[:], func=AF.Exp, scale=8.0), r=["alpha"], w=["rho"])
        th_b = bc(theta[:].unsqueeze(1), [128, 16, 128]); al_b = bc(alpha[:].unsqueeze(1), [128, 16, 128])
        jv_b = bc(jv[:].unsqueeze(2), [128, 16, 128])
        A("dve", lambda e: e.tensor_tensor(out=tmpA[:], in0=th_b, in1=jv_b, op=ALU.mult), r=["theta", "jv"], w=["tmpA"])
        A("dve", lambda e: e.tensor_scalar(out=tni[:], in0=tmpA[:], scalar1=1.0 / PI2, scalar2=0.0, op0=ALU.mult,
                                           op1=ALU.add), r=["tmpA"], w=["tni"])
        A("dve", lambda e: e.scalar_tensor_tensor(out=tmpB[:], in0=tni[:], scalar=-PI2, in1=tmpA[:], op0=ALU.mult,
                                                  op1=ALU.add), r=["tni", "tmpA"], w=["tmpB"])
        A("dve", lambda e: e.tensor_scalar(out=tmpB[:], in0=tmpB[:], scalar1=-PI_SAFE, scalar2=PI_SAFE, op0=ALU.max,
                                           op1=ALU.min), r=["tmpB"], w=["tmpB"])
        A("dve", lambda e: e.scalar_tensor_tensor(out=tmpA[:], in0=tmpB[:], scalar=-1.0, in1=tmpB[:], op0=ALU.mult, op1=ALU.max),
          r=["tmpB"], w=["tmpA"])
        A("act", lambda e: e.activation(out=eji[:], in_=tmpB[:], func=AF.Sin), r=["tmpB"], w=["eji"])
        A("act", lambda e: e.activation(out=ejr[:], in_=tmpA[:], func=AF.Sin, scale=-1.0, bias=sm[5][:, 0:1]),
          r=["tmpA", "sm5"], w=["ejr"])
        A("dve", lambda e: e.tensor_tensor(out=tmpC[:], in0=al_b, in1=jv_b, op=ALU.mult), r=["alpha", "jv"], w=["tmpC"])
        A("act", lambda e: e.activation(out=tmpC[:], in_=tmpC[:], func=AF.Exp), r=["tmpC"], w=["tmpC"])
        A("dve", lambda e: e.tensor_tensor(out=ejr[:], in0=ejr[:], in1=tmpC[:], op=ALU.mult), r=["ejr", "tmpC"], w=["ejr"])
        A("dve", lambda e: e.tensor_tensor(out=eji[:], in0=eji[:], in1=tmpC[:], op=ALU.mult), r=["eji", "tmpC"], w=["eji"])
        e1r, e1i = ejr[:, 8, :], eji[:, 8, :]
        A("dve", lambda e: e.tensor_scalar(out=sm[0][:], in0=e1r, scalar1=-1.0, scalar2=0.0, op0=ALU.add, op1=ALU.add),
          r=["ejr"], w=["sm0"])
        A("dve", lambda e: e.tensor_tensor(out=sm[1][:], in0=sm[0][:], in1=are[:], op=ALU.mult), r=["sm0", "are"], w=["sm1"])
        A("dve", lambda e: e.tensor_tensor(out=sm[2][:], in0=e1i, in1=aim[:], op=ALU.mult), r=["eji", "aim"], w=["sm2"])
        A("dve", lambda e: e.tensor_tensor(out=sm[1][:], in0=sm[1][:], in1=sm[2][:], op=ALU.add), r=["sm1", "sm2"], w=["sm1"])
        A("dve", lambda e: e.tensor_tensor(out=sm[2][:], in0=e1i, in1=are[:], op=ALU.mult), r=["eji", "are", "sm1"], w=["sm2"])
        A("dve", lambda e: e.tensor_tensor(out=sm[3][:], in0=sm[0][:], in1=aim[:], op=ALU.mult), r=["sm0", "aim"], w=["sm3"])
        A("dve", lambda e: e.tensor_tensor(out=sm[2][:], in0=sm[2][:], in1=sm[3][:], op=ALU.subtract), r=["sm2", "sm3"], w=["sm2"])
        A("dve", lambda e: e.tensor_tensor(out=sm[3][:], in0=are[:], in1=are[:], op=ALU.mult), r=["are", "sm2"], w=["sm3"])
        A("dve", lambda e: e.tensor_tensor(out=sm[4][:], in0=aim[:], in1=aim[:], op=ALU.mult), r=["aim"], w=["sm4"])
        A("dve", lambda e: e.tensor_tensor(out=sm[3][:], in0=sm[3][:], in1=sm[4][:], op=ALU.add), r=["sm3", "sm4"], w=["sm3"])
        A("dve", lambda e: e.reciprocal(out=sm[3][:], in_=sm[3][:]), r=["sm3"], w=["sm3"])
        A("dve", lambda e: e.tensor_tensor(out=sm[1][:], in0=sm[1][:], in1=sm[3][:], op=ALU.mult), r=["sm1", "sm3"], w=["sm1"])
        A("dve", lambda e: e.tensor_tensor(out=sm[2][:], in0=sm[2][:], in1=sm[3][:], op=ALU.mult), r=["sm2", "sm3"], w=["sm2"])
        br_b = bc(sm[1][:].unsqueeze(2), [128, 128, 16]); bi_b = bc(sm[2][:].unsqueeze(2), [128, 128, 16])
        tA3 = tmpA[:].rearrange("p a b -> p b a"); tB3 = tmpB[:].rearrange("p a b -> p b a")
        A("dve", lambda e: e.tensor_tensor(out=tA3, in0=br_b, in1=btr[:], op=ALU.mult), r=["sm1", "btr", "ejr", "eji"], w=["tmpA"])
        A("dve", lambda e: e.tensor_tensor(out=tB3, in0=bi_b, in1=bti[:], op=ALU.mult), r=["sm2", "bti", "ejr", "eji"], w=["tmpB"])
        A("dve", lambda e: e.tensor_tensor(out=bbr[:], in0=tA3, in1=tB3, op=ALU.subtract), r=["tmpA", "tmpB"], w=["bbr"])
        A("dve", lambda e: e.tensor_tensor(out=tA3, in0=br_b, in1=bti[:], op=ALU.mult), r=["sm1", "bti", "bbr"], w=["tmpA"])
        A("dve", lambda e: e.tensor_tensor(out=tB3, in0=bi_b, in1=btr[:], op=ALU.mult), r=["sm2", "btr", "bbr"], w=["tmpB"])
        A("dve", lambda e: e.tensor_tensor(out=bbi[:], in0=tA3, in1=tB3, op=ALU.add), r=["tmpA", "tmpB"], w=["bbi"])
        T.emit(block)
        es1.close()
        if STOP_S5_SETUP[0]:
            return
        block = es.enter_context(nc.Block())
        s5_groups(nc, T, es, sb, locals())
        T.emit(block)


def s5_groups(nc, T, es, sb, V):
    A = T.add
    kio, identf, maskf, maskb, dcol = V["kio"], V["identf"], V["maskf"], V["maskb"], V["dcol"]
    ejr, eji, bbr, bbi, ctr, cti = V["ejr"], V["eji"], V["bbr"], V["bbi"], V["ctr"], V["cti"]
    phi, phi2pi, rho = V["phi"], V["phi2pi"], V["rho"]
    pss, psZA, psZB, psZc, psY = V["pss"], V["psZA"], V["psZB"], V["psZc"], V["psY"]
    DSCR, YSC, sm = V["DSCR"], V["YSC"], V["sm"]
    PI2 = TWO_PI
    ga = sb("ga", [128, 2, 8, 128]); hm = sb("hm", [128, 2, 8, 128])
    t1 = sb("gt1", [128, 8, 128]); t2 = sb("gt2", [128, 8, 128])
    winjA = sb("winjA", [128, 2, 8, 128], BF16); winjB = sb("winjB", [128, 2, 8, 128], BF16)
    wA = sb("wA", [128, 2, 8, 128], BF16); wB = sb("wB", [128, 2, 8, 128], BF16)
    toep = sb("toep", [128, 8, 128], BF16)
    tt1 = sb("tt1", [128, 4, 128]); tt2 = sb("tt2", [128, 4, 128])
    dg = [sb("dg0", [128, NK], BF16), sb("dg1", [128, NK], BF16)]
    ni = [sb("ni0", [128, NK], I32)] * 2; ang = [sb("ang0", [128, NK])] * 2
    rr = [sb("rr0", [128, NK])] * 2; ra = [sb("ra0", [128, NK])] * 2
    ts4 = [sb("ts%d" % i, [128, NK]) for i in range(4)]; tc4 = [sb("tc%d" % i, [128, NK]) for i in range(4)]
    m1 = [sb("m1%d" % d, [128, NK]) for d in range(2)]; m2 = [sb("m20", [128, NK])] * 2
    q = [sb("q%d" % d, [128, NK]) for d in range(2)]
    FC = [sb("FC%d" % d, [128, NK + 1], BF16) for d in range(2)]
    FS = [sb("FS%d" % d, [128, NK + 1], BF16) for d in range(2)]
    ytile = [sb("ytile0", [128, 9, 8, 128], BF16)] * 2
    halfpi = sm[5]
    for d in range(2):
        A("pool", lambda e, d=d: e.memset(FC[d][:], 0.0), w=[("FC", d)])
        A("pool", lambda e, d=d: e.memset(FS[d][:], 0.0), w=[("FS", d)])

    def eview(t, lo, hi, rev, d, gb, parts):
        sl = t[parts, lo:hi, d * 64 + gb * 8:d * 64 + gb * 8 + 8]
        if rev:
            sl = t[parts, hi - 1:(lo - 1 if lo > 0 else None):-1, d * 64 + gb * 8:d * 64 + gb * 8 + 8]
        return bc(sl.rearrange("p s g -> p g s").unsqueeze(3), [64, 8, 8, 16])

    def gview(t, d, gb, parts):
        return bc(t[parts, d * 64 + gb * 8:d * 64 + gb * 8 + 8, :].unsqueeze(2), [64, 8, 8, 16])

    def cplx(eng, outv, parts, Er, Ei, Xr, Xi, kind, rk, wk):
        a = t1[parts].rearrange("p g (s h) -> p g s h", h=16); b = t2[parts].rearrange("p g (s h) -> p g s h", h=16)
        pk = "lo" if parts.start == 0 else "hi"
        X1, X2 = (Xr, Xi) if kind in ("re", "nre") else (Xi, Xr)
        A(eng, lambda e: e.tensor_tensor(out=a, in0=Er, in1=X1, op=ALU.mult), r=rk, w=[("t1", pk)])
        A(eng, lambda e: e.tensor_tensor(out=b, in0=Ei, in1=X2, op=ALU.mult), r=rk, w=[("t2", pk)])
        if kind == "re":
            A(eng, lambda e: e.tensor_tensor(out=outv, in0=a, in1=b, op=ALU.subtract), r=[("t1", pk), ("t2", pk)], w=wk)
        elif kind == "im":
            A(eng, lambda e: e.tensor_tensor(out=outv, in0=a, in1=b, op=ALU.add), r=[("t1", pk), ("t2", pk)], w=wk)
        elif kind == "nre":
            A(eng, lambda e: e.tensor_tensor(out=outv, in0=b, in1=a, op=ALU.subtract), r=[("t1", pk), ("t2", pk)], w=wk)
        else:
            A(eng, lambda e: e.tensor_tensor(out=a, in0=a, in1=b, op=ALU.add), r=[("t1", pk), ("t2", pk)], w=[("t1", pk)])
            A(eng, lambda e: e.tensor_scalar(out=outv, in0=a, scalar1=-1.0, scalar2=0.0, op0=ALU.mult, op1=ALU.add),
              r=[("t1", pk)], w=wk)

    lo, hi = slice(0, 64), slice(64, 128)
    pkeys = ["ejr", "eji", "bbr", "bbi", "ctr", "cti"]
    yt = ytile[0]
    ykey = ("ytile", 0)

    def gen_weights(gb):
        for d in range(2):
            args = (7, 15, d == 0, d, gb)
            for parts, kind, eng in ((lo, "re", "dve"), (hi, "im", "pool")):
                ov = ga[parts, d].rearrange("p g (s h) -> p g s h", h=16)
                cplx(eng, ov, parts, eview(ejr, *args, parts), eview(eji, *args, parts),
                     gview(bbr, d, gb, parts), gview(bbi, d, gb, parts), kind, pkeys, [("ga", d)])
            args = (0, 8, d == 1, d, gb)
            for parts, kind, eng in ((lo, "re", "dve"), (hi, "nim", "pool")):
                ov = hm[parts, d].rearrange("p g (s h) -> p g s h", h=16)
                cplx(eng, ov, parts, eview(ejr, *args, parts), eview(eji, *args, parts),
                     gview(ctr, d, gb, parts), gview(cti, d, gb, parts), kind, pkeys, [("hm", d)])
            args = (8, 16, d == 1, d, gb)
            for (wt_, wn, kinds) in ((wA, "wA", ("re", "nim")), (wB, "wB", ("nim", "nre"))):
                for parts, kind, eng in ((lo, kinds[0], "dve"), (hi, kinds[1], "pool")):
                    ov = wt_[parts, d].rearrange("p g (s h) -> p g s h", h=16)
                    cplx(eng, ov, parts, eview(ejr, *args, parts), eview(eji, *args, parts),
                         gview(ctr, d, gb, parts), gview(cti, d, gb, parts), kind, pkeys + ["Ymm"], [(wn, d)])
        for d in range(2):
            for gq in range(2):
                for g4 in range(4):
                    g8 = gq * 4 + g4
                    A("pe", lambda e, d=d, g8=g8, g4=g4: e.transpose(out=pss[:, g4, :], in_=ga[:, d, g8, :], identity=identf[:]),
                      r=[("ga", d), "identf"], w=["pss"])
                A("dve", lambda e, d=d, gq=gq: e.tensor_copy(out=winjA[:, d, gq * 4:gq * 4 + 4, :], in_=pss[:]),
                  r=["pss", "Zmm"], w=["winjA"])
                A("dve", lambda e, d=d, gq=gq: e.tensor_copy(out=winjB[:, d, gq * 4:gq * 4 + 4, 0:64], in_=pss[:, :, 64:128]),
                  r=["pss", "Zmm"], w=["winjB"])
                A("dve", lambda e, d=d, gq=gq: e.tensor_scalar(out=winjB[:, d, gq * 4:gq * 4 + 4, 64:128], in0=pss[:, :, 0:64],
                                                               scalar1=-1.0, scalar2=0.0, op0=ALU.mult, op1=ALU.add),
                  r=["pss", "Zmm"], w=["winjB"])
        for gq in range(2):
            for d in range(2):
                for g4 in range(4):
                    g8 = gq * 4 + g4
                    A("pe", lambda e, d=d, g8=g8, g4=g4: e.matmul(pss[:, g4, :], lhsT=ga[:, d, g8, :], rhs=hm[:, d, g8, :],
                                                                 start=True, stop=True),
                      r=[("ga", d), ("hm", d)], w=["pss"])
                mk = maskf if d == 0 else maskb
                tt = tt1 if d == 0 else tt2
                A("dve", lambda e, mk=mk, tt=tt: e.tensor_tensor(out=tt[:], in0=pss[:], in1=bc(mk[:].unsqueeze(1), [128, 4, 128]),
                                                                op=ALU.mult), r=["pss", "maskf", "maskb"], w=["tt%d" % d])
            A("dve", lambda e: e.tensor_tensor(out=tt1[:], in0=tt1[:], in1=tt2[:], op=ALU.add), r=["tt0", "tt1"], w=["tt0"])
            for g4 in range(4):
                g8 = gq * 4 + g4
                g = gb * 8 + g8
                A("dve", lambda e, g4=g4, g8=g8, g=g: e.scalar_tensor_tensor(
                    out=toep[:, g8, :], in0=identf[:], scalar=dcol[:, g:g + 1], in1=tt1[:, g4, :], op0=ALU.mult, op1=ALU.add),
                    r=["tt0", "dcol", "identf", "Ymm"], w=["toep"])
    def tables(g, d):
        col = d * 64 + g
        ti = (g % 2) * 2 + d
        ts = {d: ts4[ti]}; tc = {d: tc4[ti]}
        A("dve", lambda e: e.tensor_scalar(out=ni[d][:], in0=kio[:], scalar1=phi2pi[:, col:col + 1], scalar2=0.0,
                                           op0=ALU.mult, op1=ALU.add), r=["kio", "phi2pi"], w=[("ni", 0)])
        A("act", lambda e: e.activation(out=ang[d][:], in_=kio[:], func=AF.Copy, scale=phi[:, col:col + 1]),
          r=["kio", "phi"], w=[("ang", 0)])
        A("dve", lambda e: e.scalar_tensor_tensor(out=rr[d][:], in0=ni[d][:], scalar=-PI2, in1=ang[d][:], op0=ALU.mult,
                                                  op1=ALU.add), r=[("ni", 0), ("ang", 0)], w=[("rr", 0)])
        A("dve", lambda e: e.tensor_scalar(out=rr[d][:], in0=rr[d][:], scalar1=-PI_SAFE, scalar2=PI_SAFE, op0=ALU.max,
                                           op1=ALU.min), r=[("rr", 0)], w=[("rr", 0)])
        A("act", lambda e: e.activation(out=ra[d][:], in_=rr[d][:], func=AF.Abs), r=[("rr", 0)], w=[("ra", 0)])
        A("act", lambda e: e.activation(out=ts[d][:], in_=rr[d][:], func=AF.Sin), r=[("rr", 0)], w=[("ts", ti)])
        A("act", lambda e: e.activation(out=tc[d][:], in_=ra[d][:], func=AF.Sin, scale=-1.0, bias=halfpi[:, 0:1]),
          r=[("ra", 0), "sm5"], w=[("tc", ti)])

    def core(g, g8, d, dgt, dkey):
        col = d * 64 + g
        ti = (g % 2) * 2 + d
        ts = {d: ts4[ti]}; tc = {d: tc4[ti]}
        for (pz, wj, zk, wk_) in ((psZA, winjA, "psZA", "winjA"), (psZB, winjB, "psZB", "winjB")):
            for hf in range(2):
                A("pe", lambda e, pz=pz, wj=wj, hf=hf: e.matmul(
                    pz[:, hf, :], lhsT=wj[:, d, g8, :], rhs=dgt[:, NKC + 512 * hf:NKC + 512 * (hf + 1)],
                    start=True, stop=True), r=[dkey, wk_], w=[zk, "Zmm"])
        A("pe", lambda e: e.matmul(psZc[:, 0:32], lhsT=winjA[:, d, g8, :], rhs=dgt[:, 0:NKC], start=True, stop=True),
          r=[dkey, "winjA"], w=["psZc", "Zmm"])
        A("pe", lambda e: e.matmul(psZc[:, 32:64], lhsT=winjB[:, d, g8, :], rhs=dgt[:, 0:NKC], start=True, stop=True),
          r=[dkey, "winjB"], w=["psZc", "Zmm"])
        zaf = psZA[:].rearrange("p a b -> p (a b)"); zbf = psZB[:].rearrange("p a b -> p (a b)")
        if d == 0:
            segs = [(slice(0, NKC), psZc[:, 0:32], psZc[:, 32:64]), (slice(NKC, NK), zaf, zbf)]
        else:
            segs = [(slice(0, NKC), psZc[:, 31::-1], psZc[:, 63:31:-1]), (slice(NKC, NK), zaf[:, ::-1], zbf[:, ::-1])]
        for (js, za, zb) in segs:
            A("dve", lambda e, js=js, za=za: e.tensor_tensor(out=m1[d][:, js], in0=za, in1=tc[d][:, js], op=ALU.mult),
              r=["psZA", "psZc", ("tc", ti)], w=[("m1", d)])
            A("dve", lambda e, js=js, zb=zb: e.tensor_tensor(out=m2[d][:, js], in0=zb, in1=ts[d][:, js], op=ALU.mult),
              r=["psZB", "psZc", ("ts", ti)], w=[("m2", 0)])
        A("dve", lambda e: e.tensor_tensor(out=m1[d][:], in0=m1[d][:], in1=m2[d][:], op=ALU.add), r=[("m1", d), ("m2", 0)],
          w=[("m1", d)])

    def core2(g, g8, d):
        col = d * 64 + g
        ti = (g % 2) * 2 + d
        ts = {d: ts4[ti]}; tc = {d: tc4[ti]}
        A("dve", lambda e: e.tensor_tensor_scan(out=q[d][:], data0=bc(rho[:, col:col + 1], [128, NK]), data1=m1[d][:],
                                                initial=0.0, op0=ALU.mult, op1=ALU.add), r=[("m1", d), "rho"], w=[("q", d)])
        for (Fb, tab, fk, tk) in ((FC[d], tc[d], ("FC", d), ("tc", ti)), (FS[d], ts[d], ("FS", d), ("ts", ti))):
            if d == 0:
                A("pool", lambda e, Fb=Fb, tab=tab: e.tensor_tensor(out=Fb[:, 1:NK + 1], in0=q[d][:], in1=tab[:], op=ALU.mult),
                  r=[("q", d), tk], w=[fk])
            else:
                A("pool", lambda e, Fb=Fb, tab=tab: e.tensor_tensor(out=Fb[:, 31:0:-1], in0=q[d][:, 0:31], in1=tab[:, 0:31],
                                                                   op=ALU.mult), r=[("q", d), tk], w=[fk])
                A("pool", lambda e, Fb=Fb, tab=tab: e.tensor_tensor(out=Fb[:, NK - 1:32:-1], in0=q[d][:, 32:NK - 1],
                                                                   in1=tab[:, 32:NK - 1], op=ALU.mult), r=[("q", d), tk], w=[fk])
                A("pool", lambda e, Fb=Fb, tab=tab: e.tensor_tensor(out=Fb[:, NK:NK + 1], in0=q[d][:, 31:32], in1=tab[:, 31:32],
                                                                   op=ALU.mult), r=[("q", d), tk], w=[fk])

    def outputs(g, g8, dgt, dkey):
        rounds = [(0, [(0, NKC, 0)] + [(1 + i, 128, NKC + 128 * i) for i in range(3)]),
                  (1, [(4 + i, 128, NKC + 128 * (3 + i)) for i in range(4)]),
                  (0, [(8, 128, NKC + 128 * 7)])]
        for (pb, blks) in rounds:
            py = psY[pb]
            for slot, (kb, nk, p0) in enumerate(blks):
                ops = [(dgt[:, p0:p0 + nk], toep[:, g8, :], [dkey, "toep"]),
                       (FC[0][:, p0:p0 + nk], wA[:, 0, g8, :], [("FC", 0), ("wA", 0)]),
                       (FS[0][:, p0:p0 + nk], wB[:, 0, g8, :], [("FS", 0), ("wB", 0)]),
                       (FC[1][:, p0 + 1:p0 + 1 + nk], wA[:, 1, g8, :], [("FC", 1), ("wA", 1)]),
                       (FS[1][:, p0 + 1:p0 + 1 + nk], wB[:, 1, g8, :], [("FS", 1), ("wB", 1)])]
                for oi, (lt, rh, rk) in enumerate(ops):
                    A("pe", lambda e, py=py, slot=slot, nk=nk, lt=lt, rh=rh, oi=oi: e.matmul(
                        py[:nk, slot, :], lhsT=lt, rhs=rh, start=(oi == 0), stop=(oi == 4)),
                        r=rk, w=[("psY", pb), "Ymm"])
            if blks[0][1] == NKC:
                A("act", lambda e, py=py, yt=yt, g8=g8: e.activation(
                    out=yt[:NKC, 0, :, g8 * 16:(g8 + 1) * 16], in_=py[:NKC, 0, :].rearrange("k (t h) -> k t h", h=16),
                    func=AF.Gelu), r=[("psY", pb)], w=[ykey])
                rest = blks[1:]
                s0 = 1
            else:
                rest = blks
                s0 = 0
            kb0 = rest[0][0]
            nb = len(rest)
            A("act", lambda e, py=py, yt=yt, g8=g8, kb0=kb0, nb=nb, s0=s0: e.activation(
                out=yt[:, kb0:kb0 + nb, :, g8 * 16:(g8 + 1) * 16],
                in_=py[:, s0:s0 + nb, :].rearrange("k b (t h) -> k b t h", h=16), func=AF.Gelu),
                r=[("psY", pb)], w=[ykey])

    def store(gb):
        T.dma(YSC[0:C, gb * 128:(gb + 1) * 128].rearrange("(k t) f -> k t f", t=8), yt[:NKC, 0], r=[ykey], w=["YSC"],
              stream="ysc", eng="pool")
        for kb in range(8):
            T.dma(YSC[C + kb * 1024:C + (kb + 1) * 1024, gb * 128:(gb + 1) * 128].rearrange("(k t) f -> k t f", t=8),
                  yt[:, 1 + kb], r=[ykey], w=["YSC"], stream="ysc", eng="pool")

    def load_dg(g):
        T.dma(dg[g % 2][:], DSCR[g], w=[("dg", g % 2)], stream="dg%d" % (g % 2))

    def stage1(g):
        dgt = dg[g % 2]; dkey = ("dg", g % 2)
        core(g, g % 8, 0, dgt, dkey)
        core(g, g % 8, 1, dgt, dkey)

    gen_weights(0)
    tables(0, 0); tables(0, 1)
    load_dg(0)
    stage1(0)
    for g in range(64):
        g8 = g % 8
        core2(g, g8, 0)
        core2(g, g8, 1)
        last_in_gb = (g8 == 7)
        if g + 1 < 64:
            tables(g + 1, 0); tables(g + 1, 1)
            load_dg(g + 1)
            if not last_in_gb:
                stage1(g + 1)
        outputs(g, g8, dg[g % 2], ("dg", g % 2))
        if last_in_gb:
            store(g // 8)
            if g + 1 < 64:
                gen_weights(g // 8 + 1)
                stage1(g + 1)


def phase_post0(nc, T, blocks, VEC, YSC, SZ, w_glu, b_glu, w_out, identb, X1, CTX1):
    with ExitStack() as es:
        sb = lambda n, s, d=F32: es.enter_context(nc.sbuf_tensor("p0_" + n, s, d))
        stage = [sb("stg0", [128, D]), sb("stg1", [128, D])]
        block = es.enter_context(nc.Block())
        A = T.add
        grep = []
        for n in range(2):
            t = sb("gate%d" % n, [128, D])
            T.dma(t[:], VEC[n * 3 + 2, :].partition_broadcast(128), w=[("gate", n)], stream="gate%d" % n)
            grep.append(t)
        wg = load_cast_weight(nc, T, es, "p0_wglu", w_glu, D, stage)
        wo = load_cast_weight(nc, T, es, "p0_wout", w_out, D, stage)
        bgf = sb("bgf", [1, D]); bgb = sb("bgb", [1, D], BF16); ones = sb("ones", [1, 128], BF16)
        T.dma(bgf[:], b_glu.rearrange("(o f) -> o f", o=1), w=["bgf"], stream="bgf")
        A("dve", lambda e: e.tensor_copy(out=bgb[:], in_=bgf[:]), r=["bgf"], w=["bgb"])
        A("dve", lambda e: e.memset(ones[:], 1.0), w=["ones"])
        yg = sb("yg", [128, 8, D], BF16); szt = sb("szt", [128, 8, D], BF16)
        ygT = sb("ygT", [128, 8, 8, 128], BF16)
        sg = [sb("sg0", [128, D], BF16), sb("sg1", [128, D], BF16)]
        y3 = sb("y3", [128, 8, D], BF16); y3T = sb("y3T", [128, 8, 8, 128], BF16)
        xh = [sb("xh0", [128, 4, D]), sb("xh1", [128, 4, D])]
        tmp = [sb("tmp0", [128, 512]), sb("tmp1", [128, 512])]
        psT = [es.enter_context(nc.psum_tensor("p0psT%d" % i, [128, 8, 128], BF16)) for i in range(2)]
        psU = [es.enter_context(nc.psum_tensor("p0psU%d" % i, [128, 512], F32)) for i in range(4)]
        cnt = {"ev": 0, "u": 0, "tmp": 0}

        def transposes(src, dstT, skey, dkey, nk):
            for ft in range(8):
                pt = psT[ft % 2]
                for t in range(8):
                    A("pe", lambda e, pt=pt, t=t, ft=ft: e.transpose(
                        out=pt[:, t, :nk], in_=src[:nk, t, ft * 128:(ft + 1) * 128], identity=identb[:nk, :nk]),
                        r=[(skey, t), "identb"], w=[("psT", ft % 2)])
                if cnt["ev"] % 2 == 0:
                    A("act", lambda e, pt=pt, ft=ft: e.copy(out=dstT[:, ft, :, :nk], in_=pt[:, :, :nk]),
                      r=[("psT", ft % 2)], w=[(dkey, ft)])
                else:
                    A("dve", lambda e, pt=pt, ft=ft: e.tensor_copy(out=dstT[:, ft, :, :nk], in_=pt[:, :, :nk]),
                      r=[("psT", ft % 2)], w=[(dkey, ft)])
                cnt["ev"] += 1

        def do_block(nk, xap, kg0, row0, is_ctx, bidx):
            n = 1 if is_ctx else 0
            xv = xap.rearrange("(k s) f -> k s f", s=8)
            T.dma(yg[:nk], YSC[row0:row0 + nk * 8, :].rearrange("(k t) f -> k t f", t=8), w=[("yg", t) for t in range(8)],
                  stream="yg")
            T.dma(szt[:nk], SZ[row0:row0 + nk * 8, :].rearrange("(k t) f -> k t f", t=8), w=["szt"], stream="szt")
            for half in range(2):
                T.dma(xh[half][:nk], xv[:, half * 4:(half + 1) * 4, :], w=[("xh", half)], stream="xh%d" % half)
            transposes(yg, ygT, "yg", "ygT", nk)
            for t in range(8):
                sgt = sg[t % 2]
                for hf in range(2):
                    pu = psU[cnt["u"] % 4]; pkey = ("psU", cnt["u"] % 4); cnt["u"] += 1
                    A("pe", lambda e, pu=pu, hf=hf: e.matmul(pu[:nk, :], lhsT=ones[0:1, :nk], rhs=bgb[0:1, hf * 512:(hf + 1) * 512],
                                                            start=True, stop=False), r=["ones", "bgb"], w=[pkey])
                    for ft in range(8):
                        A("pe", lambda e, pu=pu, hf=hf, ft=ft, t=t: e.matmul(
                            pu[:nk, :], lhsT=ygT[:, ft, t, :nk], rhs=wg[:, ft, hf * 512:(hf + 1) * 512],
                            start=False, stop=(ft == 7)), r=[("ygT", ft), "p0_wglu"], w=[pkey])
                    A("act", lambda e, pu=pu, hf=hf, sgt=sgt: e.activation(out=sgt[:nk, hf * 512:(hf + 1) * 512], in_=pu[:nk, :],
                                                                          func=AF.Sigmoid), r=[pkey], w=[("sg", t % 2)])
                A("dve", lambda e, t=t, sgt=sgt: e.tensor_tensor(out=sgt[:nk], in0=sgt[:nk], in1=yg[:nk, t, :], op=ALU.mult),
                  r=[("sg", t % 2), ("yg", t)], w=[("sg", t % 2)])
                A("pool", lambda e, t=t, sgt=sgt: e.tensor_tensor(out=y3[:nk, t, :], in0=sgt[:nk], in1=szt[:nk, t, :], op=ALU.mult),
                  r=[("sg", t % 2), "szt"], w=[("y3", t)])
            transposes(y3, y3T, "y3", "y3T", nk)
            for t in range(8):
                half, s4 = divmod(t, 4)
                for hf in range(2):
                    pu = psU[cnt["u"] % 4]; pkey = ("psU", cnt["u"] % 4); cnt["u"] += 1
                    for ft in range(8):
                        A("pe", lambda e, pu=pu, hf=hf, ft=ft, t=t: e.matmul(
                            pu[:nk, :], lhsT=y3T[:, ft, t, :nk], rhs=wo[:, ft, hf * 512:(hf + 1) * 512],
                            start=(ft == 0), stop=(ft == 7)), r=[("y3T", ft), "p0_wout"], w=[pkey])
                    tm = tmp[cnt["tmp"] % 2]; tkey = ("tmp", cnt["tmp"] % 2); cnt["tmp"] += 1
                    A("dve", lambda e, pu=pu, hf=hf, tm=tm, n=n: e.tensor_tensor(
                        out=tm[:nk], in0=pu[:nk, :], in1=grep[n][:nk, hf * 512:(hf + 1) * 512], op=ALU.mult),
                        r=[pkey, ("gate", n)], w=[tkey])
                    A("pool", lambda e, hf=hf, tm=tm, half=half, s4=s4: e.tensor_tensor(
                        out=xh[half][:nk, s4, hf * 512:(hf + 1) * 512], in0=xh[half][:nk, s4, hf * 512:(hf + 1) * 512],
                        in1=tm[:nk], op=ALU.add), r=[tkey, ("xh", half)], w=[("xh", half)])
            dst = CTX1 if is_ctx else X1[bidx * 1024:(bidx + 1) * 1024, :]
            dv = dst.rearrange("(k s) f -> k s f", s=8)
            for half in range(2):
                T.dma(dv[:, half * 4:(half + 1) * 4, :], xh[half][:nk], r=[("xh", half)], w=["X1"], stream="x1st", eng="pool")
        for blk in blocks:
            do_block(*blk)
        T.emit(block)


def phase_attn(nc, T, VEC, X1, CTX1, w_in, q_norm, k_norm, w_out, fin_g, identf, identb, posr_in, posc_in, fidx_in,
               out, stop_after):
    NKT = 66
    A = T.add
    with ExitStack() as es:
        sb = lambda n, s, d=F32: es.enter_context(nc.sbuf_tensor("a_" + n, s, d))
        KT = sb("KT", [128, 2, NKT * 128], BF16)
        Vt = sb("V", [128, NKT, 4, 65], BF16)
        cosr = sb("cosr", [128, 8, 16]); sinr = sb("sinr", [128, 8, 16])
        cosc = sb("cosc", [128, 8, 16]); sinc = sb("sinc", [128, 8, 16])
        rep = {}
        for nm in ("gs0", "sh0"):
            rep[nm] = sb("rep_" + nm, [128, D])
        qn_rep = sb("qn_rep", [128, 64]); kn_rep = sb("kn_rep", [128, 64])
        halfpi = sb("halfpi", [128, 1])
        xh = [sb("xh0", [128, 4, D])] * 2
        tmp32 = [sb("tmp32a", [128, D])] * 2
        ss = sb("ss", [128, 8]); ms = sb("ms", [128, 8]); rstd = sb("rstd", [128, 8])
        h = sb("h", [128, 4, D], BF16)
        hT = sb("hT", [128, 8, 4, 128], BF16)
        sq = sb("sq", [128, 512]); qn = [sb("qn0", [128, 512])] * 2
        junk = sq[:].bitcast(BF16)
        ssh = sb("ssh", [128, 8]); msh = sb("msh", [128, 8]); rsh = sb("rsh", [128, 8])
        ra_ = sb("ra", [128, 512]); rb_ = sb("rb", [128, 512])
        cc = sb("cc", [128, 8, 2, 16]); cs_ = sb("cs", [128, 8, 2, 16])
        krope = sb("krope", [128, 4, 64], BF16)
        psU = [es.enter_context(nc.psum_tensor("apsU0", [128, 512], F32))] * 2
        psS = [es.enter_context(nc.psum_tensor("apsS%d" % i, [128, 2, 512], F32)) for i in range(2)]
        psO = [es.enter_context(nc.psum_tensor("apsO%d" % i, [128, 512], F32)) for i in range(2)]
        psN = es.enter_context(nc.psum_tensor("apsN", [128, 4, 128], F32))
        psTb = psN[:].rearrange("p a b -> p (a b)").bitcast(BF16).rearrange("p (a b) -> p a b", b=128)
        cnt = {"u": 0, "ev": 0, "s": 0}
        BS = [dict(i=0, sq=sq, qn=qn[0], ssh=ssh, msh=msh, rsh=rsh, ra=ra_, rb=rb_)]
        pu_bufs = [(psU[0][:, :], ("pu", 0))] + [(psS[i_][:, c_, :], ("bank", i_, c_)) for i_ in range(2) for c_ in range(2)]

        def next_pu():
            r_ = pu_bufs[cnt["u"] % len(pu_bufs)]
            cnt["u"] += 1
            return r_

        def norm_unit(nk, half, n, ns=4):
            xt = xh[half]
            A("dve", lambda e: e.memset(ss[:], 0.0), w=["ss"])
            for s4 in range(ns):
                A("act", lambda e, s4=s4: e.activation(out=junk[:nk, :], in_=xt[:nk, s4, :], func=AF.Square,
                                                       accum_out=ss[:nk, s4:s4 + 1]), r=[("xh", s4)], w=[("sq", 0), "ss"])
            A("dve", lambda e: e.tensor_scalar(out=ms[:nk, 0:ns], in0=ss[:nk, 0:ns], scalar1=1.0 / D, scalar2=EPS, op0=ALU.mult,
                                               op1=ALU.add), r=["ss"], w=["ms"])
            A("act", lambda e: e.sqrt(out=ms[:nk, 0:ns], in_=ms[:nk, 0:ns]), r=["ms"], w=["ms"])
            A("dve", lambda e: e.reciprocal(out=rstd[:nk, 0:ns], in_=ms[:nk, 0:ns]), r=["ms"], w=["rstd"])
            for s4 in range(ns):
                tm = tmp32[s4 % 2]
                A("dve", lambda e, s4=s4, tm=tm: e.scalar_tensor_tensor(
                    out=tm[:nk], in0=xt[:nk, s4, :], scalar=rstd[:nk, s4:s4 + 1], in1=rep["gs%d" % n][:nk],
                    op0=ALU.mult, op1=ALU.mult), r=[("xh", s4), "rstd", "rep"], w=[("tmp32", 0)])
                A("dve", lambda e, s4=s4, tm=tm: e.tensor_tensor(out=h[:nk, s4, :], in0=tm[:nk], in1=rep["sh%d" % n][:nk],
                                                                op=ALU.add), r=[("tmp32", 0), "rep"], w=["h"])
            tok_transposes(h, nk, ns)

        def tok_transposes(src, nk, ns=4):
            for ft in range(8):
                for s4 in range(ns):
                    A("pe", lambda e, s4=s4, ft=ft: e.transpose(out=psTb[:, s4, :nk], in_=src[:nk, s4, ft * 128:(ft + 1) * 128],
                                                                identity=identb[:nk, :nk]), r=["h", "identb"], w=["psNT"])
                if nk == 128:
                    ov_ = hT[:, ft, 0:ns, :]
                else:
                    ov_ = hT[:, ft].rearrange("p s k -> p (s k)")[:, 0:ns * nk].rearrange("p (s k) -> p s k", k=nk)
                if cnt["ev"] % 2 == 0:
                    A("act", lambda e, ft=ft, ov_=ov_: e.copy(out=ov_, in_=psTb[:, 0:ns, :nk]), r=["psNT"], w=["hT"])
                else:
                    A("dve", lambda e, ft=ft, ov_=ov_: e.tensor_copy(out=ov_, in_=psTb[:, 0:ns, :nk]), r=["psNT"], w=["hT"])
                cnt["ev"] += 1

        def head_norm(pu, pkey, nh, norm_rep_t, bs):
            w = nh * 64
            k_ = bs["i"]
            sq_, ssh_, msh_, rsh_, outf = bs["sq"], bs["ssh"], bs["msh"], bs["rsh"], bs["qn"]
            A("act", lambda e: e.activation(out=sq_[:, 0:w], in_=pu[:, 0:w], func=AF.Square), r=[pkey], w=[("sq", k_)])
            A("dve", lambda e: e.tensor_reduce(out=ssh_[:, 0:nh], in_=sq_[:, 0:w].rearrange("p (a d) -> p a d", d=64), axis=AX.X,
                                               op=ALU.add), r=[("sq", k_)], w=[("ssh", k_)])
            A("dve", lambda e: e.tensor_scalar(out=msh_[:, 0:nh], in0=ssh_[:, 0:nh], scalar1=1.0 / 64, scalar2=EPS, op0=ALU.mult,
                                               op1=ALU.add), r=[("ssh", k_)], w=[("msh", k_)])
            A("act", lambda e: e.sqrt(out=msh_[:, 0:nh], in_=msh_[:, 0:nh]), r=[("msh", k_)], w=[("msh", k_)])
            A("dve", lambda e: e.reciprocal(out=rsh_[:, 0:nh], in_=msh_[:, 0:nh]), r=[("msh", k_)], w=[("rsh", k_)])
            A("dve", lambda e: e.tensor_tensor(out=outf[:, 0:w].rearrange("p (a d) -> p a d", d=64),
                                               in0=pu[:, 0:w].rearrange("p (a d) -> p a d", d=64),
                                               in1=bc(rsh_[:, 0:nh].unsqueeze(2), [128, nh, 64]), op=ALU.mult),
              r=[pkey, ("rsh", k_)], w=[("qnf", k_)])
            A("pool", lambda e: e.tensor_tensor(out=outf[:, 0:w].rearrange("p (a d) -> p a d", d=64),
                                                in0=outf[:, 0:w].rearrange("p (a d) -> p a d", d=64),
                                                in1=bc(norm_rep_t[:].unsqueeze(1), [128, nh, 64]), op=ALU.mult),
              r=[("qnf", k_), "nrep"], w=[("qnf", k_)])

        def rope(bs, nh, s, outv, okey):
            k_ = bs["i"]
            src, ra__, rb__ = bs["qn"], bs["ra"], bs["rb"]
            sv = src[:, 0:nh * 64].rearrange("p (a x t f) -> p a x t f", x=2, t=2, f=16)
            ov = outv.rearrange("p a (x t f) -> p a x t f", x=2, t=2, f=16)
            x1, x2 = sv[:, :, :, 0, :], sv[:, :, :, 1, :]
            cb = bc(cc[:, s].unsqueeze(1), [128, nh, 2, 16]); sbb = bc(cs_[:, s].unsqueeze(1), [128, nh, 2, 16])
            n2 = nh * 32
            av = ra__[:, 0:n2].rearrange("p (a x f) -> p a x f", x=2, f=16)
            bv = rb__[:, 0:n2].rearrange("p (a x f) -> p a x f", x=2, f=16)
            av2 = ra__[:, n2:2 * n2].rearrange("p (a x f) -> p a x f", x=2, f=16)
            bv2 = rb__[:, n2:2 * n2].rearrange("p (a x f) -> p a x f", x=2, f=16)
            A("dve", lambda e: e.tensor_tensor(out=av, in0=x1, in1=cb, op=ALU.mult), r=[("qnf", k_), "cc"], w=[("ra", k_)])
            A("pool", lambda e: e.tensor_tensor(out=bv, in0=x2, in1=sbb, op=ALU.mult), r=[("qnf", k_), "cc"], w=[("rb", k_)])
            A("dve", lambda e: e.tensor_tensor(out=av2, in0=x2, in1=cb, op=ALU.mult), r=[("qnf", k_), "cc"], w=[("ra", k_)])
            A("pool", lambda e: e.tensor_tensor(out=bv2, in0=x1, in1=sbb, op=ALU.mult), r=[("qnf", k_), "cc"], w=[("rb", k_)])
            A("dve", lambda e: e.tensor_tensor(out=ov[:, :, :, 0, :], in0=av, in1=bv, op=ALU.subtract), r=[("ra", k_), ("rb", k_)], w=[okey])
            A("pool", lambda e: e.tensor_tensor(out=ov[:, :, :, 1, :], in0=av2, in1=bv2, op=ALU.add), r=[("ra", k_), ("rb", k_)], w=[okey])

        def block_tables(b):
            for (tab, rsrc, csrc) in ((cc, cosr, cosc), (cs_, sinr, sinc)):
                A("pool", lambda e, tab=tab, rsrc=rsrc: e.tensor_copy(out=tab[:, :, 0, :], in_=bc(rsrc[:, b, :].unsqueeze(1), [128, 8, 16])),
                  r=["tabs"], w=["cc"])
                A("pool", lambda e, tab=tab, csrc=csrc: e.tensor_copy(out=tab[:, :, 1, :], in_=csrc[:]), r=["tabs"], w=["cc"])

        with ExitStack() as es1:
            sbt = lambda n, s, d=F32: es1.enter_context(nc.sbuf_tensor("a1_" + n, s, d))
            stage = [sbt("stg0", [128, 512]), sbt("stg1", [128, 512])]
            posr = sbt("posr", [128, 8]); posc = sbt("posc", [128, 8]); fidx = sbt("fidx", [128, 16]); freq = sbt("freq", [128, 16])
            ta = sbt("ta", [128, 8, 16]); tb = sbt("tb", [128, 8, 16]); tn = sbt("tn", [128, 8, 16], I32)
            rep["gs1"] = sbt("rep_gs1", [128, D]); rep["sh1"] = sbt("rep_sh1", [128, D])
            BS.append(dict(i=1, sq=sbt("sq1", [128, 512]), qn=sbt("qn1", [128, 512]), ssh=sbt("ssh1", [128, 8]), msh=sbt("msh1", [128, 8]),
                           rsh=sbt("rsh1", [128, 8]), ra=sbt("ra1", [128, 512]), rb=sbt("rb1", [128, 512])))
            kropes = [krope, sbt("krope1", [128, 4, 64], BF16)]
            block = es1.enter_context(nc.Block())
            for nm, v in (("gs0", 6), ("sh0", 7), ("gs1", 9), ("sh1", 10)):
                T.dma(rep[nm][:], VEC[v, :].partition_broadcast(128), w=["rep"], stream="rep_" + nm)
            T.dma(qn_rep[:], q_norm.partition_broadcast(128), w=["nrep"], stream="qnr")
            T.dma(kn_rep[:], k_norm.partition_broadcast(128), w=["nrep"], stream="knr")
            T.dma(posr[:], posr_in[:, :], w=["posr"], stream="posr")
            T.dma(posc[:], posc_in[:, :], w=["posc"], stream="posc")
            T.dma(fidx[:], fidx_in[:, :], w=["fidx"], stream="fidx")
            A("pool", lambda e: e.memset(halfpi[:], float(np.pi / 2)), w=["halfpi"])
            A("pool", lambda e: e.memset(Vt[:], 1.0), w=["V"])
            A("act", lambda e: e.activation(out=freq[:], in_=fidx[:], func=AF.Exp, scale=float(-np.log(10000.0) / 16.0)),
              r=["fidx"], w=["freq"])
            for (pos, ct, st) in ((posr, cosr, sinr), (posc, cosc, sinc)):
                A("dve", lambda e, pos=pos: e.tensor_tensor(out=ta[:], in0=bc(pos[:].unsqueeze(2), [128, 8, 16]),
                                                           in1=bc(freq[:].unsqueeze(1), [128, 8, 16]), op=ALU.mult),
                  r=["posr", "posc", "freq"], w=["ta"])
                A("dve", lambda e: e.tensor_scalar(out=tn[:], in0=ta[:], scalar1=1.0 / TWO_PI, scalar2=0.0, op0=ALU.mult, op1=ALU.add),
                  r=["ta"], w=["tn"])
                A("dve", lambda e: e.scalar_tensor_tensor(out=tb[:], in0=tn[:], scalar=-TWO_PI, in1=ta[:], op0=ALU.mult, op1=ALU.add),
                  r=["tn", "ta"], w=["tb"])
                A("dve", lambda e: e.tensor_scalar(out=tb[:], in0=tb[:], scalar1=-PI_SAFE, scalar2=PI_SAFE, op0=ALU.max, op1=ALU.min),
                  r=["tb"], w=["tb"])
                A("dve", lambda e: e.scalar_tensor_tensor(out=ta[:], in0=tb[:], scalar=-1.0, in1=tb[:], op0=ALU.mult, op1=ALU.max),
                  r=["tb"], w=["ta"])
                A("act", lambda e, st=st: e.activation(out=st[:], in_=tb[:], func=AF.Sin), r=["tb"], w=["tabs"])
                A("act", lambda e, ct=ct: e.activation(out=ct[:], in_=ta[:], func=AF.Sin, scale=-1.0, bias=halfpi[:, 0:1]),
                  r=["ta", "halfpi"], w=["tabs"])
            wkv = es1.enter_context(nc.sbuf_tensor("a1_wkv", [128, 8, 512], BF16))
            for ft in range(8):
                st_ = stage[ft % 2]
                T.dma(st_[:], w_in[ft * 128:(ft + 1) * 128, 1024:1536], w=[("stage", ft % 2)], stream="stage%d" % (ft % 2))
                A("pool", lambda e, st_=st_, ft=ft: e.tensor_copy(
                    out=wkv[:, ft, 0:256].rearrange("p (gp hf d) -> p gp hf d", gp=2, hf=2),
                    in_=st_[:, 0:256].rearrange("p (hf gp d) -> p gp hf d", gp=2, hf=2)), r=[("stage", ft % 2)], w=["wkv"])
                A("pool", lambda e, st_=st_, ft=ft: e.tensor_copy(out=wkv[:, ft, 256:512], in_=st_[:, 256:512]),
                  r=[("stage", ft % 2)], w=["wkv"])

            def kv_tile(kt, lhs_fn, s, is_ctx):
                pu, pkey = next_pu()
                for ft in range(8):
                    A("pe", lambda e, pu=pu, ft=ft: e.matmul(pu[:, :], lhsT=lhs_fn(ft), rhs=wkv[:, ft, :], start=(ft == 0), stop=(ft == 7)),
                      r=["hT", "wkv"], w=[pkey])
                A("act", lambda e, pu=pu: e.copy(out=Vt[:, kt, :, 0:64], in_=pu[:, 256:512].rearrange("p (g d) -> p g d", d=64)),
                  r=[pkey], w=["V"])
                bs = BS[kt % 2]; kr = kropes[kt % 2]; okey = ("kroped", kt % 2)
                head_norm(pu, pkey, 4, kn_rep, bs)
                if is_ctx:
                    A("dve", lambda e: e.tensor_copy(out=kr[:].rearrange("p a d -> p (a d)"), in_=bs["qn"][:, 0:256]),
                      r=[("qnf", bs["i"])], w=[okey])
                else:
                    rope(bs, 4, s, kr[:], okey)
                for gp in range(2):
                    A("pe", lambda e, gp=gp, kr=kr: e.transpose(out=psTb[:, gp, :], in_=kr[:, 2 * gp:2 * gp + 2, :].rearrange("p a d -> p (a d)"), identity=identb[:]),
                      r=[okey, "identb"], w=["psNT"])
                A("dve", lambda e: e.tensor_copy(out=KT[:, :, kt * 128:(kt + 1) * 128], in_=psTb[:, 0:2, :]), r=["psNT"], w=["KT"])

            cv = CTX1.rearrange("(k s) f -> k s f", s=8)
            for j in range(2):
                for s4 in range(4):
                    T.dma(xh[0][:NKC, s4, :], cv[:, j * 4 + s4, :], w=[("xh", s4)], stream="xh%d" % s4)
                norm_unit(NKC, 0, 1)
                kv_tile(j, lambda ft: hT[:, ft].rearrange("p s k -> p (s k)")[:, 0:4 * NKC], None, True)
            for b in range(8):
                block_tables(b)
                xv = X1[b * 1024:(b + 1) * 1024, :].rearrange("(k s) f -> k s f", s=8)
                for hf in range(2):
                    for s4 in range(4):
                        T.dma(xh[0][:, s4, :], xv[:, hf * 4 + s4, :], w=[("xh", s4)], stream="xh%d" % s4)
                    norm_unit(128, 0, 0)
                    for s4 in range(4):
                        kv_tile(2 + b * 8 + hf * 4 + s4, lambda ft, s4=s4: hT[:, ft, s4, :], hf * 4 + s4, False)
            T.emit(block)
        if stop_after == 4:
            return finish(nc, T, out)

        with ExitStack() as es2:
            sbt = lambda n, s, d=F32: es2.enter_context(nc.sbuf_tensor("a2_" + n, s, d))
            stg = tmp32[0]
            rep["gate"] = sbt("rep_gate", [128, D]); rep["fin"] = sbt("rep_fin", [128, D])
            block = es2.enter_context(nc.Block())
            T.dma(rep["gate"][:], VEC[8, :].partition_broadcast(128), w=["rep"], stream="rep_gate")
            T.dma(rep["fin"][:], fin_g.partition_broadcast(128), w=["rep"], stream="rep_fin")
            wqz = es2.enter_context(nc.sbuf_tensor("a2_wqz", [128, 8, 2048], BF16))
            wo = es2.enter_context(nc.sbuf_tensor("a2_wo", [128, 8, D], BF16))
            for ft in range(8):
                for (c0, o0) in ((0, 0), (1536, 1024)):
                    T.dma(stg[:], w_in[ft * 128:(ft + 1) * 128, c0:c0 + 1024], w=[("tmp32", 0)], stream="stg")
                    if o0 == 0:
                        A("pool", lambda e, ft=ft: e.tensor_copy(
                            out=wqz[:, ft, 0:1024].rearrange("p (hp hf d) -> p hp hf d", hp=8, hf=2),
                            in_=stg[:].rearrange("p (hf hp d) -> p hp hf d", hp=8, hf=2)), r=[("tmp32", 0)], w=["wqz"])
                    else:
                        A("pool", lambda e, ft=ft, o0=o0: e.tensor_copy(out=wqz[:, ft, o0:o0 + 1024], in_=stg[:]),
                          r=[("tmp32", 0)], w=["wqz"])
                T.dma(stg[:], w_out[ft * 128:(ft + 1) * 128, :], w=[("tmp32", 0)], stream="stg")
                A("pool", lambda e, ft=ft: e.tensor_copy(out=wo[:, ft, :], in_=stg[:]), r=[("tmp32", 0)], w=["wo"])
            qrope = sbt("qrope", [128, 16, 64], BF16)
            QT = sbt("QT", [128, 8, 512], BF16)
            szq = sbt("szq", [128, 4, D], BF16)
            PT = [sbt("PT%d" % i, [128, 2, 512], BF16) for i in range(2)]
            oT = [sbt("oT0", [65, 512]), sbt("oT1", [65, 512])]
            rec = sbt("rec", [128, 4])
            tmpo = [tmp32[0][:, 0:512], tmp32[0][:, 512:1024]]
            ucount = 0
            for b in range(8):
                block_tables(b)
                xv = X1[b * 1024:(b + 1) * 1024, :].rearrange("(k s) f -> k s f", s=8)
                ov = out[b * 1024:(b + 1) * 1024, :].rearrange("(k s) f -> k s f", s=8)
                for hf in range(2):
                    xt = xh[0]; uh = 0; ucount += 1
                    for s4 in range(4):
                        T.dma(xt[:, s4, :], xv[:, hf * 4 + s4, :], w=[("xh", s4)], stream="xh%d" % s4)
                    norm_unit(128, uh, 0)
                    for s4 in range(4):
                        s = hf * 4 + s4
                        for nb in range(4):
                            pu, pkey = next_pu()
                            for ft in range(8):
                                A("pe", lambda e, pu=pu, ft=ft, s4=s4, nb=nb: e.matmul(
                                    pu[:, :], lhsT=hT[:, ft, s4, :], rhs=wqz[:, ft, nb * 512:(nb + 1) * 512],
                                    start=(ft == 0), stop=(ft == 7)), r=["hT", "wqz"], w=[pkey])
                            if nb < 2:
                                head_norm(pu, pkey, 8, qn_rep, BS[0])
                                rope(BS[0], 8, s, qrope[:, nb * 8:(nb + 1) * 8, :], "roped")
                            else:
                                A("act", lambda e, pu=pu, s4=s4, nb=nb: e.activation(
                                    out=szq[:, s4, (nb - 2) * 512:(nb - 1) * 512], in_=pu[:, :], func=AF.Silu), r=[pkey], w=["szq"])
                        for hp in range(8):
                            A("pe", lambda e, hp=hp, s4=s4: e.transpose(out=psTb[:, hp, :], in_=qrope[:, 2 * hp:2 * hp + 2, :].rearrange("p a d -> p (a d)"), identity=identb[:]),
                              r=["roped", "identb"], w=["psNT"])
                        A("dve", lambda e, s4=s4: e.tensor_copy(out=QT[:, :, s4 * 128:(s4 + 1) * 128], in_=psTb[:]), r=["psNT"], w=["QT"])
                    stream = [(hp, kt) for hp in range(8) for kt in range(NKT)]
                    pend = {}

                    def finalize(hd, ot):
                        for j in range(4):
                            A("pe", lambda e, j=j, ot=ot: e.transpose(out=psN[:, j, 0:65], in_=ot[:, j * 128:(j + 1) * 128],
                                                                     identity=identf[0:65, 0:65]), r=[("oT", hd // 8), "identf"], w=["psNT"])
                        A("dve", lambda e: e.reciprocal(out=rec[:], in_=psN[:, :, 64]), r=["psNT"], w=["rec"])
                        A("dve", lambda e, hd=hd: e.tensor_tensor(out=h[:, :, hd * 64:(hd + 1) * 64], in0=psN[:, :, 0:64],
                                                                 in1=bc(rec[:].unsqueeze(2), [128, 4, 64]), op=ALU.mult),
                          r=["psNT", "rec"], w=["h"])

                    def pv(idx):
                        hp, kt = stream[idx]
                        g = hp // 4
                        pt = PT[idx % 2]
                        for c in range(2):
                            A("pe", lambda e, pt=pt, kt=kt, g=g, c=c: e.matmul(
                                psO[c][0:65, :], lhsT=Vt[:, kt, g + 2 * c, :], rhs=pt[:, c, :], start=(kt == 0), stop=(kt == NKT - 1)),
                                r=[("PT", idx % 2), "V"], w=[("psO", c)])
                        if kt == NKT - 1:
                            for c in range(2):
                                A("dve", lambda e, c=c: e.tensor_copy(out=oT[c][:], in_=psO[c][0:65, :]), r=[("psO", c)], w=[("oT", c)])
                            pend[idx + 2] = hp

                    for idx, (hp, kt) in enumerate(stream):
                        gp = hp // 4
                        ps = psS[idx % 2]; pt = PT[idx % 2]
                        for c in range(2):
                            rows = slice(c * 64, c * 64 + 64)
                            A("pe", lambda e, ps=ps, rows=rows, gp=gp, kt=kt, hp=hp, c=c: e.matmul(
                                ps[:, c, :], lhsT=KT[rows, gp, kt * 128:(kt + 1) * 128], rhs=QT[rows, hp, :], start=True, stop=True,
                                tile_position=(64 * c, 0)),
                                r=["KT", "QT"], w=[("bank", idx % 2, c)])
                        A("act", lambda e, ps=ps, pt=pt: e.activation(out=pt[:], in_=ps[:], func=AF.Exp, scale=0.125),
                          r=[("bank", idx % 2, 0), ("bank", idx % 2, 1)], w=[("PT", idx % 2)])
                        if idx >= 1:
                            pv(idx - 1)
                        if idx in pend:
                            hp_ = pend.pop(idx)
                            finalize(hp_, oT[0]); finalize(hp_ + 8, oT[1])
                    pv(len(stream) - 1)
                    for k_ in sorted(pend):
                        finalize(pend[k_], oT[0]); finalize(pend[k_] + 8, oT[1])
                    A("dve", lambda e: e.tensor_tensor(out=h[:], in0=h[:], in1=szq[:], op=ALU.mult), r=["h", "szq"], w=["h"])
                    tok_transposes(h, 128)
                    for s4 in range(4):
                        for nb in range(2):
                            pu, pkey = next_pu()
                            for ft in range(8):
                                A("pe", lambda e, pu=pu, ft=ft, s4=s4, nb=nb: e.matmul(
                                    pu[:, :], lhsT=hT[:, ft, s4, :], rhs=wo[:, ft, nb * 512:(nb + 1) * 512],
                                    start=(ft == 0), stop=(ft == 7)), r=["hT", "wo"], w=[pkey])
                            tm = tmpo[nb]
                            A("dve", lambda e, pu=pu, tm=tm, nb=nb: e.tensor_tensor(out=tm, in0=pu[:, :],
                                                                                   in1=rep["gate"][:, nb * 512:(nb + 1) * 512], op=ALU.mult),
                              r=[pkey, "rep"], w=[("tmp32", 0)])
                            A("pool", lambda e, tm=tm, nb=nb, s4=s4, xt=xt: e.tensor_tensor(
                                out=xt[:, s4, nb * 512:(nb + 1) * 512], in0=xt[:, s4, nb * 512:(nb + 1) * 512], in1=tm, op=ALU.add),
                                r=[("tmp32", 0), ("xh", s4)], w=[("xh", s4)])
                    A("dve", lambda e: e.memset(ss[:], 0.0), w=["ss"])
                    for s4 in range(4):
                        A("act", lambda e, s4=s4, xt=xt: e.activation(out=junk[:, :], in_=xt[:, s4, :], func=AF.Square,
                                                                     accum_out=ss[:, s4:s4 + 1]), r=[("xh", s4)], w=[("sq", 0), "ss"])
                    A("dve", lambda e: e.tensor_scalar(out=ms[:, 0:4], in0=ss[:, 0:4], scalar1=1.0 / D, scalar2=EPS, op0=ALU.mult,
                                                       op1=ALU.add), r=["ss"], w=["ms"])
                    A("act", lambda e: e.sqrt(out=ms[:, 0:4], in_=ms[:, 0:4]), r=["ms"], w=["ms"])
                    A("dve", lambda e: e.reciprocal(out=rstd[:, 0:4], in_=ms[:, 0:4]), r=["ms"], w=["rstd"])
                    for s4 in range(4):
                        A("dve", lambda e, s4=s4, xt=xt: e.scalar_tensor_tensor(
                            out=xt[:, s4, :], in0=xt[:, s4, :], scalar=rstd[:, s4:s4 + 1], in1=rep["fin"][:], op0=ALU.mult, op1=ALU.mult),
                            r=[("xh", s4), "rstd", "rep"], w=[("xh", s4)])
                        T.dma(ov[:, hf * 4 + s4, :], xt[:, s4, :], r=[("xh", s4)], w=["out"], stream="outst", eng="pool")
            A("sp", None, r=["out"])
            T.emit(block)
    return None


def _host_consts():
    ident = np.eye(128, dtype=np.float32)
    sidx = np.arange(128) // 16
    maskf = (sidx[None, :] >= sidx[:, None]).astype(np.float32)
    maskb = (sidx[None, :] <= sidx[:, None]).astype(np.float32)
    kio = np.tile(np.arange(NK, dtype=np.float32)[None, :], (128, 1))
    jv = np.tile(np.arange(-7, 9, dtype=np.float32)[None, :], (128, 1))
    k = np.arange(128)
    posr = (16 * np.arange(8)[None, :] + (k // 8)[:, None]).astype(np.float32)
    posc = (8 * (k % 8)[:, None] + np.arange(8)[None, :]).astype(np.float32)
    fidx = np.tile(np.arange(16, dtype=np.float32)[None, :], (128, 1))
    return dict(ident=ident, maskf=maskf, maskb=maskb, kio=kio, jvals=jv, posr=posr, posc=posc, fidx=fidx)


def make_in_maps(inputs):
    consts = _host_consts()
    f = lambda a: np.ascontiguousarray(np.asarray(a, dtype=np.float32))
    shared = dict(
        w_mod=f(inputs["w_mod"]), b_mod=f(inputs["b_mod"]), norm_g=f(inputs["norm_g"]),
        ssm_w_in=f(inputs["ssm_w_in"][0]), ssm_a_re=f(inputs["ssm_a_re"][0]), ssm_a_im=f(inputs["ssm_a_im"][0]),
        ssm_log_dt=f(inputs["ssm_log_dt"][0]), ssm_b_re=f(inputs["ssm_b_re"][0]), ssm_b_im=f(inputs["ssm_b_im"][0]),
        ssm_c_re=f(inputs["ssm_c_re"][0]), ssm_c_im=f(inputs["ssm_c_im"][0]), ssm_d=f(inputs["ssm_d"][0]),
        ssm_w_glu=f(inputs["ssm_w_glu"][0]), ssm_b_glu=f(inputs["ssm_b_glu"][0]), ssm_w_out=f(inputs["ssm_w_out"][0]),
        attn_w_in=f(inputs["attn_w_in"][0]), attn_q_norm=f(inputs["attn_q_norm"][0]),
        attn_k_norm=f(inputs["attn_k_norm"][0]), attn_w_out=f(inputs["attn_w_out"][0]),
        final_norm_g=f(inputs["final_norm_g"]), **consts)
    maps = []
    for b in range(8):
        m = dict(shared)
        m["x"] = f(inputs["x"][b]); m["ctx"] = f(inputs["ctx"][b])
        m["cvec"] = f(np.stack([np.asarray(inputs["c"][b]), np.asarray(inputs["c_ctx"])], 0))
        maps.append(m)
    return maps


def kernel(**inputs):
    nc, _ = build_program()
    maps = make_in_maps(inputs)
    res = run_bass_kernel_spmd(nc, maps, core_ids=list(range(8)))
    return np.stack([np.asarray(r["out"], dtype=np.float32) for r in res.results], 0)
```

```python
import numpy as np
from contextlib import ExitStack
import concourse.bass as bass
import concourse.mybir as mybir
from concourse.bass_utils import run_bass_kernel_spmd

F32 = mybir.dt.float32
BF16 = mybir.dt.bfloat16
I32 = mybir.dt.int32
ALU = mybir.AluOpType
AF = mybir.ActivationFunctionType
AX = mybir.AxisListType

D = 1024
L = 8192
C = 256
NKL = 1024
NKC = 32
NK = NKL + NKC
EPS = 1e-6
TWO_PI = float(2 * np.pi)
PI_SAFE = 3.1415925
STOP_S5_SETUP = [False]


class Op:
    __slots__ = ("eng", "fn", "deps", "is_dma", "stream", "signal", "ticket", "semname")


class Tracker:
    ENGS = ["pe", "act", "dve", "pool", "sp"]

    def __init__(self, nc, es):
        self.nc = nc
        self.es = es
        self.sem = {}
        self.count = {}
        self.ops = []
        self.last_w = {}
        self.readers = {}
        self.barrier = {}
        self.waited = {e: {} for e in self.ENGS}

    def _sem(self, name):
        if name not in self.sem:
            self.sem[name] = self.es.enter_context(self.nc.semaphore(name))
            self.count[name] = 0
        return self.sem[name]

    def add(self, eng, fn, r=(), w=(), dma=False, stream=None):
        op = Op()
        op.eng, op.fn, op.is_dma, op.stream = eng, fn, dma, stream
        op.signal, op.ticket, op.semname = dma, 0, None
        deps = set()
        for k in r:
            if k in self.last_w:
                deps.add(self.last_w[k])
        for k in w:
            if k in self.last_w:
                deps.add(self.last_w[k])
            deps.update(self.readers.get(k, ()))
        i = len(self.ops)
        op.deps = deps
        self.ops.append(op)
        for k in r:
            self.readers.setdefault(k, []).append(i)
        for k in w:
            self.last_w[k] = i
            self.readers[k] = []
        return i

    def dma(self, out, in_, r=(), w=(), stream=None, eng="sp", **kw):
        assert stream is not None
        return self.add(eng, lambda e: e.dma_start(out=out, in_=in_, **kw), r=r, w=w, dma=True, stream=stream)

    def emit(self, block):
        ops = self.ops
        for op in ops:
            for d in op.deps:
                dep = ops[d]
                if dep.is_dma or dep.eng != op.eng or op.eng != "pe":
                    dep.signal = True
        last = {}
        for op in ops:
            if not op.is_dma and op.fn is not None:
                last[op.eng] = op
        for op in last.values():
            op.signal = True
        for op in ops:
            if op.signal and op.fn is not None:
                name = ("D_" + op.stream) if op.is_dma else ("E_" + op.eng)
                self._sem(name)
                self.count[name] += 16 if op.is_dma else 1
                op.ticket = self.count[name]
                op.semname = name
        reg = {"pe": block.tensor, "act": block.scalar, "dve": block.vector, "pool": block.gpsimd, "sp": block.sync}
        for eng in self.ENGS:
            eops = [op for op in ops if op.eng == eng]
            if not eops:
                continue

            def body(e, eops=eops, eng=eng):
                waited = self.waited[eng]
                first = True
                for op in eops:
                    waits = {}
                    if first:
                        waits.update(self.barrier)
                        first = False
                    for d in op.deps:
                        dep = ops[d]
                        if dep.is_dma or dep.eng != eng or eng != "pe":
                            if dep.semname is not None:
                                waits[dep.semname] = max(waits.get(dep.semname, 0), dep.ticket)
                    for s, v in waits.items():
                        if v > 0 and waited.get(s, 0) < v:
                            e.wait_ge(self.sem[s], v)
                            waited[s] = v
                    if op.fn is not None:
                        ins = op.fn(e)
                        if op.signal:
                            ins.then_inc(self.sem[op.semname], 16 if op.is_dma else 1)

            reg[eng](body)
        self.barrier = dict(self.count)
        self.ops = []
        self.last_w = {}
        self.readers = {}


def bc(ap, shape):
    return ap.to_broadcast(shape)


def build_program(stop_after=None, debug=False):
    nc = bass.Bass("TRN2", target_bir_lowering=False)
    dt_in = {}

    def din(name, shape, dt=F32):
        dt_in[name] = nc.dram_tensor(name, list(shape), dt, kind="ExternalInput").ap()
        return dt_in[name]

    x = din("x", [L, D]); ctx = din("ctx", [C, D]); cvec = din("cvec", [2, D])
    w_mod = din("w_mod", [2, D, 3 * D]); b_mod = din("b_mod", [2, 3 * D]); norm_g = din("norm_g", [2, D])
    ssm_w_in = din("ssm_w_in", [D, 2 * D])
    a_re = din("ssm_a_re", [2, 64, 64]); a_im = din("ssm_a_im", [2, 64, 64]); log_dt = din("ssm_log_dt", [2, 64])
    b_re = din("ssm_b_re", [2, 64, 64, 16]); b_im = din("ssm_b_im", [2, 64, 64, 16])
    c_re = din("ssm_c_re", [2, 64, 16, 64]); c_im = din("ssm_c_im", [2, 64, 16, 64])
    ssm_d = din("ssm_d", [D]); w_glu = din("ssm_w_glu", [D, D]); b_glu = din("ssm_b_glu", [D])
    ssm_w_out = din("ssm_w_out", [D, D])
    attn_w_in = din("attn_w_in", [D, 2560]); q_norm = din("attn_q_norm", [64]); k_norm = din("attn_k_norm", [64])
    attn_w_out = din("attn_w_out", [D, D]); fin_g = din("final_norm_g", [D])
    ident_in = din("ident", [128, 128]); maskf_in = din("maskf", [128, 128]); maskb_in = din("maskb", [128, 128])
    kio_in = din("kio", [128, NK]); jv_in = din("jvals", [128, 16])
    posr_in = din("posr", [128, 8]); posc_in = din("posc", [128, 8]); fidx_in = din("fidx", [128, 16])

    out = nc.dram_tensor("out", [L, D], F32, kind="ExternalOutput").ap()
    dbg = {}

    def dout(name, shape, dt=F32):
        dbg[name] = nc.dram_tensor(name, list(shape), dt, kind="ExternalOutput" if debug else "Internal").ap()
        return dbg[name]

    VEC = dout("VEC", [12, D])
    DSCR = dout("DSCR", [64, 128, NK], BF16)
    SZ = dout("SZ", [C + L, D], BF16)
    YSC = dout("YSC", [C + L, D], BF16)
    X1 = dout("X1", [L, D])
    CTX1 = dout("CTX1", [C, D])

    blocks = [(NKC, ctx, 0, 0, True, 0)]
    for b in range(8):
        blocks.append((128, x[b * 1024:(b + 1) * 1024, :], NKC + 128 * b, C + 1024 * b, False, b))

    with ExitStack() as ges:
        T = Tracker(nc, ges)
        identf = ges.enter_context(nc.sbuf_tensor("identf", [128, 128], F32))
        identb = ges.enter_context(nc.sbuf_tensor("identb", [128, 128], BF16))

        with ExitStack() as es:
            sb = lambda n, s, d=F32: es.enter_context(nc.sbuf_tensor(n, s, d))
            scraw = sb("scraw", [128, 2, 8]); sc = sb("sc", [128, 8, 2])
            wst = sb("wst", [128, 8, 3 * D])
            bcol = sb("bcol", [128, 2, 24]); gcol = sb("gcol", [128, 2, 8])
            modc = sb("modc", [128, 2, 24, 2]); gs = sb("gs", [128, 2, 8, 2])
            psm = es.enter_context(nc.psum_tensor("psm", [128, 512], F32))
            block = es.enter_context(nc.Block())
            T.dma(identf[:], ident_in[:, :], w=["identf"], stream="c0")
            T.add("dve", lambda e: e.tensor_copy(out=identb[:], in_=identf[:]), r=["identf"], w=["identb"])
            T.dma(scraw[:], cvec.rearrange("n (kt p) -> p n kt", p=128), w=["scraw"], stream="c1",
                  allow_slow_non_contiguous=True)
            T.dma(bcol[:], b_mod.rearrange("i (j p) -> p i j", p=128), w=["bcol"], stream="c2",
                  allow_slow_non_contiguous=True)
            T.dma(gcol[:], norm_g.rearrange("i (j p) -> p i j", p=128), w=["gcol"], stream="c3",
                  allow_slow_non_contiguous=True)
            T.add("act", lambda e: e.activation(out=sc[:].rearrange("p kt n -> p n kt"), in_=scraw[:], func=AF.Silu),
                  r=["scraw"], w=["sc"])
            for i in range(2):
                psv = psm[:, i * 48:(i + 1) * 48].rearrange("p (j n) -> p j n", n=2)
                for kt in range(8):
                    T.dma(wst[:, kt, :], w_mod[i, kt * 128:(kt + 1) * 128, :], w=[("wst", kt)], stream="wst%d" % kt)
                for j in range(24):
                    for kt in range(8):
                        T.add("pe", lambda e, j=j, kt=kt, psv=psv: e.matmul(
                            psv[:, j, :], lhsT=wst[:, kt, j * 128:(j + 1) * 128], rhs=sc[:, kt, :],
                            start=(kt == 0), stop=(kt == 7)),
                            r=[("wst", kt), "sc"], w=[("psm", i)])
                T.add("dve", lambda e, i=i, psv=psv: e.tensor_tensor(
                    out=modc[:, i], in0=psv, in1=bc(bcol[:, i, :].unsqueeze(2), [128, 24, 2]), op=ALU.add),
                    r=[("psm", i), "bcol"], w=[("modc", i)])
                T.add("dve", lambda e, i=i: e.scalar_tensor_tensor(
                    out=gs[:, i], in0=modc[:, i, 8:16, :], scalar=1.0,
                    in1=bc(gcol[:, i, :].unsqueeze(2), [128, 8, 2]), op0=ALU.add, op1=ALU.mult),
                    r=[("modc", i), "gcol"], w=[("gs", i)])
                for n in range(2):
                    for which, src in ((0, gs[:, i, :, n]), (1, modc[:, i, 0:8, n]), (2, modc[:, i, 16:24, n])):
                        v = i * 6 + n * 3 + which
                        T.dma(VEC[v, :].rearrange("(ft p) -> p ft", p=128), src, r=[("gs", i), ("modc", i)],
                              w=["VEC"], stream="vec", allow_slow_non_contiguous=True)
            T.emit(block)
        if stop_after == 0:
            return nc, finish(nc, T, out)

        phase_front(nc, T, blocks, VEC, 0, ssm_w_in, 2 * D, identb, DSCR, SZ, mode="ssm")
        if stop_after == 1:
            return nc, finish(nc, T, out)
        phase_s5(nc, T, dict(a_re=a_re, a_im=a_im, log_dt=log_dt, b_re=b_re, b_im=b_im, c_re=c_re, c_im=c_im,
                             ssm_d=ssm_d, kio=kio_in, jv=jv_in, maskf=maskf_in, maskb=maskb_in),
                 identf, DSCR, YSC)
        if stop_after == 2:
            return nc, finish(nc, T, out)
        phase_post0(nc, T, blocks, VEC, YSC, SZ, w_glu, b_glu, ssm_w_out, identb, X1, CTX1)
        if stop_after == 3:
            return nc, finish(nc, T, out)
        phase_attn(nc, T, VEC, X1, CTX1, attn_w_in, q_norm, k_norm, attn_w_out, fin_g, identf, identb,
                   posr_in, posc_in, fidx_in, out, stop_after)
    return nc, None


def finish(nc, T, out):
    with ExitStack() as es:
        z = es.enter_context(nc.sbuf_tensor("zfin", [128, D], F32))
        block = es.enter_context(nc.Block())
        T.add("dve", lambda e: e.memset(z[:], 0.0), w=["z"])
        T.dma(out[0:128, :], z[:], r=["z"], w=["out"], stream="fin")
        T.add("sp", None, r=["out"])
        T.emit(block)
    return None


def load_cast_weight(nc, T, es, name, w_ap, ncols, stage):
    wt = es.enter_context(nc.sbuf_tensor(name, [128, 8, ncols], BF16))
    for ft in range(8):
        st = stage[ft % 2]
        T.dma(st[:, 0:ncols], w_ap[ft * 128:(ft + 1) * 128, :], w=[("stage", ft % 2)], stream="stage%d" % (ft % 2))
        T.add("pool", lambda e, st=st, ft=ft: e.tensor_copy(out=wt[:, ft, :], in_=st[:, 0:ncols]),
              r=[("stage", ft % 2)], w=[name])
    return wt


def phase_front(nc, T, blocks, VEC, layer, w_in_ap, ncols, identb, DSCR, SZ, mode):
    with ExitStack() as es:
        sb = lambda n, s, d=F32: es.enter_context(nc.sbuf_tensor(n, s, d))
        stage = [sb("stg0", [128, 2560]), sb("stg1", [128, 2560])]
        block = es.enter_context(nc.Block())
        rep = {}
        for n in range(2):
            for which, nm in ((0, "gs"), (1, "sh")):
                t = sb("rep_%s%d" % (nm, n), [128, D])
                v = layer * 6 + n * 3 + which
                T.dma(t[:], VEC[v, :].partition_broadcast(128), w=[("rep", nm, n)], stream="rep%s%d" % (nm, n))
                rep[(nm, n)] = t
        wt = load_cast_weight(nc, T, es, "w_in_bf", w_in_ap, ncols, stage)
        xh = [sb("xh0", [128, 4, D]), sb("xh1", [128, 4, D])]
        junk = sb("junk", [128, D], BF16)
        tmp32 = [sb("tmp32a", [128, D]), sb("tmp32b", [128, D])]
        ss = sb("ss", [128, 8]); ms = sb("ms", [128, 8]); rstd = sb("rstd", [128, 8])
        h = sb("h", [128, 8, D], BF16)
        hT = sb("hT", [128, 8, 8, 128], BF16)
        ucat = sb("ucat", [128, 64, 8, 16], BF16)
        sz = sb("sz", [128, 8, D], BF16)
        dst = [sb("dst0", [128, 8, 128], BF16), sb("dst1", [128, 8, 128], BF16)]
        psT = [es.enter_context(nc.psum_tensor("psT%d" % i, [128, 8, 128], BF16)) for i in range(2)]
        psU = [es.enter_context(nc.psum_tensor("psU%d" % i, [128, 512], F32)) for i in range(4)]
        psD = [es.enter_context(nc.psum_tensor("psD%d" % i, [128, 8, 128], BF16)) for i in range(2)]
        evac_i = [0]
        ucnt = [0]
        for (nk, xap, kg0, row0, is_ctx, bidx) in blocks:
            n = 1 if is_ctx else 0
            xv = xap.rearrange("(k s) f -> k s f", s=8)
            T.add("dve", lambda e: e.memset(ss[:], 0.0), w=["ss"])
            for half in range(2):
                T.dma(xh[half][:nk], xv[:, half * 4:(half + 1) * 4, :], w=[("xh", half)], stream="xh%d" % half)
                for s4 in range(4):
                    s = half * 4 + s4
                    T.add("act", lambda e, half=half, s4=s4, s=s, nk=nk: e.activation(
                        out=junk[:nk], in_=xh[half][:nk, s4, :], func=AF.Square, accum_out=ss[:nk, s:s + 1]),
                        r=[("xh", half)], w=["junk", "ss"])
            T.add("dve", lambda e, nk=nk: e.tensor_scalar(out=ms[:nk], in0=ss[:nk], scalar1=1.0 / D, scalar2=EPS,
                                                          op0=ALU.mult, op1=ALU.add), r=["ss"], w=["ms"])
            T.add("act", lambda e, nk=nk: e.sqrt(out=ms[:nk], in_=ms[:nk]), r=["ms"], w=["ms"])
            T.add("dve", lambda e, nk=nk: e.reciprocal(out=rstd[:nk], in_=ms[:nk]), r=["ms"], w=["rstd"])
            for s in range(8):
                half, s4 = divmod(s, 4)
                tm = tmp32[s % 2]
                T.add("dve", lambda e, half=half, s4=s4, s=s, nk=nk, tm=tm, n=n: e.scalar_tensor_tensor(
                    out=tm[:nk], in0=xh[half][:nk, s4, :], scalar=rstd[:nk, s:s + 1], in1=rep[("gs", n)][:nk],
                    op0=ALU.mult, op1=ALU.mult), r=[("xh", half), "rstd", ("rep", "gs", n)], w=[("tmp32", s % 2)])
                T.add("pool", lambda e, s=s, nk=nk, tm=tm, n=n: e.tensor_tensor(
                    out=h[:nk, s, :], in0=tm[:nk], in1=rep[("sh", n)][:nk], op=ALU.add),
                    r=[("tmp32", s % 2), ("rep", "sh", n)], w=[("h", s)])
            for ft in range(8):
                pt = psT[ft % 2]
                for s in range(8):
                    T.add("pe", lambda e, pt=pt, s=s, ft=ft, nk=nk: e.transpose(
                        out=pt[:, s, :nk], in_=h[:nk, s, ft * 128:(ft + 1) * 128], identity=identb[:nk, :nk]),
                        r=[("h", s), "identb"], w=[("psT", ft % 2)])
                eng = "act" if (evac_i[0] % 2 == 0) else "dve"
                evac_i[0] += 1
                if eng == "act":
                    T.add("act", lambda e, pt=pt, ft=ft, nk=nk: e.copy(out=hT[:, ft, :, :nk], in_=pt[:, :, :nk]),
                          r=[("psT", ft % 2)], w=[("hT", ft)])
                else:
                    T.add("dve", lambda e, pt=pt, ft=ft, nk=nk: e.tensor_copy(out=hT[:, ft, :, :nk], in_=pt[:, :, :nk]),
                          r=[("psT", ft % 2)], w=[("hT", ft)])
            for s in range(8):
                for nb in range(4):
                    pu = psU[ucnt[0] % 4]
                    pkey = ("psU", ucnt[0] % 4)
                    ucnt[0] += 1
                    for ft in range(8):
                        T.add("pe", lambda e, pu=pu, s=s, nb=nb, ft=ft, nk=nk: e.matmul(
                            pu[:nk, :], lhsT=hT[:, ft, s, :nk], rhs=wt[:, ft, nb * 512:(nb + 1) * 512],
                            start=(ft == 0), stop=(ft == 7)),
                            r=[("hT", ft), "w_in_bf"], w=[pkey])
                    if nb < 2:
                        T.add("dve", lambda e, pu=pu, s=s, nb=nb, nk=nk: e.tensor_copy(
                            out=ucat[:nk, nb * 32:(nb + 1) * 32, s, :],
                            in_=pu[:nk, :].rearrange("k (g h) -> k g h", h=16)),
                            r=[pkey], w=["ucat"])
                    else:
                        T.add("act", lambda e, pu=pu, s=s, nb=nb, nk=nk: e.activation(
                            out=sz[:nk, s, (nb - 2) * 512:(nb - 1) * 512], in_=pu[:nk, :], func=AF.Silu),
                            r=[pkey], w=["sz"])
            T.dma(SZ[row0:row0 + nk * 8, :].rearrange("(k s) f -> k s f", s=8), sz[:nk], r=["sz"], w=["SZ"],
                  stream="szst", eng="pool")
            for gq in range(8):
                pd = psD[gq % 2]
                for gi in range(8):
                    g = gq * 8 + gi
                    T.add("pe", lambda e, pd=pd, gi=gi, g=g, nk=nk: e.transpose(
                        out=pd[:, gi, :nk], in_=ucat[:nk, g, :, :].rearrange("k s h -> k (s h)"),
                        identity=identb[:nk, :nk]),
                        r=["ucat", "identb"], w=[("psD", gq % 2)])
                T.add("dve", lambda e, pd=pd, gq=gq, nk=nk: e.tensor_copy(out=dst[gq % 2][:, :, :nk], in_=pd[:, :, :nk]),
                      r=[("psD", gq % 2)], w=[("dst", gq % 2)])
                T.dma(DSCR[gq * 8:(gq + 1) * 8, :, kg0:kg0 + nk].rearrange("g p k -> p g k"), dst[gq % 2][:, :, :nk],
                      r=[("dst", gq % 2)], w=["DSCR"], stream="dscr%d" % (gq % 2), eng="pool")
        T.emit(block)


def phase_s5(nc, T, P, identf, DSCR, YSC):
    PI2 = TWO_PI
    with ExitStack() as es:
        sb = lambda n, s, d=F32: es.enter_context(nc.sbuf_tensor("s5_" + n, s, d))
        es1 = ExitStack()
        sbt = lambda n, s, d=F32: es1.enter_context(nc.sbuf_tensor("s5t_" + n, s, d))
        kio = sb("kio", [128, NK]); jv = sb("jv", [128, 16])
        maskf = sb("maskf", [128, 128]); maskb = sb("maskb", [128, 128])
        phi = sb("phi", [128, 128]); phi2pi = sb("phi2pi", [128, 128]); rho = sb("rho", [128, 128])
        ctr = sb("ctr", [128, 128, 16]); cti = sb("cti", [128, 128, 16])
        ejr = sb("ejr", [128, 16, 128]); eji = sb("eji", [128, 16, 128])
        bbr = sb("bbr", [128, 128, 16]); bbi = sb("bbi", [128, 128, 16])
        dcol = sb("dcol", [128, 64])
        sm5 = sb("sm5", [128, 128])
        an_r = sbt("an_r", [64, 2, 2, 64]); an_i = sbt("an_i", [64, 2, 2, 64])
        are = sbt("are", [128, 128]); aim = sbt("aim", [128, 128]); ldt = sbt("ldt", [128, 128])
        dtt = sbt("dtt", [128, 128]); alpha = sbt("alpha", [128, 128]); theta = sbt("theta", [128, 128])
        btr = sbt("btr", [128, 128, 16]); bti = sbt("bti", [128, 128, 16])
        cn_r = sbt("cn_r", [128, 8, 2, 2, 64]); cn_i = sbt("cn_i", [128, 8, 2, 2, 64])
        tmpA = sbt("tmpA", [128, 16, 128]); tmpB = sbt("tmpB", [128, 16, 128]); tmpC = sbt("tmpC", [128, 16, 128])
        tni = sbt("tni", [128, 16, 128], I32)
        sm = [sbt("sm%d" % i, [128, 128]) for i in range(5)] + [sm5]
        pss = es.enter_context(nc.psum_tensor("pss", [128, 4, 128], F32))
        psZA = es.enter_context(nc.psum_tensor("psZA", [128, 2, 512], F32))
        psZB = es.enter_context(nc.psum_tensor("psZB", [128, 2, 512], F32))
        psZc = es.enter_context(nc.psum_tensor("psZc", [128, 512], F32))
        psY = [es.enter_context(nc.psum_tensor("psY%d" % i, [128, 4, 128], F32)) for i in range(2)]
        block = es1.enter_context(nc.Block())
        A = T.add
        A("pool", lambda e: e.memset(sm[5][:], float(np.pi / 2)), w=["sm5"])
        T.dma(kio[:], P["kio"][:, :], w=["kio"], stream="p0")
        T.dma(jv[:], P["jv"][:, :], w=["jv"], stream="p1")
        T.dma(maskf[:], P["maskf"][:, :], w=["maskf"], stream="p2")
        T.dma(maskb[:], P["maskb"][:, :], w=["maskb"], stream="p3")
        for c2 in range(2):
            T.dma(an_r[:, :, c2, :], P["a_re"].rearrange("d g p -> g d p"), w=["an_r"], stream="p4")
            T.dma(an_i[:, :, c2, :], P["a_im"].rearrange("d g p -> g d p"), w=["an_i"], stream="p5")
        T.dma(ldt[:], P["log_dt"].rearrange("d g -> (d g)").partition_broadcast(128), w=["ldt"], stream="p6")
        for c2 in range(2):
            for d in range(2):
                for gq in range(4):
                    T.dma(btr[c2 * 64:(c2 + 1) * 64, d * 64 + gq * 16:d * 64 + gq * 16 + 16, :],
                          P["b_re"][d, gq * 16:(gq + 1) * 16].rearrange("g p h -> p g h"), w=["btr"], stream="p7")
                    T.dma(bti[c2 * 64:(c2 + 1) * 64, d * 64 + gq * 16:d * 64 + gq * 16 + 16, :],
                          P["b_im"][d, gq * 16:(gq + 1) * 16].rearrange("g p h -> p g h"), w=["bti"], stream="p8")
            for d in range(2):
                T.dma(cn_r[:, :, d, c2, :], P["c_re"][d].rearrange("(gb g8) h p -> (g8 h) gb p", g8=8),
                      w=["cn_r"], stream="p9")
                T.dma(cn_i[:, :, d, c2, :], P["c_im"][d].rearrange("(gb g8) h p -> (g8 h) gb p", g8=8),
                      w=["cn_i"], stream="p10")
        for s8 in range(8):
            T.dma(dcol[s8 * 16:(s8 + 1) * 16, :], P["ssm_d"].rearrange("(g h) -> h g", h=16), w=["dcol"],
                  stream="p11", allow_slow_non_contiguous=True)
        for (src, dstt, nm) in ((an_r, are, "are"), (an_i, aim, "aim")):
            for d in range(2):
                A("pe", lambda e, src=src, d=d: e.transpose(
                    out=pss[:, d, 0:64], in_=src[:, d, :, :].rearrange("g c p -> g (c p)"), identity=identf[0:64, 0:64]),
                    r=[nm.replace("a", "an_", 1) if False else ("an_r" if nm == "are" else "an_i"), "identf"], w=["pss"])
            A("dve", lambda e, dstt=dstt: e.tensor_copy(out=dstt[:].rearrange("p (d g) -> p d g", d=2), in_=pss[:, 0:2, 0:64]),
              r=["pss"], w=[nm])
        for (src, dstt, nm, snm) in ((cn_r, ctr, "ctr", "cn_r"), (cn_i, cti, "cti", "cn_i")):
            for d in range(2):
                for gq in range(2):
                    for g4 in range(4):
                        gb = gq * 4 + g4
                        A("pe", lambda e, src=src, d=d, gb=gb, g4=g4: e.transpose(
                            out=pss[:, g4, :], in_=src[:, gb, d, :, :].rearrange("q c p -> q (c p)"), identity=identf[:]),
                            r=[snm, "identf"], w=["pss"])
                    A("dve", lambda e, dstt=dstt, d=d, gq=gq: e.tensor_copy(
                        out=dstt[:, d * 64 + gq * 32:d * 64 + gq * 32 + 32, :].rearrange("p (a b) h -> p a b h", a=4),
                        in_=pss[:].rearrange("p a (b h) -> p a b h", h=16)),
                        r=["pss"], w=[nm])
        A("act", lambda e: e.activation(out=dtt[:], in_=ldt[:], func=AF.Exp), r=["ldt"], w=["dtt"])
        A("dve", lambda e: e.tensor_tensor(out=alpha[:], in0=are[:], in1=dtt[:], op=ALU.mult), r=["are", "dtt"], w=["alpha"])
        A("dve", lambda e: e.tensor_tensor(out=theta[:], in0=aim[:], in1=dtt[:], op=ALU.mult), r=["aim", "dtt"], w=["theta"])
        A("dve", lambda e: e.tensor_scalar(out=phi[:], in0=theta[:], scalar1=8.0, scalar2=0.0, op0=ALU.mult, op1=ALU.add),
          r=["theta"], w=["phi"])
        A("dve", lambda e: e.tensor_scalar(out=phi2pi[:], in0=theta[:], scalar1=8.0 / PI2, scalar2=0.0, op0=ALU.mult,
                                           op1=ALU.add), r=["theta"], w=["phi2pi"])
        A("act", lambda e: e.activation(out=rho[:], in_=alpha[:], func=AF.Exp, scale=8.0), r=["alpha"], w=["rho"])
        th_b = bc(theta[:].unsqueeze(1), [128, 16, 128]); al_b = bc(alpha[:].unsqueeze(1), [128, 16, 128])
        jv_b = bc(jv[:].unsqueeze(2), [128, 16, 128])
        A("dve", lambda e: e.tensor_tensor(out=tmpA[:], in0=th_b, in1=jv_b, op=ALU.mult), r=["theta", "jv"], w=["tmpA"])
        A("dve", lambda e: e.tensor_scalar(out=tni[:], in0=tmpA[:], scalar1=1.0 / PI2, scalar2=0.0, op0=ALU.mult,
                                           op1=ALU.add), r=["tmpA"], w=["tni"])
        A("dve", lambda e: e.scalar_tensor_tensor(out=tmpB[:], in0=tni[:], scalar=-PI2, in1=tmpA[:], op0=ALU.mult,
                                                  op1=ALU.add), r=["tni", "tmpA"], w=["tmpB"])
        A("dve", lambda e: e.tensor_scalar(out=tmpB[:], in0=tmpB[:], scalar1=-PI_SAFE, scalar2=PI_SAFE, op0=ALU.max,
                                           op1=ALU.min), r=["tmpB"], w=["tmpB"])
        A("dve", lambda e: e.scalar_tensor_tensor(out=tmpA[:], in0=tmpB[:], scalar=-1.0, in1=tmpB[:], op0=ALU.mult, op1=ALU.max),
          r=["tmpB"], w=["tmpA"])
        A("act", lambda e: e.activation(out=eji[:], in_=tmpB[:], func=AF.Sin), r=["tmpB"], w=["eji"])
        A("act", lambda e: e.activation(out=ejr[:], in_=tmpA[:], func=AF.Sin, scale=-1.0, bias=sm[5][:, 0:1]),
          r=["tmpA", "sm5"], w=["ejr"])
        A("dve", lambda e: e.tensor_tensor(out=tmpC[:], in0=al_b, in1=jv_b, op=ALU.mult), r=["alpha", "jv"], w=["tmpC"])
        A("act", lambda e: e.activation(out=tmpC[:], in_=tmpC[:], func=AF.Exp), r=["tmpC"], w=["tmpC"])
        A("dve", lambda e: e.tensor_tensor(out=ejr[:], in0=ejr[:], in1=tmpC[:], op=ALU.mult), r=["ejr", "tmpC"], w=["ejr"])
        A("dve", lambda e: e.tensor_tensor(out=eji[:], in0=eji[:], in1=tmpC[:], op=ALU.mult), r=["eji", "tmpC"], w=["eji"])
        e1r, e1i = ejr[:, 8, :], eji[:, 8, :]
        A("dve", lambda e: e.tensor_scalar(out=sm[0][:], in0=e1r, scalar1=-1.0, scalar2=0.0, op0=ALU.add, op1=ALU.add),
          r=["ejr"], w=["sm0"])
        A("dve", lambda e: e.tensor_tensor(out=sm[1][:], in0=sm[0][:], in1=are[:], op=ALU.mult), r=["sm0", "are"], w=["sm1"])
        A("dve", lambda e: e.tensor_tensor(out=sm[2][:], in0=e1i, in1=aim[:], op=ALU.mult), r=["eji", "aim"], w=["sm2"])
        A("dve", lambda e: e.tensor_tensor(out=sm[1][:], in0=sm[1][:], in1=sm[2][:], op=ALU.add), r=["sm1", "sm2"], w=["sm1"])
        A("dve", lambda e: e.tensor_tensor(out=sm[2][:], in0=e1i, in1=are[:], op=ALU.mult), r=["eji", "are", "sm1"], w=["sm2"])
        A("dve", lambda e: e.tensor_tensor(out=sm[3][:], in0=sm[0][:], in1=aim[:], op=ALU.mult), r=["sm0", "aim"], w=["sm3"])
        A("dve", lambda e: e.tensor_tensor(out=sm[2][:], in0=sm[2][:], in1=sm[3][:], op=ALU.subtract), r=["sm2", "sm3"], w=["sm2"])
        A("dve", lambda e: e.tensor_tensor(out=sm[3][:], in0=are[:], in1=are[:], op=ALU.mult), r=["are", "sm2"], w=["sm3"])
        A("dve", lambda e: e.tensor_tensor(out=sm[4][:], in0=aim[:], in1=aim[:], op=ALU.mult), r=["aim"], w=["sm4"])
        A("dve", lambda e: e.tensor_tensor(out=sm[3][:], in0=sm[3][:], in1=sm[4][:], op=ALU.add), r=["sm3", "sm4"], w=["sm3"])
        A("dve", lambda e: e.reciprocal(out=sm[3][:], in_=sm[3][:]), r=["sm3"], w=["sm3"])
        A("dve", lambda e: e.tensor_tensor(out=sm[1][:], in0=sm[1][:], in1=sm[3][:], op=ALU.mult), r=["sm1", "sm3"], w=["sm1"])
        A("dve", lambda e: e.tensor_tensor(out=sm[2][:], in0=sm[2][:], in1=sm[3][:], op=ALU.mult), r=["sm2", "sm3"], w=["sm2"])
        br_b = bc(sm[1][:].unsqueeze(2), [128, 128, 16]); bi_b = bc(sm[2][:].unsqueeze(2), [128, 128, 16])
        tA3 = tmpA[:].rearrange("p a b -> p b a"); tB3 = tmpB[:].rearrange("p a b -> p b a")
        A("dve", lambda e: e.tensor_tensor(out=tA3, in0=br_b, in1=btr[:], op=ALU.mult), r=["sm1", "btr", "ejr", "eji"], w=["tmpA"])
        A("dve", lambda e: e.tensor_tensor(out=tB3, in0=bi_b, in1=bti[:], op=ALU.mult), r=["sm2", "bti", "ejr", "eji"], w=["tmpB"])
        A("dve", lambda e: e.tensor_tensor(out=bbr[:], in0=tA3, in1=tB3, op=ALU.subtract), r=["tmpA", "tmpB"], w=["bbr"])
        A("dve", lambda e: e.tensor_tensor(out=tA3, in0=br_b, in1=bti[:], op=ALU.mult), r=["sm1", "bti", "bbr"], w=["tmpA"])
        A("dve", lambda e: e.tensor_tensor(out=tB3, in0=bi_b, in1=btr[:], op=ALU.mult), r=["sm2", "btr", "bbr"], w=["tmpB"])
        A("dve", lambda e: e.tensor_tensor(out=bbi[:], in0=tA3, in1=tB3, op=ALU.add), r=["tmpA", "tmpB"], w=["bbi"])
        T.emit(block)
        es1.close()
        if STOP_S5_SETUP[0]:
            return
        block = es.enter_context(nc.Block())
        s5_groups(nc, T, es, sb, locals())
        T.emit(block)


def s5_groups(nc, T, es, sb, V):
    A = T.add
    kio, identf, maskf, maskb, dcol = V["kio"], V["identf"], V["maskf"], V["maskb"], V["dcol"]
    ejr, eji, bbr, bbi, ctr, cti = V["ejr"], V["eji"], V["bbr"], V["bbi"], V["ctr"], V["cti"]
    phi, phi2pi, rho = V["phi"], V["phi2pi"], V["rho"]
    pss, psZA, psZB, psZc, psY = V["pss"], V["psZA"], V["psZB"], V["psZc"], V["psY"]
    DSCR, YSC, sm = V["DSCR"], V["YSC"], V["sm"]
    PI2 = TWO_PI
    ga = sb("ga", [128, 2, 8, 128]); hm = sb("hm", [128, 2, 8, 128])
    t1 = sb("gt1", [128, 8, 128]); t2 = sb("gt2", [128, 8, 128])
    winjA = sb("winjA", [128, 2, 8, 128], BF16); winjB = sb("winjB", [128, 2, 8, 128], BF16)
    wA = sb("wA", [128, 2, 8, 128], BF16); wB = sb("wB", [128, 2, 8, 128], BF16)
    toep = sb("toep", [128, 8, 128], BF16)
    tt1 = sb("tt1", [128, 4, 128]); tt2 = sb("tt2", [128, 4, 128])
    dg = [sb("dg0", [128, NK], BF16), sb("dg1", [128, NK], BF16)]
    ni = [sb("ni0", [128, NK], I32)] * 2; ang = [sb("ang0", [128, NK])] * 2
    rr = [sb("rr0", [128, NK])] * 2; ra = [sb("ra0", [128, NK])] * 2
    ts4 = [sb("ts%d" % i, [128, NK]) for i in range(4)]; tc4 = [sb("tc%d" % i, [128, NK]) for i in range(4)]
    m1 = [sb("m1%d" % d, [128, NK]) for d in range(2)]; m2 = [sb("m20", [128, NK])] * 2
    q = [sb("q%d" % d, [128, NK]) for d in range(2)]
    FC = [sb("FC%d" % d, [128, NK + 1], BF16) for d in range(2)]
    FS = [sb("FS%d" % d, [128, NK + 1], BF16) for d in range(2)]
    ytile = [sb("ytile0", [128, 9, 8, 128], BF16)] * 2
    halfpi = sm[5]
    for d in range(2):
        A("pool", lambda e, d=d: e.memset(FC[d][:], 0.0), w=[("FC", d)])
        A("pool", lambda e, d=d: e.memset(FS[d][:], 0.0), w=[("FS", d)])

    def eview(t, lo, hi, rev, d, gb, parts):
        sl = t[parts, lo:hi, d * 64 + gb * 8:d * 64 + gb * 8 + 8]
        if rev:
            sl = t[parts, hi - 1:(lo - 1 if lo > 0 else None):-1, d * 64 + gb * 8:d * 64 + gb * 8 + 8]
        return bc(sl.rearrange("p s g -> p g s").unsqueeze(3), [64, 8, 8, 16])

    def gview(t, d, gb, parts):
        return bc(t[parts, d * 64 + gb * 8:d * 64 + gb * 8 + 8, :].unsqueeze(2), [64, 8, 8, 16])

    def cplx(eng, outv, parts, Er, Ei, Xr, Xi, kind, rk, wk):
        a = t1[parts].rearrange("p g (s h) -> p g s h", h=16); b = t2[parts].rearrange("p g (s h) -> p g s h", h=16)
        pk = "lo" if parts.start == 0 else "hi"
        X1, X2 = (Xr, Xi) if kind in ("re", "nre") else (Xi, Xr)
        A(eng, lambda e: e.tensor_tensor(out=a, in0=Er, in1=X1, op=ALU.mult), r=rk, w=[("t1", pk)])
        A(eng, lambda e: e.tensor_tensor(out=b, in0=Ei, in1=X2, op=ALU.mult), r=rk, w=[("t2", pk)])
        if kind == "re":
            A(eng, lambda e: e.tensor_tensor(out=outv, in0=a, in1=b, op=ALU.subtract), r=[("t1", pk), ("t2", pk)], w=wk)
        elif kind == "im":
            A(eng, lambda e: e.tensor_tensor(out=outv, in0=a, in1=b, op=ALU.add), r=[("t1", pk), ("t2", pk)], w=wk)
        elif kind == "nre":
            A(eng, lambda e: e.tensor_tensor(out=outv, in0=b, in1=a, op=ALU.subtract), r=[("t1", pk), ("t2", pk)], w=wk)
        else:
            A(eng, lambda e: e.tensor_tensor(out=a, in0=a, in1=b, op=ALU.add), r=[("t1", pk), ("t2", pk)], w=[("t1", pk)])
            A(eng, lambda e: e.tensor_scalar(out=outv, in0=a, scalar1=-1.0, scalar2=0.0, op0=ALU.mult, op1=ALU.add),
              r=[("t1", pk)], w=wk)

    lo, hi = slice(0, 64), slice(64, 128)
    pkeys = ["ejr", "eji", "bbr", "bbi", "ctr", "cti"]
    yt = ytile[0]
    ykey = ("ytile", 0)

    def gen_weights(gb):
        for d in range(2):
            args = (7, 15, d == 0, d, gb)
            for parts, kind, eng in ((lo, "re", "dve"), (hi, "im", "pool")):
                ov = ga[parts, d].rearrange("p g (s h) -> p g s h", h=16)
                cplx(eng, ov, parts, eview(ejr, *args, parts), eview(eji, *args, parts),
                     gview(bbr, d, gb, parts), gview(bbi, d, gb, parts), kind, pkeys, [("ga", d)])
            args = (0, 8, d == 1, d, gb)
            for parts, kind, eng in ((lo, "re", "dve"), (hi, "nim", "pool")):
                ov = hm[parts, d].rearrange("p g (s h) -> p g s h", h=16)
                cplx(eng, ov, parts, eview(ejr, *args, parts), eview(eji, *args, parts),
                     gview(ctr, d, gb, parts), gview(cti, d, gb, parts), kind, pkeys, [("hm", d)])
            args = (8, 16, d == 1, d, gb)
            for (wt_, wn, kinds) in ((wA, "wA", ("re", "nim")), (wB, "wB", ("nim", "nre"))):
                for parts, kind, eng in ((lo, kinds[0], "dve"), (hi, kinds[1], "pool")):
                    ov = wt_[parts, d].rearrange("p g (s h) -> p g s h", h=16)
                    cplx(eng, ov, parts, eview(ejr, *args, parts), eview(eji, *args, parts),
                         gview(ctr, d, gb, parts), gview(cti, d, gb, parts), kind, pkeys + ["Ymm"], [(wn, d)])
        for d in range(2):
            for gq in range(2):
                for g4 in range(4):
                    g8 = gq * 4 + g4
                    A("pe", lambda e, d=d, g8=g8, g4=g4: e.transpose(out=pss[:, g4, :], in_=ga[:, d, g8, :], identity=identf[:]),
                      r=[("ga", d), "identf"], w=["pss"])
                A("dve", lambda e, d=d, gq=gq: e.tensor_copy(out=winjA[:, d, gq * 4:gq * 4 + 4, :], in_=pss[:]),
                  r=["pss", "Zmm"], w=["winjA"])
                A("dve", lambda e, d=d, gq=gq: e.tensor_copy(out=winjB[:, d, gq * 4:gq * 4 + 4, 0:64], in_=pss[:, :, 64:128]),
                  r=["pss", "Zmm"], w=["winjB"])
                A("dve", lambda e, d=d, gq=gq: e.tensor_scalar(out=winjB[:, d, gq * 4:gq * 4 + 4, 64:128], in0=pss[:, :, 0:64],
                                                               scalar1=-1.0, scalar2=0.0, op0=ALU.mult, op1=ALU.add),
                  r=["pss", "Zmm"], w=["winjB"])
        for gq in range(2):
            for d in range(2):
                for g4 in range(4):
                    g8 = gq * 4 + g4
                    A("pe", lambda e, d=d, g8=g8, g4=g4: e.matmul(pss[:, g4, :], lhsT=ga[:, d, g8, :], rhs=hm[:, d, g8, :],
                                                                 start=True, stop=True),
                      r=[("ga", d), ("hm", d)], w=["pss"])
                mk = maskf if d == 0 else maskb
                tt = tt1 if d == 0 else tt2
                A("dve", lambda e, mk=mk, tt=tt: e.tensor_tensor(out=tt[:], in0=pss[:], in1=bc(mk[:].unsqueeze(1), [128, 4, 128]),
                                                                op=ALU.mult), r=["pss", "maskf", "maskb"], w=["tt%d" % d])
            A("dve", lambda e: e.tensor_tensor(out=tt1[:], in0=tt1[:], in1=tt2[:], op=ALU.add), r=["tt0", "tt1"], w=["tt0"])
            for g4 in range(4):
                g8 = gq * 4 + g4
                g = gb * 8 + g8
                A("dve", lambda e, g4=g4, g8=g8, g=g: e.scalar_tensor_tensor(
                    out=toep[:, g8, :], in0=identf[:], scalar=dcol[:, g:g + 1], in1=tt1[:, g4, :], op0=ALU.mult, op1=ALU.add),
                    r=["tt0", "dcol", "identf", "Ymm"], w=["toep"])
    def tables(g, d):
        col = d * 64 + g
        ti = (g % 2) * 2 + d
        ts = {d: ts4[ti]}; tc = {d: tc4[ti]}
        A("dve", lambda e: e.tensor_scalar(out=ni[d][:], in0=kio[:], scalar1=phi2pi[:, col:col + 1], scalar2=0.0,
                                           op0=ALU.mult, op1=ALU.add), r=["kio", "phi2pi"], w=[("ni", 0)])
        A("act", lambda e: e.activation(out=ang[d][:], in_=kio[:], func=AF.Copy, scale=phi[:, col:col + 1]),
          r=["kio", "phi"], w=[("ang", 0)])
        A("dve", lambda e: e.scalar_tensor_tensor(out=rr[d][:], in0=ni[d][:], scalar=-PI2, in1=ang[d][:], op0=ALU.mult,
                                                  op1=ALU.add), r=[("ni", 0), ("ang", 0)], w=[("rr", 0)])
        A("dve", lambda e: e.tensor_scalar(out=rr[d][:], in0=rr[d][:], scalar1=-PI_SAFE, scalar2=PI_SAFE, op0=ALU.max,
                                           op1=ALU.min), r=[("rr", 0)], w=[("rr", 0)])
        A("act", lambda e: e.activation(out=ra[d][:], in_=rr[d][:], func=AF.Abs), r=[("rr", 0)], w=[("ra", 0)])
        A("act", lambda e: e.activation(out=ts[d][:], in_=rr[d][:], func=AF.Sin), r=[("rr", 0)], w=[("ts", ti)])
        A("act", lambda e: e.activation(out=tc[d][:], in_=ra[d][:], func=AF.Sin, scale=-1.0, bias=halfpi[:, 0:1]),
          r=[("ra", 0), "sm5"], w=[("tc", ti)])

    def core(g, g8, d, dgt, dkey):
        col = d * 64 + g
        ti = (g % 2) * 2 + d
        ts = {d: ts4[ti]}; tc = {d: tc4[ti]}
        for (pz, wj, zk, wk_) in ((psZA, winjA, "psZA", "winjA"), (psZB, winjB, "psZB", "winjB")):
            for hf in range(2):
                A("pe", lambda e, pz=pz, wj=wj, hf=hf: e.matmul(
                    pz[:, hf, :], lhsT=wj[:, d, g8, :], rhs=dgt[:, NKC + 512 * hf:NKC + 512 * (hf + 1)],
                    start=True, stop=True), r=[dkey, wk_], w=[zk, "Zmm"])
        A("pe", lambda e: e.matmul(psZc[:, 0:32], lhsT=winjA[:, d, g8, :], rhs=dgt[:, 0:NKC], start=True, stop=True),
          r=[dkey, "winjA"], w=["psZc", "Zmm"])
        A("pe", lambda e: e.matmul(psZc[:, 32:64], lhsT=winjB[:, d, g8, :], rhs=dgt[:, 0:NKC], start=True, stop=True),
          r=[dkey, "winjB"], w=["psZc", "Zmm"])
        zaf = psZA[:].rearrange("p a b -> p (a b)"); zbf = psZB[:].rearrange("p a b -> p (a b)")
        if d == 0:
            segs = [(slice(0, NKC), psZc[:, 0:32], psZc[:, 32:64]), (slice(NKC, NK), zaf, zbf)]
        else:
            segs = [(slice(0, NKC), psZc[:, 31::-1], psZc[:, 63:31:-1]), (slice(NKC, NK), zaf[:, ::-1], zbf[:, ::-1])]
        for (js, za, zb) in segs:
            A("dve", lambda e, js=js, za=za: e.tensor_tensor(out=m1[d][:, js], in0=za, in1=tc[d][:, js], op=ALU.mult),
              r=["psZA", "psZc", ("tc", ti)], w=[("m1", d)])
            A("dve", lambda e, js=js, zb=zb: e.tensor_tensor(out=m2[d][:, js], in0=zb, in1=ts[d][:, js], op=ALU.mult),
              r=["psZB", "psZc", ("ts", ti)], w=[("m2", 0)])
        A("dve", lambda e: e.tensor_tensor(out=m1[d][:], in0=m1[d][:], in1=m2[d][:], op=ALU.add), r=[("m1", d), ("m2", 0)],
          w=[("m1", d)])

    def core2(g, g8, d):
        col = d * 64 + g
        ti = (g % 2) * 2 + d
        ts = {d: ts4[ti]}; tc = {d: tc4[ti]}
        A("dve", lambda e: e.tensor_tensor_scan(out=q[d][:], data0=bc(rho[:, col:col + 1], [128, NK]), data1=m1[d][:],
                                                initial=0.0, op0=ALU.mult, op1=ALU.add), r=[("m1", d), "rho"], w=[("q", d)])
        for (Fb, tab, fk, tk) in ((FC[d], tc[d], ("FC", d), ("tc", ti)), (FS[d], ts[d], ("FS", d), ("ts", ti))):
            if d == 0:
                A("pool", lambda e, Fb=Fb, tab=tab: e.tensor_tensor(out=Fb[:, 1:NK + 1], in0=q[d][:], in1=tab[:], op=ALU.mult),
                  r=[("q", d), tk], w=[fk])
            else:
                A("pool", lambda e, Fb=Fb, tab=tab: e.tensor_tensor(out=Fb[:, 31:0:-1], in0=q[d][:, 0:31], in1=tab[:, 0:31],
                                                                   op=ALU.mult), r=[("q", d), tk], w=[fk])
                A("pool", lambda e, Fb=Fb, tab=tab: e.tensor_tensor(out=Fb[:, NK - 1:32:-1], in0=q[d][:, 32:NK - 1],
                                                                   in1=tab[:, 32:NK - 1], op=ALU.mult), r=[("q", d), tk], w=[fk])
                A("pool", lambda e, Fb=Fb, tab=tab: e.tensor_tensor(out=Fb[:, NK:NK + 1], in0=q[d][:, 31:32], in1=tab[:, 31:32],
                                                                   op=ALU.mult), r=[("q", d), tk], w=[fk])

    def outputs(g, g8, dgt, dkey):
        rounds = [(0, [(0, NKC, 0)] + [(1 + i, 128, NKC + 128 * i) for i in range(3)]),
                  (1, [(4 + i, 128, NKC + 128 * (3 + i)) for i in range(4)]),
                  (0, [(8, 128, NKC + 128 * 7)])]
        for (pb, blks) in rounds:
            py = psY[pb]
            for slot, (kb, nk, p0) in enumerate(blks):
                ops = [(dgt[:, p0:p0 + nk], toep[:, g8, :], [dkey, "toep"]),
                       (FC[0][:, p0:p0 + nk], wA[:, 0, g8, :], [("FC", 0), ("wA", 0)]),
                       (FS[0][:, p0:p0 + nk], wB[:, 0, g8, :], [("FS", 0), ("wB", 0)]),
                       (FC[1][:, p0 + 1:p0 + 1 + nk], wA[:, 1, g8, :], [("FC", 1), ("wA", 1)]),
                       (FS[1][:, p0 + 1:p0 + 1 + nk], wB[:, 1, g8, :], [("FS", 1), ("wB", 1)])]
                for oi, (lt, rh, rk) in enumerate(ops):
                    A("pe", lambda e, py=py, slot=slot, nk=nk, lt=lt, rh=rh, oi=oi: e.matmul(
                        py[:nk, slot, :], lhsT=lt, rhs=rh, start=(oi == 0), stop=(oi == 4)),
                        r=rk, w=[("psY", pb), "Ymm"])
            if blks[0][1] == NKC:
                A("act", lambda e, py=py, yt=yt, g8=g8: e.activation(
                    out=yt[:NKC, 0, :, g8 * 16:(g8 + 1) * 16], in_=py[:NKC, 0, :].rearrange("k (t h) -> k t h", h=16),
                    func=AF.Gelu), r=[("psY", pb)], w=[ykey])
                rest = blks[1:]
                s0 = 1
            else:
                rest = blks
                s0 = 0
            kb0 = rest[0][0]
            nb = len(rest)
            A("act", lambda e, py=py, yt=yt, g8=g8, kb0=kb0, nb=nb, s0=s0: e.activation(
                out=yt[:, kb0:kb0 + nb, :, g8 * 16:(g8 + 1) * 16],
                in_=py[:, s0:s0 + nb, :].rearrange("k b (t h) -> k b t h", h=16), func=AF.Gelu),
                r=[("psY", pb)], w=[ykey])

    def store(gb):
        T.dma(YSC[0:C, gb * 128:(gb + 1) * 128].rearrange("(k t) f -> k t f", t=8), yt[:NKC, 0], r=[ykey], w=["YSC"],
              stream="ysc", eng="pool")
        for kb in range(8):
            T.dma(YSC[C + kb * 1024:C + (kb + 1) * 1024, gb * 128:(gb + 1) * 128].rearrange("(k t) f -> k t f", t=8),
                  yt[:, 1 + kb], r=[ykey], w=["YSC"], stream="ysc", eng="pool")

    def load_dg(g):
        T.dma(dg[g % 2][:], DSCR[g], w=[("dg", g % 2)], stream="dg%d" % (g % 2))

    def stage1(g):
        dgt = dg[g % 2]; dkey = ("dg", g % 2)
        core(g, g % 8, 0, dgt, dkey)
        core(g, g % 8, 1, dgt, dkey)

    gen_weights(0)
    tables(0, 0); tables(0, 1)
    load_dg(0)
    stage1(0)
    for g in range(64):
        g8 = g % 8
        core2(g, g8, 0)
        core2(g, g8, 1)
        last_in_gb = (g8 == 7)
        if g + 1 < 64:
            tables(g + 1, 0); tables(g + 1, 1)
            load_dg(g + 1)
            if not last_in_gb:
                stage1(g + 1)
        outputs(g, g8, dg[g % 2], ("dg", g % 2))
        if last_in_gb:
            store(g // 8)
            if g + 1 < 64:
                gen_weights(g // 8 + 1)
                stage1(g + 1)


def phase_post0(nc, T, blocks, VEC, YSC, SZ, w_glu, b_glu, w_out, identb, X1, CTX1):
    with ExitStack() as es:
        sb = lambda n, s, d=F32: es.enter_context(nc.sbuf_tensor("p0_" + n, s, d))
        stage = [sb("stg0", [128, D]), sb("stg1", [128, D])]
        block = es.enter_context(nc.Block())
        A = T.add
        grep = []
        for n in range(2):
            t = sb("gate%d" % n, [128, D])
            T.dma(t[:], VEC[n * 3 + 2, :].partition_broadcast(128), w=[("gate", n)], stream="gate%d" % n)
            grep.append(t)
        wg = load_cast_weight(nc, T, es, "p0_wglu", w_glu, D, stage)
        wo = load_cast_weight(nc, T, es, "p0_wout", w_out, D, stage)
        bgf = sb("bgf", [1, D]); bgb = sb("bgb", [1, D], BF16); ones = sb("ones", [1, 128], BF16)
        T.dma(bgf[:], b_glu.rearrange("(o f) -> o f", o=1), w=["bgf"], stream="bgf")
        A("dve", lambda e: e.tensor_copy(out=bgb[:], in_=bgf[:]), r=["bgf"], w=["bgb"])
        A("dve", lambda e: e.memset(ones[:], 1.0), w=["ones"])
        yg = sb("yg", [128, 8, D], BF16); szt = sb("szt", [128, 8, D], BF16)
        ygT = sb("ygT", [128, 8, 8, 128], BF16)
        sg = [sb("sg0", [128, D], BF16), sb("sg1", [128, D], BF16)]
        y3 = sb("y3", [128, 8, D], BF16); y3T = sb("y3T", [128, 8, 8, 128], BF16)
        xh = [sb("xh0", [128, 4, D]), sb("xh1", [128, 4, D])]
        tmp = [sb("tmp0", [128, 512]), sb("tmp1", [128, 512])]
        psT = [es.enter_context(nc.psum_tensor("p0psT%d" % i, [128, 8, 128], BF16)) for i in range(2)]
        psU = [es.enter_context(nc.psum_tensor("p0psU%d" % i, [128, 512], F32)) for i in range(4)]
        cnt = {"ev": 0, "u": 0, "tmp": 0}

        def transposes(src, dstT, skey, dkey, nk):
            for ft in range(8):
                pt = psT[ft % 2]
                for t in range(8):
                    A("pe", lambda e, pt=pt, t=t, ft=ft: e.transpose(
                        out=pt[:, t, :nk], in_=src[:nk, t, ft * 128:(ft + 1) * 128], identity=identb[:nk, :nk]),
                        r=[(skey, t), "identb"], w=[("psT", ft % 2)])
                if cnt["ev"] % 2 == 0:
                    A("act", lambda e, pt=pt, ft=ft: e.copy(out=dstT[:, ft, :, :nk], in_=pt[:, :, :nk]),
                      r=[("psT", ft % 2)], w=[(dkey, ft)])
                else:
                    A("dve", lambda e, pt=pt, ft=ft: e.tensor_copy(out=dstT[:, ft, :, :nk], in_=pt[:, :, :nk]),
                      r=[("psT", ft % 2)], w=[(dkey, ft)])
                cnt["ev"] += 1

        def do_block(nk, xap, kg0, row0, is_ctx, bidx):
            n = 1 if is_ctx else 0
            xv = xap.rearrange("(k s) f -> k s f", s=8)
            T.dma(yg[:nk], YSC[row0:row0 + nk * 8, :].rearrange("(k t) f -> k t f", t=8), w=[("yg", t) for t in range(8)],
                  stream="yg")
            T.dma(szt[:nk], SZ[row0:row0 + nk * 8, :].rearrange("(k t) f -> k t f", t=8), w=["szt"], stream="szt")
            for half in range(2):
                T.dma(xh[half][:nk], xv[:, half * 4:(half + 1) * 4, :], w=[("xh", half)], stream="xh%d" % half)
            transposes(yg, ygT, "yg", "ygT", nk)
            for t in range(8):
                sgt = sg[t % 2]
                for hf in range(2):
                    pu = psU[cnt["u"] % 4]; pkey = ("psU", cnt["u"] % 4); cnt["u"] += 1
                    A("pe", lambda e, pu=pu, hf=hf: e.matmul(pu[:nk, :], lhsT=ones[0:1, :nk], rhs=bgb[0:1, hf * 512:(hf + 1) * 512],
                                                            start=True, stop=False), r=["ones", "bgb"], w=[pkey])
                    for ft in range(8):
                        A("pe", lambda e, pu=pu, hf=hf, ft=ft, t=t: e.matmul(
                            pu[:nk, :], lhsT=ygT[:, ft, t, :nk], rhs=wg[:, ft, hf * 512:(hf + 1) * 512],
                            start=False, stop=(ft == 7)), r=[("ygT", ft), "p0_wglu"], w=[pkey])
                    A("act", lambda e, pu=pu, hf=hf, sgt=sgt: e.activation(out=sgt[:nk, hf * 512:(hf + 1) * 512], in_=pu[:nk, :],
                                                                          func=AF.Sigmoid), r=[pkey], w=[("sg", t % 2)])
                A("dve", lambda e, t=t, sgt=sgt: e.tensor_tensor(out=sgt[:nk], in0=sgt[:nk], in1=yg[:nk, t, :], op=ALU.mult),
                  r=[("sg", t % 2), ("yg", t)], w=[("sg", t % 2)])
                A("pool", lambda e, t=t, sgt=sgt: e.tensor_tensor(out=y3[:nk, t, :], in0=sgt[:nk], in1=szt[:nk, t, :], op=ALU.mult),
                  r=[("sg", t % 2), "szt"], w=[("y3", t)])
            transposes(y3, y3T, "y3", "y3T", nk)
            for t in range(8):
                half, s4 = divmod(t, 4)
                for hf in range(2):
                    pu = psU[cnt["u"] % 4]; pkey = ("psU", cnt["u"] % 4); cnt["u"] += 1
                    for ft in range(8):
                        A("pe", lambda e, pu=pu, hf=hf, ft=ft, t=t: e.matmul(
                            pu[:nk, :], lhsT=y3T[:, ft, t, :nk], rhs=wo[:, ft, hf * 512:(hf + 1) * 512],
                            start=(ft == 0), stop=(ft == 7)), r=[("y3T", ft), "p0_wout"], w=[pkey])
                    tm = tmp[cnt["tmp"] % 2]; tkey = ("tmp", cnt["tmp"] % 2); cnt["tmp"] += 1
                    A("dve", lambda e, pu=pu, hf=hf, tm=tm, n=n: e.tensor_tensor(
                        out=tm[:nk], in0=pu[:nk, :], in1=grep[n][:nk, hf * 512:(hf + 1) * 512], op=ALU.mult),
                        r=[pkey, ("gate", n)], w=[tkey])
                    A("pool", lambda e, hf=hf, tm=tm, half=half, s4=s4: e.tensor_tensor(
                        out=xh[half][:nk, s4, hf * 512:(hf + 1) * 512], in0=xh[half][:nk, s4, hf * 512:(hf + 1) * 512],
                        in1=tm[:nk], op=ALU.add), r=[tkey, ("xh", half)], w=[("xh", half)])
            dst = CTX1 if is_ctx else X1[bidx * 1024:(bidx + 1) * 1024, :]
            dv = dst.rearrange("(k s) f -> k s f", s=8)
            for half in range(2):
                T.dma(dv[:, half * 4:(half + 1) * 4, :], xh[half][:nk], r=[("xh", half)], w=["X1"], stream="x1st", eng="pool")
        for blk in blocks:
            do_block(*blk)
        T.emit(block)


def phase_attn(nc, T, VEC, X1, CTX1, w_in, q_norm, k_norm, w_out, fin_g, identf, identb, posr_in, posc_in, fidx_in,
               out, stop_after):
    NKT = 66
    A = T.add
    with ExitStack() as es:
        sb = lambda n, s, d=F32: es.enter_context(nc.sbuf_tensor("a_" + n, s, d))
        KT = sb("KT", [128, 2, NKT * 128], BF16)
        Vt = sb("V", [128, NKT, 4, 65], BF16)
        cosr = sb("cosr", [128, 8, 16]); sinr = sb("sinr", [128, 8, 16])
        cosc = sb("cosc", [128, 8, 16]); sinc = sb("sinc", [128, 8, 16])
        rep = {}
        for nm in ("gs0", "sh0"):
            rep[nm] = sb("rep_" + nm, [128, D])
        qn_rep = sb("qn_rep", [128, 64]); kn_rep = sb("kn_rep", [128, 64])
        halfpi = sb("halfpi", [128, 1])
        xh = [sb("xh0", [128, 4, D])] * 2
        tmp32 = [sb("tmp32a", [128, D])] * 2
        ss = sb("ss", [128, 8]); ms = sb("ms", [128, 8]); rstd = sb("rstd", [128, 8])
        h = sb("h", [128, 4, D], BF16)
        hT = sb("hT", [128, 8, 4, 128], BF16)
        sq = sb("sq", [128, 512]); qn = [sb("qn0", [128, 512])] * 2
        junk = sq[:].bitcast(BF16)
        ssh = sb("ssh", [128, 8]); msh = sb("msh", [128, 8]); rsh = sb("rsh", [128, 8])
        ra_ = sb("ra", [128, 512]); rb_ = sb("rb", [128, 512])
        cc = sb("cc", [128, 8, 2, 16]); cs_ = sb("cs", [128, 8, 2, 16])
        krope = sb("krope", [128, 4, 64], BF16)
        psU = [es.enter_context(nc.psum_tensor("apsU0", [128, 512], F32))] * 2
        psS = [es.enter_context(nc.psum_tensor("apsS%d" % i, [128, 2, 512], F32)) for i in range(2)]
        psO = [es.enter_context(nc.psum_tensor("apsO%d" % i, [128, 512], F32)) for i in range(2)]
        psN = es.enter_context(nc.psum_tensor("apsN", [128, 4, 128], F32))
        psTb = psN[:].rearrange("p a b -> p (a b)").bitcast(BF16).rearrange("p (a b) -> p a b", b=128)
        cnt = {"u": 0, "ev": 0, "s": 0}
        BS = [dict(i=0, sq=sq, qn=qn[0], ssh=ssh, msh=msh, rsh=rsh, ra=ra_, rb=rb_)]
        pu_bufs = [(psU[0][:, :], ("pu", 0))] + [(psS[i_][:, c_, :], ("bank", i_, c_)) for i_ in range(2) for c_ in range(2)]

        def next_pu():
            r_ = pu_bufs[cnt["u"] % len(pu_bufs)]
            cnt["u"] += 1
            return r_

        def norm_unit(nk, half, n, ns=4):
            xt = xh[half]
            A("dve", lambda e: e.memset(ss[:], 0.0), w=["ss"])
            for s4 in range(ns):
                A("act", lambda e, s4=s4: e.activation(out=junk[:nk, :], in_=xt[:nk, s4, :], func=AF.Square,
                                                       accum_out=ss[:nk, s4:s4 + 1]), r=[("xh", s4)], w=[("sq", 0), "ss"])
            A("dve", lambda e: e.tensor_scalar(out=ms[:nk, 0:ns], in0=ss[:nk, 0:ns], scalar1=1.0 / D, scalar2=EPS, op0=ALU.mult,
                                               op1=ALU.add), r=["ss"], w=["ms"])
            A("act", lambda e: e.sqrt(out=ms[:nk, 0:ns], in_=ms[:nk, 0:ns]), r=["ms"], w=["ms"])
            A("dve", lambda e: e.reciprocal(out=rstd[:nk, 0:ns], in_=ms[:nk, 0:ns]), r=["ms"], w=["rstd"])
            for s4 in range(ns):
                tm = tmp32[s4 % 2]
                A("dve", lambda e, s4=s4, tm=tm: e.scalar_tensor_tensor(
                    out=tm[:nk], in0=xt[:nk, s4, :], scalar=rstd[:nk, s4:s4 + 1], in1=rep["gs%d" % n][:nk],
                    op0=ALU.mult, op1=ALU.mult), r=[("xh", s4), "rstd", "rep"], w=[("tmp32", 0)])
                A("dve", lambda e, s4=s4, tm=tm: e.tensor_tensor(out=h[:nk, s4, :], in0=tm[:nk], in1=rep["sh%d" % n][:nk],
                                                                op=ALU.add), r=[("tmp32", 0), "rep"], w=["h"])
            tok_transposes(h, nk, ns)

        def tok_transposes(src, nk, ns=4):
            for ft in range(8):
                for s4 in range(ns):
                    A("pe", lambda e, s4=s4, ft=ft: e.transpose(out=psTb[:, s4, :nk], in_=src[:nk, s4, ft * 128:(ft + 1) * 128],
                                                                identity=identb[:nk, :nk]), r=["h", "identb"], w=["psNT"])
                if nk == 128:
                    ov_ = hT[:, ft, 0:ns, :]
                else:
                    ov_ = hT[:, ft].rearrange("p s k -> p (s k)")[:, 0:ns * nk].rearrange("p (s k) -> p s k", k=nk)
                if cnt["ev"] % 2 == 0:
                    A("act", lambda e, ft=ft, ov_=ov_: e.copy(out=ov_, in_=psTb[:, 0:ns, :nk]), r=["psNT"], w=["hT"])
                else:
                    A("dve", lambda e, ft=ft, ov_=ov_: e.tensor_copy(out=ov_, in_=psTb[:, 0:ns, :nk]), r=["psNT"], w=["hT"])
                cnt["ev"] += 1

        def head_norm(pu, pkey, nh, norm_rep_t, bs):
            w = nh * 64
            k_ = bs["i"]
            sq_, ssh_, msh_, rsh_, outf = bs["sq"], bs["ssh"], bs["msh"], bs["rsh"], bs["qn"]
            A("act", lambda e: e.activation(out=sq_[:, 0:w], in_=pu[:, 0:w], func=AF.Square), r=[pkey], w=[("sq", k_)])
            A("dve", lambda e: e.tensor_reduce(out=ssh_[:, 0:nh], in_=sq_[:, 0:w].rearrange("p (a d) -> p a d", d=64), axis=AX.X,
                                               op=ALU.add), r=[("sq", k_)], w=[("ssh", k_)])
            A("dve", lambda e: e.tensor_scalar(out=msh_[:, 0:nh], in0=ssh_[:, 0:nh], scalar1=1.0 / 64, scalar2=EPS, op0=ALU.mult,
                                               op1=ALU.add), r=[("ssh", k_)], w=[("msh", k_)])
            A("act", lambda e: e.sqrt(out=msh_[:, 0:nh], in_=msh_[:, 0:nh]), r=[("msh", k_)], w=[("msh", k_)])
            A("dve", lambda e: e.reciprocal(out=rsh_[:, 0:nh], in_=msh_[:, 0:nh]), r=[("msh", k_)], w=[("rsh", k_)])
            A("dve", lambda e: e.tensor_tensor(out=outf[:, 0:w].rearrange("p (a d) -> p a d", d=64),
                                               in0=pu[:, 0:w].rearrange("p (a d) -> p a d", d=64),
                                               in1=bc(rsh_[:, 0:nh].unsqueeze(2), [128, nh, 64]), op=ALU.mult),
              r=[pkey, ("rsh", k_)], w=[("qnf", k_)])
            A("pool", lambda e: e.tensor_tensor(out=outf[:, 0:w].rearrange("p (a d) -> p a d", d=64),
                                                in0=outf[:, 0:w].rearrange("p (a d) -> p a d", d=64),
                                                in1=bc(norm_rep_t[:].unsqueeze(1), [128, nh, 64]), op=ALU.mult),
              r=[("qnf", k_), "nrep"], w=[("qnf", k_)])

        def rope(bs, nh, s, outv, okey):
            k_ = bs["i"]
            src, ra__, rb__ = bs["qn"], bs["ra"], bs["rb"]
            sv = src[:, 0:nh * 64].rearrange("p (a x t f) -> p a x t f", x=2, t=2, f=16)
            ov = outv.rearrange("p a (x t f) -> p a x t f", x=2, t=2, f=16)
            x1, x2 = sv[:, :, :, 0, :], sv[:, :, :, 1, :]
            cb = bc(cc[:, s].unsqueeze(1), [128, nh, 2, 16]); sbb = bc(cs_[:, s].unsqueeze(1), [128, nh, 2, 16])
            n2 = nh * 32
            av = ra__[:, 0:n2].rearrange("p (a x f) -> p a x f", x=2, f=16)
            bv = rb__[:, 0:n2].rearrange("p (a x f) -> p a x f", x=2, f=16)
            av2 = ra__[:, n2:2 * n2].rearrange("p (a x f) -> p a x f", x=2, f=16)
            bv2 = rb__[:, n2:2 * n2].rearrange("p (a x f) -> p a x f", x=2, f=16)
            A("dve", lambda e: e.tensor_tensor(out=av, in0=x1, in1=cb, op=ALU.mult), r=[("qnf", k_), "cc"], w=[("ra", k_)])
            A("pool", lambda e: e.tensor_tensor(out=bv, in0=x2, in1=sbb, op=ALU.mult), r=[("qnf", k_), "cc"], w=[("rb", k_)])
            A("dve", lambda e: e.tensor_tensor(out=av2, in0=x2, in1=cb, op=ALU.mult), r=[("qnf", k_), "cc"], w=[("ra", k_)])
            A("pool", lambda e: e.tensor_tensor(out=bv2, in0=x1, in1=sbb, op=ALU.mult), r=[("qnf", k_), "cc"], w=[("rb", k_)])
            A("dve", lambda e: e.tensor_tensor(out=ov[:, :, :, 0, :], in0=av, in1=bv, op=ALU.subtract), r=[("ra", k_), ("rb", k_)], w=[okey])
            A("pool", lambda e: e.tensor_tensor(out=ov[:, :, :, 1, :], in0=av2, in1=bv2, op=ALU.add), r=[("ra", k_), ("rb", k_)], w=[okey])

        def block_tables(b):
            for (tab, rsrc, csrc) in ((cc, cosr, cosc), (cs_, sinr, sinc)):
                A("pool", lambda e, tab=tab, rsrc=rsrc: e.tensor_copy(out=tab[:, :, 0, :], in_=bc(rsrc[:, b, :].unsqueeze(1), [128, 8, 16])),
                  r=["tabs"], w=["cc"])
                A("pool", lambda e, tab=tab, csrc=csrc: e.tensor_copy(out=tab[:, :, 1, :], in_=csrc[:]), r=["tabs"], w=["cc"])

        with ExitStack() as es1:
            sbt = lambda n, s, d=F32: es1.enter_context(nc.sbuf_tensor("a1_" + n, s, d))
            stage = [sbt("stg0", [128, 512]), sbt("stg1", [128, 512])]
            posr = sbt("posr", [128, 8]); posc = sbt("posc", [128, 8]); fidx = sbt("fidx", [128, 16]); freq = sbt("freq", [128, 16])
            ta = sbt("ta", [128, 8, 16]); tb = sbt("tb", [128, 8, 16]); tn = sbt("tn", [128, 8, 16], I32)
            rep["gs1"] = sbt("rep_gs1", [128, D]); rep["sh1"] = sbt("rep_sh1", [128, D])
            kropes = [krope]
            for bi in range(1, 4):
                BS.append(dict(i=bi, sq=sbt("sq%d" % bi, [128, 512]), qn=sbt("qn%d" % bi, [128, 512]), ssh=sbt("ssh%d" % bi, [128, 8]),
                               msh=sbt("msh%d" % bi, [128, 8]), rsh=sbt("rsh%d" % bi, [128, 8]), ra=sbt("ra%d" % bi, [128, 512]),
                               rb=sbt("rb%d" % bi, [128, 512])))
                kropes.append(sbt("krope%d" % bi, [128, 4, 64], BF16))
            block = es1.enter_context(nc.Block())
            for nm, v in (("gs0", 6), ("sh0", 7), ("gs1", 9), ("sh1", 10)):
                T.dma(rep[nm][:], VEC[v, :].partition_broadcast(128), w=["rep"], stream="rep_" + nm)
            T.dma(qn_rep[:], q_norm.partition_broadcast(128), w=["nrep"], stream="qnr")
            T.dma(kn_rep[:], k_norm.partition_broadcast(128), w=["nrep"], stream="knr")
            T.dma(posr[:], posr_in[:, :], w=["posr"], stream="posr")
            T.dma(posc[:], posc_in[:, :], w=["posc"], stream="posc")
            T.dma(fidx[:], fidx_in[:, :], w=["fidx"], stream="fidx")
            A("pool", lambda e: e.memset(halfpi[:], float(np.pi / 2)), w=["halfpi"])
            A("pool", lambda e: e.memset(Vt[:], 1.0), w=["V"])
            A("act", lambda e: e.activation(out=freq[:], in_=fidx[:], func=AF.Exp, scale=float(-np.log(10000.0) / 16.0)),
              r=["fidx"], w=["freq"])
            for (pos, ct, st) in ((posr, cosr, sinr), (posc, cosc, sinc)):
                A("dve", lambda e, pos=pos: e.tensor_tensor(out=ta[:], in0=bc(pos[:].unsqueeze(2), [128, 8, 16]),
                                                           in1=bc(freq[:].unsqueeze(1), [128, 8, 16]), op=ALU.mult),
                  r=["posr", "posc", "freq"], w=["ta"])
                A("dve", lambda e: e.tensor_scalar(out=tn[:], in0=ta[:], scalar1=1.0 / TWO_PI, scalar2=0.0, op0=ALU.mult, op1=ALU.add),
                  r=["ta"], w=["tn"])
                A("dve", lambda e: e.scalar_tensor_tensor(out=tb[:], in0=tn[:], scalar=-TWO_PI, in1=ta[:], op0=ALU.mult, op1=ALU.add),
                  r=["tn", "ta"], w=["tb"])
                A("dve", lambda e: e.tensor_scalar(out=tb[:], in0=tb[:], scalar1=-PI_SAFE, scalar2=PI_SAFE, op0=ALU.max, op1=ALU.min),
                  r=["tb"], w=["tb"])
                A("dve", lambda e: e.scalar_tensor_tensor(out=ta[:], in0=tb[:], scalar=-1.0, in1=tb[:], op0=ALU.mult, op1=ALU.max),
                  r=["tb"], w=["ta"])
                A("act", lambda e, st=st: e.activation(out=st[:], in_=tb[:], func=AF.Sin), r=["tb"], w=["tabs"])
                A("act", lambda e, ct=ct: e.activation(out=ct[:], in_=ta[:], func=AF.Sin, scale=-1.0, bias=halfpi[:, 0:1]),
                  r=["ta", "halfpi"], w=["tabs"])
            wkv = es1.enter_context(nc.sbuf_tensor("a1_wkv", [128, 8, 512], BF16))
            for ft in range(8):
                st_ = stage[ft % 2]
                T.dma(st_[:], w_in[ft * 128:(ft + 1) * 128, 1024:1536], w=[("stage", ft % 2)], stream="stage%d" % (ft % 2))
                A("pool", lambda e, st_=st_, ft=ft: e.tensor_copy(
                    out=wkv[:, ft, 0:256].rearrange("p (gp hf d) -> p gp hf d", gp=2, hf=2),
                    in_=st_[:, 0:256].rearrange("p (hf gp d) -> p gp hf d", gp=2, hf=2)), r=[("stage", ft % 2)], w=["wkv"])
                A("pool", lambda e, st_=st_, ft=ft: e.tensor_copy(out=wkv[:, ft, 256:512], in_=st_[:, 256:512]),
                  r=[("stage", ft % 2)], w=["wkv"])

            def kv_tile(kt, lhs_fn, s, is_ctx):
                pu, pkey = next_pu()
                for ft in range(8):
                    A("pe", lambda e, pu=pu, ft=ft: e.matmul(pu[:, :], lhsT=lhs_fn(ft), rhs=wkv[:, ft, :], start=(ft == 0), stop=(ft == 7)),
                      r=["hT", "wkv"], w=[pkey])
                A("act", lambda e, pu=pu: e.copy(out=Vt[:, kt, :, 0:64], in_=pu[:, 256:512].rearrange("p (g d) -> p g d", d=64)),
                  r=[pkey], w=["V"])
                bs = BS[kt % 4]; kr = kropes[kt % 4]; okey = ("kroped", kt % 4)
                head_norm(pu, pkey, 4, kn_rep, bs)
                if is_ctx:
                    A("dve", lambda e: e.tensor_copy(out=kr[:].rearrange("p a d -> p (a d)"), in_=bs["qn"][:, 0:256]),
                      r=[("qnf", bs["i"])], w=[okey])
                else:
                    rope(bs, 4, s, kr[:], okey)
                for gp in range(2):
                    A("pe", lambda e, gp=gp, kr=kr: e.transpose(out=psTb[:, gp, :], in_=kr[:, 2 * gp:2 * gp + 2, :].rearrange("p a d -> p (a d)"), identity=identb[:]),
                      r=[okey, "identb"], w=["psNT"])
                A("dve", lambda e: e.tensor_copy(out=KT[:, :, kt * 128:(kt + 1) * 128], in_=psTb[:, 0:2, :]), r=["psNT"], w=["KT"])

            cv = CTX1.rearrange("(k s) f -> k s f", s=8)
            for j in range(2):
                for s4 in range(4):
                    T.dma(xh[0][:NKC, s4, :], cv[:, j * 4 + s4, :], w=[("xh", s4)], stream="xh%d" % s4)
                norm_unit(NKC, 0, 1)
                kv_tile(j, lambda ft: hT[:, ft].rearrange("p s k -> p (s k)")[:, 0:4 * NKC], None, True)
            for b in range(8):
                block_tables(b)
                xv = X1[b * 1024:(b + 1) * 1024, :].rearrange("(k s) f -> k s f", s=8)
                for hf in range(2):
                    for s4 in range(4):
                        T.dma(xh[0][:, s4, :], xv[:, hf * 4 + s4, :], w=[("xh", s4)], stream="xh%d" % s4)
                    norm_unit(128, 0, 0)
                    for s4 in range(4):
                        kv_tile(2 + b * 8 + hf * 4 + s4, lambda ft, s4=s4: hT[:, ft, s4, :], hf * 4 + s4, False)
            T.emit(block)
        if stop_after == 4:
            return finish(nc, T, out)

        with ExitStack() as es2:
            sbt = lambda n, s, d=F32: es2.enter_context(nc.sbuf_tensor("a2_" + n, s, d))
            stg = tmp32[0]
            rep["gate"] = sbt("rep_gate", [128, D]); rep["fin"] = sbt("rep_fin", [128, D])
            block = es2.enter_context(nc.Block())
            T.dma(rep["gate"][:], VEC[8, :].partition_broadcast(128), w=["rep"], stream="rep_gate")
            T.dma(rep["fin"][:], fin_g.partition_broadcast(128), w=["rep"], stream="rep_fin")
            wqz = es2.enter_context(nc.sbuf_tensor("a2_wqz", [128, 8, 2048], BF16))
            wo = es2.enter_context(nc.sbuf_tensor("a2_wo", [128, 8, D], BF16))
            for ft in range(8):
                for (c0, o0) in ((0, 0), (1536, 1024)):
                    T.dma(stg[:], w_in[ft * 128:(ft + 1) * 128, c0:c0 + 1024], w=[("tmp32", 0)], stream="stg")
                    if o0 == 0:
                        A("pool", lambda e, ft=ft: e.tensor_copy(
                            out=wqz[:, ft, 0:1024].rearrange("p (hp hf d) -> p hp hf d", hp=8, hf=2),
                            in_=stg[:].rearrange("p (hf hp d) -> p hp hf d", hp=8, hf=2)), r=[("tmp32", 0)], w=["wqz"])
                    else:
                        A("pool", lambda e, ft=ft, o0=o0: e.tensor_copy(out=wqz[:, ft, o0:o0 + 1024], in_=stg[:]),
                          r=[("tmp32", 0)], w=["wqz"])
                T.dma(stg[:], w_out[ft * 128:(ft + 1) * 128, :], w=[("tmp32", 0)], stream="stg")
                A("pool", lambda e, ft=ft: e.tensor_copy(out=wo[:, ft, :], in_=stg[:]), r=[("tmp32", 0)], w=["wo"])
            qrope = sbt("qrope", [128, 16, 64], BF16)
            QT = sbt("QT", [128, 8, 512], BF16)
            szq = sbt("szq", [128, 4, D], BF16)
            PT = [sbt("PT%d" % i, [128, 2, 512], BF16) for i in range(2)]
            oT = [sbt("oT0", [65, 512]), sbt("oT1", [65, 512])]
            rec = sbt("rec", [128, 4])
            tmpo = [tmp32[0][:, 0:512], tmp32[0][:, 512:1024]]
            ucount = 0
            for b in range(8):
                block_tables(b)
                xv = X1[b * 1024:(b + 1) * 1024, :].rearrange("(k s) f -> k s f", s=8)
                ov = out[b * 1024:(b + 1) * 1024, :].rearrange("(k s) f -> k s f", s=8)
                for hf in range(2):
                    xt = xh[0]; uh = 0; ucount += 1
                    for s4 in range(4):
                        T.dma(xt[:, s4, :], xv[:, hf * 4 + s4, :], w=[("xh", s4)], stream="xh%d" % s4)
                    norm_unit(128, uh, 0)
                    for s4 in range(4):
                        s = hf * 4 + s4
                        for nb in range(4):
                            pu, pkey = next_pu()
                            for ft in range(8):
                                A("pe", lambda e, pu=pu, ft=ft, s4=s4, nb=nb: e.matmul(
                                    pu[:, :], lhsT=hT[:, ft, s4, :], rhs=wqz[:, ft, nb * 512:(nb + 1) * 512],
                                    start=(ft == 0), stop=(ft == 7)), r=["hT", "wqz"], w=[pkey])
                            if nb < 2:
                                head_norm(pu, pkey, 8, qn_rep, BS[0])
                                rope(BS[0], 8, s, qrope[:, nb * 8:(nb + 1) * 8, :], "roped")
                            else:
                                A("act", lambda e, pu=pu, s4=s4, nb=nb: e.activation(
                                    out=szq[:, s4, (nb - 2) * 512:(nb - 1) * 512], in_=pu[:, :], func=AF.Silu), r=[pkey], w=["szq"])
                        for hp in range(8):
                            A("pe", lambda e, hp=hp, s4=s4: e.transpose(out=psTb[:, hp, :], in_=qrope[:, 2 * hp:2 * hp + 2, :].rearrange("p a d -> p (a d)"), identity=identb[:]),
                              r=["roped", "identb"], w=["psNT"])
                        A("dve", lambda e, s4=s4: e.tensor_copy(out=QT[:, :, s4 * 128:(s4 + 1) * 128], in_=psTb[:]), r=["psNT"], w=["QT"])
                    stream = [(hp, kt) for hp in range(8) for kt in range(NKT)]
                    pend = {}

                    def finalize(hd, ot):
                        for j in range(4):
                            A("pe", lambda e, j=j, ot=ot: e.transpose(out=psN[:, j, 0:65], in_=ot[:, j * 128:(j + 1) * 128],
                                                                     identity=identf[0:65, 0:65]), r=[("oT", hd // 8), "identf"], w=["psNT"])
                        A("dve", lambda e: e.reciprocal(out=rec[:], in_=psN[:, :, 64]), r=["psNT"], w=["rec"])
                        A("dve", lambda e, hd=hd: e.tensor_tensor(out=h[:, :, hd * 64:(hd + 1) * 64], in0=psN[:, :, 0:64],
                                                                 in1=bc(rec[:].unsqueeze(2), [128, 4, 64]), op=ALU.mult),
                          r=["psNT", "rec"], w=["h"])

                    def pv(idx):
                        hp, kt = stream[idx]
                        g = hp // 4
                        pt = PT[idx % 2]
                        for c in range(2):
                            A("pe", lambda e, pt=pt, kt=kt, g=g, c=c: e.matmul(
                                psO[c][0:65, :], lhsT=Vt[:, kt, g + 2 * c, :], rhs=pt[:, c, :], start=(kt == 0), stop=(kt == NKT - 1)),
                                r=[("PT", idx % 2), "V"], w=[("psO", c)])
                        if kt == NKT - 1:
                            for c in range(2):
                                A("dve", lambda e, c=c: e.tensor_copy(out=oT[c][:], in_=psO[c][0:65, :]), r=[("psO", c)], w=[("oT", c)])
                            pend[idx + 2] = hp

                    def s_exp_issue(idx):
                        hp, kt = stream[idx]
                        gp = hp // 4
                        ps = psS[idx % 2]
                        for c in range(2):
                            rows = slice(c * 64, c * 64 + 64)
                            A("pe", lambda e, ps=ps, rows=rows, gp=gp, kt=kt, hp=hp, c=c: e.matmul(
                                ps[:, c, :], lhsT=KT[rows, gp, kt * 128:(kt + 1) * 128], rhs=QT[rows, hp, :], start=True, stop=True,
                                tile_position=(64 * c, 0)),
                                r=["KT", "QT"], w=[("bank", idx % 2, c)])

                    def exp_issue(idx):
                        ps = psS[idx % 2]; pt = PT[idx % 2]
                        A("act", lambda e, ps=ps, pt=pt: e.activation(out=pt[:], in_=ps[:], func=AF.Exp, scale=0.125),
                          r=[("bank", idx % 2, 0), ("bank", idx % 2, 1)], w=[("PT", idx % 2)])

                    NS = len(stream)
                    s_exp_issue(0); exp_issue(0); s_exp_issue(1); exp_issue(1)
                    for k in range(NS):
                        if k + 2 < NS:
                            s_exp_issue(k + 2)
                        if k < NS - 1:
                            pv(k)
                        if k + 2 < NS:
                            exp_issue(k + 2)
                        if k in pend:
                            hp_ = pend.pop(k)
                            finalize(hp_, oT[0]); finalize(hp_ + 8, oT[1])
                    pv(len(stream) - 1)
                    for k_ in sorted(pend):
                        finalize(pend[k_], oT[0]); finalize(pend[k_] + 8, oT[1])
                    A("dve", lambda e: e.tensor_tensor(out=h[:], in0=h[:], in1=szq[:], op=ALU.mult), r=["h", "szq"], w=["h"])
                    tok_transposes(h, 128)
                    for s4 in range(4):
                        for nb in range(2):
                            pu, pkey = next_pu()
                            for ft in range(8):
                                A("pe", lambda e, pu=pu, ft=ft, s4=s4, nb=nb: e.matmul(
                                    pu[:, :], lhsT=hT[:, ft, s4, :], rhs=wo[:, ft, nb * 512:(nb + 1) * 512],
                                    start=(ft == 0), stop=(ft == 7)), r=["hT", "wo"], w=[pkey])
                            tm = tmpo[nb]
                            A("dve", lambda e, pu=pu, tm=tm, nb=nb: e.tensor_tensor(out=tm, in0=pu[:, :],
                                                                                   in1=rep["gate"][:, nb * 512:(nb + 1) * 512], op=ALU.mult),
                              r=[pkey, "rep"], w=[("tmp32", 0)])
                            A("pool", lambda e, tm=tm, nb=nb, s4=s4, xt=xt: e.tensor_tensor(
                                out=xt[:, s4, nb * 512:(nb + 1) * 512], in0=xt[:, s4, nb * 512:(nb + 1) * 512], in1=tm, op=ALU.add),
                                r=[("tmp32", 0), ("xh", s4)], w=[("xh", s4)])
                    A("dve", lambda e: e.memset(ss[:], 0.0), w=["ss"])
                    for s4 in range(4):
                        A("act", lambda e, s4=s4, xt=xt: e.activation(out=junk[:, :], in_=xt[:, s4, :], func=AF.Square,
                                                                     accum_out=ss[:, s4:s4 + 1]), r=[("xh", s4)], w=[("sq", 0), "ss"])
                    A("dve", lambda e: e.tensor_scalar(out=ms[:, 0:4], in0=ss[:, 0:4], scalar1=1.0 / D, scalar2=EPS, op0=ALU.mult,
                                                       op1=ALU.add), r=["ss"], w=["ms"])
                    A("act", lambda e: e.sqrt(out=ms[:, 0:4], in_=ms[:, 0:4]), r=["ms"], w=["ms"])
                    A("dve", lambda e: e.reciprocal(out=rstd[:, 0:4], in_=ms[:, 0:4]), r=["ms"], w=["rstd"])
                    for s4 in range(4):
                        A("dve", lambda e, s4=s4, xt=xt: e.scalar_tensor_tensor(
                            out=xt[:, s4, :], in0=xt[:, s4, :], scalar=rstd[:, s4:s4 + 1], in1=rep["fin"][:], op0=ALU.mult, op1=ALU.mult),
                            r=[("xh", s4), "rstd", "rep"], w=[("xh", s4)])
                        T.dma(ov[:, hf * 4 + s4, :], xt[:, s4, :], r=[("xh", s4)], w=["out"], stream="outst", eng="pool")
            A("sp", None, r=["out"])
            T.emit(block)
    return None


def _host_consts():
    ident = np.eye(128, dtype=np.float32)
    sidx = np.arange(128) // 16
    maskf = (sidx[None, :] >= sidx[:, None]).astype(np.float32)
    maskb = (sidx[None, :] <= sidx[:, None]).astype(np.float32)
    kio = np.tile(np.arange(NK, dtype=np.float32)[None, :], (128, 1))
    jv = np.tile(np.arange(-7, 9, dtype=np.float32)[None, :], (128, 1))
    k = np.arange(128)
    posr = (16 * np.arange(8)[None, :] + (k // 8)[:, None]).astype(np.float32)
    posc = (8 * (k % 8)[:, None] + np.arange(8)[None, :]).astype(np.float32)
    fidx = np.tile(np.arange(16, dtype=np.float32)[None, :], (128, 1))
    return dict(ident=ident, maskf=maskf, maskb=maskb, kio=kio, jvals=jv, posr=posr, posc=posc, fidx=fidx)


def make_in_maps(inputs):
    consts = _host_consts()
    f = lambda a: np.ascontiguousarray(np.asarray(a, dtype=np.float32))
    shared = dict(
        w_mod=f(inputs["w_mod"]), b_mod=f(inputs["b_mod"]), norm_g=f(inputs["norm_g"]),
        ssm_w_in=f(inputs["ssm_w_in"][0]), ssm_a_re=f(inputs["ssm_a_re"][0]), ssm_a_im=f(inputs["ssm_a_im"][0]),
        ssm_log_dt=f(inputs["ssm_log_dt"][0]), ssm_b_re=f(inputs["ssm_b_re"][0]), ssm_b_im=f(inputs["ssm_b_im"][0]),
        ssm_c_re=f(inputs["ssm_c_re"][0]), ssm_c_im=f(inputs["ssm_c_im"][0]), ssm_d=f(inputs["ssm_d"][0]),
        ssm_w_glu=f(inputs["ssm_w_glu"][0]), ssm_b_glu=f(inputs["ssm_b_glu"][0]), ssm_w_out=f(inputs["ssm_w_out"][0]),
        attn_w_in=f(inputs["attn_w_in"][0]), attn_q_norm=f(inputs["attn_q_norm"][0]),
        attn_k_norm=f(inputs["attn_k_norm"][0]), attn_w_out=f(inputs["attn_w_out"][0]),
        final_norm_g=f(inputs["final_norm_g"]), **consts)
    maps = []
    for b in range(8):
        m = dict(shared)
        m["x"] = f(inputs["x"][b]); m["ctx"] = f(inputs["ctx"][b])
        m["cvec"] = f(np.stack([np.asarray(inputs["c"][b]), np.asarray(inputs["c_ctx"])], 0))
        maps.append(m)
    return maps


def kernel(**inputs):
    nc, _ = build_program()
    maps = make_in_maps(inputs)
    res = run_bass_kernel_spmd(nc, maps, core_ids=list(range(8)))
    return np.stack([np.asarray(r["out"], dtype=np.float32) for r in res.results], 0)
```
